# Optimizing a Trainium2 kernel written in Bass

```python
import jax
import jax.numpy as jnp
from jax import lax
import numpy as np

D_MODEL = 1024
BATCH = 2
SEQ = 8192
DEPTH = 2

GRID_W = 64
CTX_LEN = 256
EPS = 1e-6

LSTM_HEADS = 4
LSTM_DIM = 128
LSTM_WIDTH = LSTM_HEADS * LSTM_DIM
LSTM_CONV = 5
LSTM_CHUNK = 128

ATTN_HEADS = 8
ATTN_KV_HEADS = 2
ATTN_GROUP = ATTN_HEADS // ATTN_KV_HEADS
ATTN_DIM = 64
ATTN_WINDOW = 128
ATTN_BLOCK = 128
ROPE_BASE = 10000.0

AB_SPLITS = (LSTM_WIDTH, LSTM_WIDTH, LSTM_WIDTH, LSTM_WIDTH, 4 * LSTM_HEADS,
             ATTN_HEADS * ATTN_DIM, ATTN_KV_HEADS * ATTN_DIM, ATTN_KV_HEADS * ATTN_DIM)
AB_IN = sum(AB_SPLITS)
AB_OFFSETS = tuple(sum(AB_SPLITS[:i + 1]) for i in range(len(AB_SPLITS) - 1))
AB_MIX = LSTM_WIDTH + ATTN_HEADS * ATTN_DIM

GM_CHUNK = 128
GM_GROUPS = 8
GM_HALF = 2 * D_MODEL

N_EXPERTS = 16
EC_FACTOR = 2
D_EXPERT = D_MODEL

kernel_name = 'hybrid_mlstm_swa_gmlp_ec_dit'


def rmsnorm(x, g):
    xf = x.astype(jnp.float32)
    y = xf * lax.rsqrt(jnp.mean(xf * xf, axis=-1, keepdims=True) + EPS)
    return (y * g.astype(jnp.float32)).astype(x.dtype)


def layernorm(x, g, b):
    xf = x.astype(jnp.float32)
    mu = jnp.mean(xf, axis=-1, keepdims=True)
    var = jnp.mean(jnp.square(xf - mu), axis=-1, keepdims=True)
    return ((xf - mu) * lax.rsqrt(var + EPS) * g.astype(jnp.float32) + b.astype(jnp.float32)).astype(x.dtype)


def adaln(cond, w, b):
    m = jax.nn.silu(cond) @ w + b
    return jnp.split(m[..., None, :], 6, axis=-1)


def modulate(x, g, shift, scale):
    return rmsnorm(x, g) * (1 + scale) + shift


def axial_rope(n, dim):
    rows = n // GRID_W
    row = jnp.repeat(jnp.arange(rows), GRID_W).astype(jnp.float32)
    col = jnp.tile(jnp.arange(GRID_W), rows).astype(jnp.float32)
    nf = dim // 4
    inv = ROPE_BASE ** (-jnp.arange(nf, dtype=jnp.float32) / nf)
    ang = jnp.concatenate([row[:, None] * inv, col[:, None] * inv], axis=-1)
    return jnp.cos(ang), jnp.sin(ang)


def apply_rope(x, cos, sin):
    x1, x2 = jnp.split(x.astype(jnp.float32), 2, axis=-1)
    c = cos[None, :, None, :]
    s = sin[None, :, None, :]
    return jnp.concatenate([x1 * c - x2 * s, x2 * c + x1 * s], axis=-1).astype(x.dtype)


def centred_depthwise_conv(x, w):
    return lax.conv_general_dilated(
        x, w[:, None, :].astype(x.dtype), window_strides=(1,),
        padding=[(LSTM_CONV // 2, LSTM_CONV // 2)],
        dimension_numbers=('NWC', 'WIO', 'NWC'), feature_group_count=x.shape[-1])


def zero_state(batch):
    return (jnp.zeros((batch, LSTM_HEADS, LSTM_DIM, LSTM_DIM), jnp.float32),
            jnp.zeros((batch, LSTM_HEADS, LSTM_DIM), jnp.float32),
            jnp.zeros((batch, LSTM_HEADS), jnp.float32))


def mlstm_state_update(state, k, v, logi, logf):
    cm, nv, m = state
    b = jnp.cumsum(logf, axis=-1)
    g = b[..., -1]
    w = g[..., None] - b + logi
    m_new = jnp.maximum(g + m, jnp.max(w, axis=-1))
    decay = jnp.exp(g + m - m_new)
    wt = jnp.exp(w - m_new[..., None])
    c_new = decay[..., None, None] * cm + jnp.einsum('bhl,bhlv,bhlk->bhvk', wt, v, k)
    n_new = decay[..., None] * nv + jnp.einsum('bhl,bhlk->bhk', wt, k)
    return (c_new, n_new, m_new)


def mlstm_chunk(state, q, k, v, logi, logf):
    cm, nv, m = state
    length = q.shape[-2]
    b = jnp.cumsum(logf, axis=-1)
    order = jnp.tril(jnp.ones((length, length), bool))
    log_d = jnp.where(order, b[..., :, None] - b[..., None, :] + logi[..., None, :], -jnp.inf)
    log_inter = b + m[..., None]
    m_row = jnp.maximum(log_inter, jnp.max(log_d, axis=-1))
    s = jnp.einsum('bhqd,bhkd->bhqk', q, k) * jnp.exp(log_d - m_row[..., None])
    a = jnp.exp(log_inter - m_row)
    num = jnp.einsum('bhqk,bhkv->bhqv', s, v) + a[..., None] * jnp.einsum('bhvk,bhqk->bhqv', cm, q)
    den = jnp.sum(s, axis=-1) + a * jnp.einsum('bhk,bhqk->bhq', nv, q)
    h = num / jnp.maximum(jnp.abs(den), jnp.exp(-m_row))[..., None]
    return mlstm_state_update(state, k, v, logi, logf), h


def mlstm_scan(q, k, v, logi, logf, state0):
    bsz, heads, n, d = q.shape
    nc = n // LSTM_CHUNK

    def to_chunks(t):
        t = t.reshape(t.shape[:2] + (nc, LSTM_CHUNK) + t.shape[3:])
        return jnp.moveaxis(t, 2, 0)

    state, h = lax.scan(lambda st, xs: mlstm_chunk(st, *xs), state0,
                        tuple(to_chunks(t) for t in (q, k, v, logi, logf)))
    return jnp.moveaxis(h, 0, 2).reshape(bsz, heads, n, d), state


def mlstm_inputs(aq, ak, av, ag, conv_w, gate_b):
    bsz, n, _ = aq.shape
    qk = jax.nn.silu(centred_depthwise_conv(jnp.concatenate([aq, ak], axis=-1), conv_w))
    q, k = jnp.split(qk, 2, axis=-1)
    heads = lambda t: t.reshape(bsz, n, LSTM_HEADS, LSTM_DIM).transpose(0, 2, 1, 3).astype(jnp.float32)
    gates = (ag + gate_b).astype(jnp.float32).reshape(bsz, n, 4, LSTM_HEADS).transpose(2, 0, 3, 1)
    fwd = (gates[0], jax.nn.log_sigmoid(gates[1]))
    bwd = (gates[2], jax.nn.log_sigmoid(gates[3]))
    return heads(q), heads(k) * LSTM_DIM ** -0.5, heads(av), fwd, bwd


def mlstm_output(h, o, g):
    bsz, heads, n, d = h.shape
    h = h.transpose(0, 2, 1, 3)
    h = h * lax.rsqrt(jnp.mean(h * h, axis=-1, keepdims=True) + EPS)
    h = h.reshape(bsz, n, heads * d) * g.astype(jnp.float32)
    return (h * jax.nn.sigmoid(o.astype(jnp.float32))).astype(o.dtype)


def window_attn(q, k, v, kc, vc, sink):
    bsz, n = q.shape[0], q.shape[1]
    nb = n // ATTN_BLOCK
    nctx = kc.shape[1]
    qb = q.reshape(bsz, nb, ATTN_BLOCK, ATTN_KV_HEADS, ATTN_GROUP, ATTN_DIM)

    def windows(t):
        tp = jnp.pad(t, ((0, 0), (ATTN_BLOCK, ATTN_BLOCK), (0, 0), (0, 0)))
        tp = tp.reshape(bsz, nb + 2, ATTN_BLOCK, ATTN_KV_HEADS, ATTN_DIM)
        return jnp.concatenate([tp[:, :-2], tp[:, 1:-1], tp[:, 2:]], axis=2)

    kw, vw = windows(k), windows(v)
    qpos = jnp.arange(nb)[:, None] * ATTN_BLOCK + jnp.arange(ATTN_BLOCK)[None]
    kpos = jnp.arange(nb)[:, None] * ATTN_BLOCK - ATTN_BLOCK + jnp.arange(3 * ATTN_BLOCK)[None]
    valid = ((jnp.abs(qpos[:, :, None] - kpos[:, None, :]) <= ATTN_WINDOW)
             & (kpos[:, None, :] >= 0) & (kpos[:, None, :] < n))
    scale = ATTN_DIM ** -0.5
    s_loc = jnp.einsum('bnqgrd,bnkgd->bngrqk', qb, kw).astype(jnp.float32) * scale
    s_loc = jnp.where(valid[None, :, None, None], s_loc, -jnp.inf)
    s_ctx = jnp.einsum('bnqgrd,bcgd->bngrqc', qb, kc).astype(jnp.float32) * scale
    s_sink = jnp.broadcast_to(sink.reshape(ATTN_KV_HEADS, ATTN_GROUP).astype(jnp.float32)[None, None, :, :, None, None],
                              s_loc.shape[:-1] + (1,))
    p = jax.nn.softmax(jnp.concatenate([s_loc, s_ctx, s_sink], axis=-1), axis=-1).astype(v.dtype)
    w3 = 3 * ATTN_BLOCK
    out = (jnp.einsum('bngrqk,bnkgd->bnqgrd', p[..., :w3], vw)
           + jnp.einsum('bngrqc,bcgd->bnqgrd', p[..., w3:w3 + nctx], vc))
    return out.reshape(bsz, n, ATTN_HEADS * ATTN_DIM)


def context_attn(q, k, v, sink):
    bsz, nctx = q.shape[0], q.shape[1]
    qg = q.reshape(bsz, nctx, ATTN_KV_HEADS, ATTN_GROUP, ATTN_DIM)
    s = jnp.einsum('bqgrd,bkgd->bgrqk', qg, k).astype(jnp.float32) * ATTN_DIM ** -0.5
    s_sink = jnp.broadcast_to(sink.reshape(ATTN_KV_HEADS, ATTN_GROUP).astype(jnp.float32)[None, :, :, None, None],
                              s.shape[:-1] + (1,))
    p = jax.nn.softmax(jnp.concatenate([s, s_sink], axis=-1), axis=-1).astype(v.dtype)
    out = jnp.einsum('bgrqk,bkgd->bqgrd', p[..., :nctx], v)
    return out.reshape(bsz, nctx, ATTN_HEADS * ATTN_DIM)


def mixer_ab(hx, hc, ctx_needed, w_in, conv_w, gate_b, head_g, sink, w_out, cos, sin):
    bsz, n, _ = hx.shape
    nctx = hc.shape[1]
    aqx, akx, avx, aox, agx, bqx, bkx, bvx = jnp.split(hx @ w_in, AB_OFFSETS, axis=-1)
    aqc, akc, avc, aoc, agc, bqc, bkc, bvc = jnp.split(hc @ w_in, AB_OFFSETS, axis=-1)
    rev = lambda t: jnp.flip(t, axis=2)

    qx, kx, vx, fx, bx = mlstm_inputs(aqx, akx, avx, agx, conv_w, gate_b)
    qc, kc, vc, fc, bc = mlstm_inputs(aqc, akc, avc, agc, conv_w, gate_b)
    zero = zero_state(bsz)
    if ctx_needed:
        hcf, st_f = mlstm_scan(qc, kc, vc, fc[0], fc[1], zero)
        hcb, st_b = mlstm_scan(rev(qc), rev(kc), rev(vc), rev(bc[0]), rev(bc[1]), zero)
    else:
        st_f = mlstm_state_update(zero, kc, vc, fc[0], fc[1])
        st_b = mlstm_state_update(zero, rev(kc), rev(vc), rev(bc[0]), rev(bc[1]))
    hxf, _ = mlstm_scan(qx, kx, vx, fx[0], fx[1], st_f)
    hxb, _ = mlstm_scan(rev(qx), rev(kx), rev(vx), rev(bx[0]), rev(bx[1]), st_b)
    a_x = mlstm_output(hxf + rev(hxb), aox, head_g)

    q_lat = apply_rope(bqx.reshape(bsz, n, ATTN_HEADS, ATTN_DIM), cos, sin)
    k_lat = apply_rope(bkx.reshape(bsz, n, ATTN_KV_HEADS, ATTN_DIM), cos, sin)
    v_lat = bvx.reshape(bsz, n, ATTN_KV_HEADS, ATTN_DIM)
    k_ctx = bkc.reshape(bsz, nctx, ATTN_KV_HEADS, ATTN_DIM)
    v_ctx = bvc.reshape(bsz, nctx, ATTN_KV_HEADS, ATTN_DIM)
    b_x = window_attn(q_lat, k_lat, v_lat, k_ctx, v_ctx, sink)

    y_x = jnp.concatenate([a_x, b_x], axis=-1) @ w_out
    y_c = None
    if ctx_needed:
        a_c = mlstm_output(hcf + rev(hcb), aoc, head_g)
        b_c = context_attn(bqc.reshape(bsz, nctx, ATTN_HEADS, ATTN_DIM), k_ctx, v_ctx, sink)
        y_c = jnp.concatenate([a_c, b_c], axis=-1) @ w_out
    return y_x, y_c


def mixer_chunk_gmlp(h, w_in, ln_g, ln_b, w_s, b_s, w_out):
    bsz, n, _ = h.shape
    u, v = jnp.split(jax.nn.gelu(h @ w_in), 2, axis=-1)
    v = layernorm(v, ln_g, ln_b).reshape(bsz, n // GM_CHUNK, GM_CHUNK, GM_GROUPS, GM_HALF // GM_GROUPS)
    v = jnp.einsum('gpq,bnqgc->bnpgc', w_s, v) + b_s.T[None, None, :, :, None]
    return (u * v.reshape(bsz, n, GM_HALF)) @ w_out


def ec_moe(h, w_router, w_gate, w_up, w_down):
    n, d = h.shape[1], h.shape[2]
    cap = max(1, EC_FACTOR * n // N_EXPERTS)
    aff = jax.nn.softmax((h @ w_router).astype(jnp.float32), axis=-1)
    gate, idx = lax.top_k(jnp.swapaxes(aff, 1, 2), cap)
    xe = jax.vmap(lambda hb, ib: hb[ib])(h, idx)
    hid = jax.nn.silu(jnp.einsum('becd,edf->becf', xe, w_gate)) * jnp.einsum('becd,edf->becf', xe, w_up)
    ye = jnp.einsum('becf,efd->becd', hid, w_down) * gate[..., None].astype(h.dtype)
    return jax.vmap(lambda yb, ib: jnp.zeros((n, d), yb.dtype).at[ib.reshape(-1)].add(yb.reshape(-1, d)))(ye, idx)


def setup_inputs(seed: int = 0) -> dict:
    key = jax.random.key(seed)
    ks = iter(jax.random.split(key, 32))
    nrm = lambda shape, scale: jax.random.normal(next(ks), shape, jnp.float32) * scale
    d = D_MODEL
    ne, no = (DEPTH + 1) // 2, DEPTH // 2
    f_bias = jnp.linspace(3.0, 6.0, LSTM_HEADS)
    zh = jnp.zeros((LSTM_HEADS,), jnp.float32)
    gate_base = jnp.concatenate([zh, f_bias, zh, f_bias])
    return {
        'x': nrm((BATCH, SEQ, d), 1.0),
        'c': nrm((BATCH, d), 1.0),
        'ctx': nrm((BATCH, CTX_LEN, d), 1.0),
        'c_ctx': nrm((d,), 1.0),
        'w_mod': nrm((DEPTH, d, 6 * d), 0.25 * d ** -0.5),
        'b_mod': nrm((DEPTH, 6 * d), 0.01),
        'norm_mix_g': 1.0 + nrm((DEPTH, d), 0.01),
        'norm_ffn_g': 1.0 + nrm((DEPTH, d), 0.01),
        'final_norm_g': 1.0 + nrm((d,), 0.01),
        'ab_w_in': nrm((ne, d, AB_IN), d ** -0.5),
        'ab_conv_w': nrm((ne, LSTM_CONV, 2 * LSTM_WIDTH), LSTM_CONV ** -0.5),
        'ab_gate_b': gate_base + nrm((ne, 4 * LSTM_HEADS), 0.1),
        'ab_head_g': 1.0 + nrm((ne, LSTM_WIDTH), 0.01),
        'ab_sink': nrm((ne, ATTN_HEADS), 0.5),
        'ab_w_out': nrm((ne, AB_MIX, d), AB_MIX ** -0.5),
        'gm_w_in': nrm((no, d, 2 * GM_HALF), d ** -0.5),
        'gm_ln_g': 1.0 + nrm((no, GM_HALF), 0.01),
        'gm_ln_b': nrm((no, GM_HALF), 0.01),
        'gm_w_s': nrm((no, GM_GROUPS, GM_CHUNK, GM_CHUNK), GM_CHUNK ** -0.5),
        'gm_b_s': 1.0 + nrm((no, GM_GROUPS, GM_CHUNK), 0.01),
        'gm_w_out': nrm((no, GM_HALF, d), GM_HALF ** -0.5),
        'moe_w_router': nrm((DEPTH, d, N_EXPERTS), d ** -0.5),
        'moe_w_gate': nrm((DEPTH, N_EXPERTS, d, D_EXPERT), d ** -0.5),
        'moe_w_up': nrm((DEPTH, N_EXPERTS, d, D_EXPERT), d ** -0.5),
        'moe_w_down': nrm((DEPTH, N_EXPERTS, D_EXPERT, d), D_EXPERT ** -0.5),
    }


def reference(x, c, ctx, c_ctx, w_mod, b_mod, norm_mix_g, norm_ffn_g, final_norm_g,
              ab_w_in, ab_conv_w, ab_gate_b, ab_head_g, ab_sink, ab_w_out,
              gm_w_in, gm_ln_g, gm_ln_b, gm_w_s, gm_b_s, gm_w_out,
              moe_w_router, moe_w_gate, moe_w_up, moe_w_down):
    n = x.shape[1]
    cos, sin = axial_rope(n, ATTN_DIM)
    for layer in range(DEPTH):
        ctx_needed = any(j % 2 == 0 for j in range(layer + 1, DEPTH))
        even = layer % 2 == 0
        sh1, sc1, gt1, sh2, sc2, gt2 = adaln(c, w_mod[layer], b_mod[layer])
        if even or ctx_needed:
            csh1, csc1, cgt1, csh2, csc2, cgt2 = adaln(c_ctx, w_mod[layer], b_mod[layer])
        hx = modulate(x, norm_mix_g[layer], sh1, sc1)
        if even:
            e = layer // 2
            hc = modulate(ctx, norm_mix_g[layer], csh1, csc1)
            yx, yc = mixer_ab(hx, hc, ctx_needed, ab_w_in[e], ab_conv_w[e], ab_gate_b[e],
                              ab_head_g[e], ab_sink[e], ab_w_out[e], cos, sin)
        else:
            o = layer // 2
            gm = (gm_w_in[o], gm_ln_g[o], gm_ln_b[o], gm_w_s[o], gm_b_s[o], gm_w_out[o])
            yx = mixer_chunk_gmlp(hx, *gm)
            yc = mixer_chunk_gmlp(modulate(ctx, norm_mix_g[layer], csh1, csc1), *gm) if ctx_needed else None
        moe = (moe_w_router[layer], moe_w_gate[layer], moe_w_up[layer], moe_w_down[layer])
        x = x + gt1 * yx
        x = x + gt2 * ec_moe(modulate(x, norm_ffn_g[layer], sh2, sc2), *moe)
        if ctx_needed:
            ctx = ctx + cgt1 * yc
            ctx = ctx + cgt2 * ec_moe(modulate(ctx, norm_ffn_g[layer], csh2, csc2), *moe)
    return rmsnorm(x, final_norm_g)
```

```python
import contextlib
import numpy as np
import concourse.bass as bass
import concourse.mybir as mybir
from concourse.bass_utils import run_bass_kernel_spmd

F32 = mybir.dt.float32
BF16 = mybir.dt.bfloat16
AF = mybir.ActivationFunctionType
ALU = mybir.AluOpType
AX = mybir.AxisListType

NCORES = 8
T = 2048
NBLK = 4
NTILE = 16
D = 1024
KC = 8
EPS = 1e-6
NEXP = 16
CAP = 1024


class Sched:
    COMPUTE = ("pe", "act", "dve", "pool")

    def __init__(self, nc, n_dma_sems=8):
        self.nc = nc
        self.ops = {e: [] for e in ("pe", "act", "dve", "pool", "sp")}
        self.seq = {e: 0 for e in self.COMPUTE}
        self.res = {}
        self.waited = {e: {} for e in self.ops}
        self.n_dma_sems = n_dma_sems
        self.dma_cnt = {e: 0 for e in self.ops}
        self.dma_val = {}
        self.final_tokens = []
        self.buf_pending = {}
        self.n_ops = 0

    @staticmethod
    def _buf(tag):
        return tag[0] if isinstance(tag, tuple) else tag

    def _deps(self, eng, reads, writes):
        deps = {}

        def add(tok, same_ok):
            if tok is None:
                return
            k, v = tok
            if k == eng and not same_ok:
                return
            if deps.get(k, 0) < v:
                deps[k] = v

        raw_same = eng != "pe"
        for r in reads:
            st = self.res.get(r)
            if st is not None:
                add(st["w"], raw_same)
        for w in writes:
            st = self.res.get(w)
            if st is not None:
                add(st["w"], raw_same)
                for k, v in st["r"].items():
                    add((k, v), False)
        for t in list(reads) + list(writes):
            pend = self.buf_pending.get(self._buf(t))
            if pend:
                for k, v in pend.items():
                    add((k, v), k != "pe")
        out = []
        wd = self.waited[eng]
        for k, v in deps.items():
            if wd.get(k, 0) >= v:
                continue
            wd[k] = v
            out.append((k, v))
        return out

    def _commit(self, tok, reads, writes):
        k, v = tok
        for r in reads:
            st = self.res.setdefault(r, {"w": None, "r": {}})
            if st["r"].get(k, 0) < v:
                st["r"][k] = v
        for w in writes:
            self.res[w] = {"w": tok, "r": {}}

    def collect(self, bufname):
        toks = {}
        for tag, st in self.res.items():
            if self._buf(tag) != bufname:
                continue
            if st["w"] is not None:
                k, v = st["w"]
                toks[k] = max(toks.get(k, 0), v)
            for k, v in st["r"].items():
                toks[k] = max(toks.get(k, 0), v)
        for tag in [t for t in self.res if self._buf(t) == bufname]:
            del self.res[tag]
        return toks

    def op(self, eng, fn, reads=(), writes=()):
        waits = self._deps(eng, reads, writes)
        self.seq[eng] += 1
        tok = (eng, self.seq[eng])
        self.ops[eng].append([waits, fn, tok])
        self._commit(tok, reads, writes)
        self.n_ops += 1
        return tok

    def dma(self, eng, fn, reads=(), writes=(), final=False):
        waits = self._deps(eng, reads, writes)
        i = self.dma_cnt[eng] % self.n_dma_sems
        self.dma_cnt[eng] += 1
        key = "d_%s_%d" % (eng, i)
        prev = self.dma_val.get(key, 0)
        if prev and self.waited[eng].get(key, 0) < prev:
            self.waited[eng][key] = prev
            waits.append((key, prev))
        tok = (key, prev + 16)
        self.dma_val[key] = prev + 16
        self.ops[eng].append([waits, fn, tok])
        self._commit(tok, reads, writes)
        if final:
            self.final_tokens.append(tok)
        self.n_ops += 1
        return tok

    def wait_tokens(self, eng, toks):
        waits = []
        for k, v in toks:
            if self.waited[eng].get(k, 0) < v:
                self.waited[eng][k] = v
                waits.append((k, v))
        if waits:
            self.ops[eng].append([waits, None, None])

    def emit(self):
        nc = self.nc
        needed = {e: set() for e in self.COMPUTE}
        for e, lst in self.ops.items():
            for waits, fn, tok in lst:
                for k, v in waits:
                    if k in needed:
                        needed[k].add(v)
        remap = {e: {v: i + 1 for i, v in enumerate(sorted(needed[e]))} for e in self.COMPUTE}
        sem_keys = list(self.COMPUTE) + sorted(self.dma_val.keys())
        with contextlib.ExitStack() as st:
            sems = {k: st.enter_context(nc.semaphore("s_" + k)) for k in sem_keys}
            block = st.enter_context(nc.Block())

            def replay(ename, e):
                for waits, fn, tok in self.ops[ename]:
                    for k, v in waits:
                        vv = remap[k][v] if k in remap else v
                        e.wait_ge(sems[k], vv)
                    if fn is None:
                        continue
                    inst = fn(e)
                    k, v = tok
                    if k in remap:
                        if v in remap[k]:
                            inst.then_inc(sems[k], 1)
                    else:
                        inst.then_inc(sems[k], 16)

            @block.tensor
            def _(e):
                replay("pe", e)

            @block.scalar
            def _(e):
                replay("act", e)

            @block.vector
            def _(e):
                replay("dve", e)

            @block.gpsimd
            def _(e):
                replay("pool", e)

            @block.sync
            def _(e):
                replay("sp", e)


class Arena:
    def __init__(self, nc, S, lo=16512, hi=229344):
        self.nc, self.S = nc, S
        self.free = [(lo, hi)]
        self.live = {}
        self.freed = []
        self.uid = 0
        self.peak = 0
        self.hi = hi

    def alloc(self, name, shape, dtype):
        nbytes = int(np.prod(shape[1:])) * mybir.dt.size(dtype)
        nbytes = (nbytes + 31) // 32 * 32
        for i, (a, b) in enumerate(self.free):
            if b - a >= nbytes:
                off = a
                if b - a == nbytes:
                    self.free.pop(i)
                else:
                    self.free[i] = (a + nbytes, b)
                break
        else:
            raise RuntimeError("SBUF arena full allocating %s (%d B); live=%s" % (
                name, nbytes, {k: v[1] for k, v in self.live.items()}))
        assert name not in self.live, name
        self.live[name] = (off, nbytes)
        self.peak = max(self.peak, off + nbytes)
        pend = {}
        for (a, b, toks) in self.freed:
            if a < off + nbytes and off < b:
                for k, v in toks.items():
                    pend[k] = max(pend.get(k, 0), v)
        self.S.buf_pending[name] = pend
        self.uid += 1
        return self.nc.alloc_sbuf_tensor_at("%s_%d" % (name, self.uid), list(shape), dtype, offset=off)

    def release(self, *names):
        for name in names:
            off, nbytes = self.live.pop(name)
            toks = self.S.collect(name)
            pend = self.S.buf_pending.pop(name, {})
            for k, v in pend.items():
                toks[k] = max(toks.get(k, 0), v)
            self.freed.append((off, off + nbytes, toks))
            self.free.append((off, off + nbytes))
            self.free.sort()
            merged = []
            for a, b in self.free:
                if merged and merged[-1][1] == a:
                    merged[-1] = (merged[-1][0], b)
                else:
                    merged.append((a, b))
            self.free = merged


def AP(t, off, dims, nparts=128, rowlen=None):
    if rowlen is None:
        rowlen = int(np.prod(list(t.shape)[1:]))
    return bass.AP(t, off, [[rowlen, nparts]] + [list(d) for d in dims])


VEC_OFF = {}
_o = 0
for _n, _w in [("bmod0", 96), ("bmod1", 96), ("gmix0", 16), ("gmix1", 16), ("gffn0", 16), ("gffn1", 16),
               ("gfin", 8), ("conv", 40), ("lng", 16), ("lnb", 16)]:
    VEC_OFF[_n] = _o
    _o += _w
NV = _o

ROW_OFF = {}
_o = 0
for _n, _w in [("sameb", 128), ("gate_b", 16), ("head_g", 512), ("sink", 8), ("mf", 8), ("mb", 8), ("vl", 1), ("vr", 1)]:
    ROW_OFF[_n] = _o
    _o += _w
NR = _o


def fm_cols(v):
    v = np.asarray(v, np.float32)
    return np.ascontiguousarray(v.reshape(-1, 128).T)


def dup2(a):
    return np.repeat(a, 2, axis=1)


class K:
    pass


def build(cfg):
    nc = bass.Bass("TRN2", target_bir_lowering=False)
    k = K()
    k.nc = nc
    k.cfg = cfg
    k.in_names = []

    def din(n, s):
        k.in_names.append(n)
        return nc.dram_tensor(n, list(s), F32, kind="ExternalInput").ap()
    has_moe = any(p.startswith("moe") for p in cfg["phases"])
    k.xT = din("xT", [128, KC, T + 256])
    k.xin = din("xin", [128, KC, T])
    k.ctxT = din("ctxT", [128, KC, 256])
    k.cT = din("cT", [128, 16])
    k.w_mod = din("w_mod", [2, D, 6 * D])
    k.vec = din("vec", [128, NV])
    k.rowb = din("rowb", [128, NR])
    k.ident = din("ident", [128, 128])
    k.bsrow = din("bsrow", [128, 1024])
    k.w_router = din("w_router", [2, D, NEXP])
    if has_moe:
        k.w_gate = din("w_gate", [2, NEXP, D, D])
        k.w_up = din("w_up", [2, NEXP, D, D])
        k.w_down = din("w_down", [2, NEXP, D, D])
    k.w_in_fm = din("w_in_fm", [D, 2304])
    k.w_in_tm = din("w_in_tm", [D, 1168])
    k.ab_w_out = din("ab_w_out", [D, D])
    k.cosT = din("cosT", [128, T + 256])
    k.sinT = din("sinT", [128, T + 256])
    k.amask = din("amask", [128, 2048])
    k.tri = din("tri", [128, 512])
    k.iota = din("iota", [128, 129])
    k.gm_w_in = din("gm_w_in", [D, 4096])
    k.gm_wsT = din("gm_wsT", [128, 8 * 128])
    k.gm_w_out = din("gm_w_out", [2048, D])
    k.outT = nc.dram_tensor("outT", [128, KC, T], F32, kind="ExternalOutput").ap()
    k.ag2_src = nc.dram_tensor("ag2_src", [128, 1040], F32)
    k.ag2_dst = nc.dram_tensor("ag2_dst", [NCORES * 128, 1040], F32)
    k.ag_src = [nc.dram_tensor("ag_src%d" % l, [128, 256], F32) for l in range(2)]
    k.ag_dst = [nc.dram_tensor("ag_dst%d" % l, [NCORES * 128, 256], F32) for l in range(2)]

    S = k.S = Sched(nc)
    A = k.A = Arena(nc, S)
    with contextlib.ExitStack() as st:
        k.PS = [st.enter_context(nc.psum_tensor("ps%d" % i, [128, 512], F32)) for i in range(8)]
        setup_consts(k)
        phases = cfg["phases"]
        if "loadx" in phases:
            X = A.alloc("X", [128, KC, T], F32)
            k.X = X
            for b in range(NBLK):
                S.dma("sp", lambda e, b=b: e.dma_start(out=X[:, :, b * 512:(b + 1) * 512],
                                                       in_=k.xin[:, :, b * 512:(b + 1) * 512]),
                      writes=xt(b))
        for ph in phases:
            if ph == "mod0":
                phase_mod(k, 0)
            elif ph == "mod1":
                phase_mod(k, 1)
            elif ph == "moe0":
                phase_moe(k, 0)
            elif ph == "moe1":
                phase_moe(k, 1)
            elif ph == "gmlp":
                phase_gmlp(k)
            elif ph == "final":
                phase_final(k)
            elif ph == "hx0":
                mix0_hx(k)
            elif ph == "attn":
                phase_attn(k)
            elif ph == "lstm":
                phase_lstm(k)
            elif ph == "mixout":
                phase_mixout(k)
            elif ph == "store_bxt":
                S.dma("pool", lambda e: e.dma_start(out=k.outT[:, 4:8, :], in_=k.BXT[:]), reads=[("BXT", q) for q in range(NTILE)], final=True)
            elif ph == "store_axt":
                S.dma("pool", lambda e: e.dma_start(out=k.outT[:, 0:4, :], in_=k.AXT[:]), reads=[("AXT", q) for q in range(NTILE)], final=True)
            elif ph == "storex":
                for b in range(NBLK):
                    S.dma("sp", lambda e, b=b: e.dma_start(out=k.outT[:, :, b * 512:(b + 1) * 512],
                                                           in_=k.X[:, :, b * 512:(b + 1) * 512]),
                          reads=xt(b), final=True)
        S.wait_tokens("sp", S.final_tokens)
        S.emit()
    k.peak = A.peak
    build.last = k
    return nc


def xt(b):
    return [("X", d, b) for d in range(KC)]


def setup_consts(k):
    nc, S, A = k.nc, k.S, k.A
    k.VEC = A.alloc("VEC", [128, NV], F32)
    k.ROWB = A.alloc("ROWB", [128, NR], F32)
    k.ONES_F = A.alloc("ONES_F", [128, 128], F32)
    k.ONES_B = A.alloc("ONES_B", [128, 128], BF16)
    k.IDENT_F = A.alloc("IDENT_F", [128, 128], F32)
    k.IDENT_B = A.alloc("IDENT_B", [128, 128], BF16)
    k.EPSC = A.alloc("EPSC", [128, 1], F32)
    k.CS = A.alloc("CS", [128, 16], F32)
    k.MODV = [A.alloc("MODV%d" % l, [128, 96], F32) for l in range(2)]
    k.GS1 = [A.alloc("GS1_%d" % l, [128, 16], F32) for l in range(2)]
    k.GS2 = [A.alloc("GS2_%d" % l, [128, 16], F32) for l in range(2)]
    S.dma("sp", lambda e: e.dma_start(out=k.VEC[:], in_=k.vec), writes=["VEC"])
    S.dma("sp", lambda e: e.dma_start(out=k.ROWB[:], in_=k.rowb), writes=["ROWB"])
    S.dma("sp", lambda e: e.dma_start(out=k.IDENT_F[:], in_=k.ident), writes=["IDENT_F"])
    S.dma("sp", lambda e: e.dma_start(out=k.CS[:], in_=k.cT), writes=["CS"])
    S.dma("pool", lambda e: e.dma_start(out=k.IDENT_B[:], in_=k.ident), writes=["IDENT_B"])
    S.op("dve", lambda e: e.memset(k.ONES_F[:], 1.0), writes=["ONES_F"])
    S.op("dve", lambda e: e.memset(k.ONES_B[:], 1.0), writes=["ONES_B"])
    S.op("dve", lambda e: e.memset(k.EPSC[:], EPS), writes=["EPSC"])
    S.op("act", lambda e: e.activation(out=k.CS[:], in_=k.CS[:], func=AF.Silu), reads=["CS"], writes=["CS"])


def vcol(k, name, j):
    o = VEC_OFF[name] + j
    return k.VEC[:, o:o + 1]


def modcol(k, l, which, ch, c=0):
    j = 2 * (which * 8 + ch) + c
    return k.MODV[l][:, j:j + 1]


def mod_begin(k, l):
    S, A = k.S, k.A
    st = K()
    st.l = l
    st.NW = 3
    st.WM = [A.alloc("WM%d" % i, [128, KC, 512], BF16) for i in range(st.NW)]
    st.CSB = A.alloc("CSB", [128, 16], BF16)
    st.ROWM = A.alloc("ROWM", [2, 6 * D], F32)
    S.op("dve", lambda e: e.tensor_copy(out=st.CSB[:], in_=k.CS[:]), reads=["CS"], writes=["CSB"])
    st.src = k.w_mod[l].rearrange("(k p) n -> p k n", p=128)
    for s in range(st.NW):
        mod_load(k, st, s)
    return st


def mod_load(k, st, s):
    wm = st.WM[s % st.NW]
    k.S.dma("pool", lambda e: e.dma_start(out=wm[:], in_=st.src[:, :, s * 512:(s + 1) * 512]), writes=["WM%d" % (s % st.NW)])


def mod_slab(k, st, s):
    S = k.S
    wm = st.WM[s % st.NW]
    psi = s % 2
    ps = k.PS[psi]
    for kk in range(KC):
        S.op("pe", lambda e, kk=kk: e.matmul(ps[0:2, :], lhsT=st.CSB[:, 2 * kk:2 * kk + 2], rhs=wm[:, kk, :],
                                             start=(kk == 0), stop=(kk == KC - 1)),
             reads=["WM%d" % (s % st.NW), "CSB"], writes=[("ps", psi)])
    S.op("act", lambda e: e.activation(out=st.ROWM[:, s * 512:(s + 1) * 512], in_=ps[0:2, :], func=AF.Copy),
         reads=[("ps", psi)], writes=[("ROWM", s)])
    if s + st.NW < 12:
        mod_load(k, st, s + st.NW)


def mod_end(k, st):
    S, A = k.S, k.A
    l = st.l
    pst = k.PS[7]
    for oc in range(48):
        S.op("pe", lambda e, oc=oc: e.transpose(out=pst[:, 2 * oc:2 * oc + 2], in_=st.ROWM[0:2, oc * 128:(oc + 1) * 128], identity=k.IDENT_F[0:2, 0:2]),
             reads=[("ROWM", oc // 4), "IDENT_F"], writes=[("ps", 7)])
    bo = VEC_OFF["bmod%d" % l]
    S.op("dve", lambda e: e.tensor_tensor(out=k.MODV[l][:], in0=pst[:, 0:96], in1=k.VEC[:, bo:bo + 96], op=ALU.add),
         reads=[("ps", 7), "VEC"], writes=["MODV%d" % l])
    go = VEC_OFF["gmix%d" % l]
    S.op("dve", lambda e: e.scalar_tensor_tensor(out=k.GS1[l][:], in0=k.MODV[l][:, 16:32], scalar=1.0,
                                                 in1=k.VEC[:, go:go + 16], op0=ALU.add, op1=ALU.mult),
         reads=["MODV%d" % l, "VEC"], writes=["GS1_%d" % l])
    go2 = VEC_OFF["gffn%d" % l]
    S.op("dve", lambda e: e.scalar_tensor_tensor(out=k.GS2[l][:], in0=k.MODV[l][:, 64:80], scalar=1.0,
                                                 in1=k.VEC[:, go2:go2 + 16], op0=ALU.add, op1=ALU.mult),
         reads=["MODV%d" % l, "VEC"], writes=["GS2_%d" % l])
    A.release("CSB", "ROWM", *["WM%d" % i for i in range(st.NW)])


def phase_mod(k, l):
    st = mod_begin(k, l)
    for s in range(12):
        mod_slab(k, st, s)
    mod_end(k, st)


def rstd_block(k, src_ap, n, src_tags, SQ, sq_tag, RS, rs_tag, psi):
    S = k.S
    ps = k.PS[psi]
    sq_tags = sq_tag if isinstance(sq_tag, list) else [sq_tag]
    S.op("act", lambda e: e.activation(out=SQ[:, :, 0:n], in_=src_ap, func=AF.Square), reads=src_tags, writes=sq_tags)
    for kk in range(KC):
        S.op("pe", lambda e, kk=kk: e.matmul(ps[:, 0:n], lhsT=k.ONES_F[:], rhs=SQ[:, kk, 0:n],
                                             start=(kk == 0), stop=(kk == KC - 1)),
             reads=sq_tags + ["ONES_F"], writes=[("ps", psi)])
    S.op("act", lambda e: e.activation(out=RS[:, 0:n], in_=ps[:, 0:n], func=AF.Ln, bias=k.EPSC[:, 0:1], scale=1.0 / D),
         reads=[("ps", psi), "EPSC"], writes=[rs_tag])
    S.op("act", lambda e: e.activation(out=RS[:, 0:n], in_=RS[:, 0:n], func=AF.Exp, scale=-0.5),
         reads=[rs_tag], writes=[rs_tag])


def phase_final(k):
    nc, S, A = k.nc, k.S, k.A
    SQ = A.alloc("SQ", [128, KC, 512], F32)
    RS = A.alloc("RS", [128, 512], F32)
    OB = [A.alloc("OB%d" % i, [128, KC, 512], F32) for i in range(1)]
    for b in range(NBLK):
        sl = slice(b * 512, (b + 1) * 512)
        rstd_block(k, k.X[:, :, sl], 512, xt(b), SQ, "SQ", RS, "RS", 6)
        ob = OB[0]
        for kk in range(KC):
            S.op("dve", lambda e, kk=kk, sl=sl: e.scalar_tensor_tensor(
                out=ob[:, kk, :], in0=k.X[:, kk, sl], scalar=vcol(k, "gfin", kk), in1=RS[:], op0=ALU.mult, op1=ALU.mult),
                reads=[("X", kk, b), "RS", "VEC"], writes=[("OB0", kk)])
        S.dma("sp", lambda e, sl=sl: e.dma_start(out=k.outT[:, :, sl], in_=ob[:]),
              reads=[("OB0", kk) for kk in range(KC)], final=True)
    A.release("SQ", "RS", "OB0")


def phase_moe(k, l):
    nc, S, A = k.nc, k.S, k.A
    X = k.X
    sparse = k.cfg.get("sparse", True)
    htc = [0]
    H2 = A.alloc("H2", [128, KC, T], BF16) if not sparse else A.alloc("H2", [128, NTILE, D], BF16)
    AFF = A.alloc("AFF", [128, 256], F32)
    WR = A.alloc("WR", [128, KC, NEXP], F32)
    S.dma("sp", lambda e: e.dma_start(out=WR[:], in_=k.w_router[l].rearrange("(k p) n -> p k n", p=128)), writes=["WR"])
    SQ = A.alloc("SQ", [128, KC, 512], F32)
    TB = A.alloc("TB", [128, KC, 512], F32)
    RS = A.alloc("RS", [128, 512], F32)
    EX = A.alloc("EX", [128, 64], F32)
    SM = A.alloc("SM", [128, 8], F32)
    for b in range(NBLK):
        sl = slice(b * 512, (b + 1) * 512)
        rstd_block(k, X[:, :, sl], 512, xt(b), SQ, "SQ", RS, "RS", 6)
        for kk in range(KC):
            S.op("dve", lambda e, kk=kk, sl=sl: e.scalar_tensor_tensor(
                out=TB[:, kk, :], in0=X[:, kk, sl], scalar=k.GS2[l][:, 2 * kk:2 * kk + 1], in1=RS[:],
                op0=ALU.mult, op1=ALU.mult),
                reads=[("X", kk, b), "RS", "GS2_%d" % l], writes=[("TB", kk)])
            S.op("act", lambda e, kk=kk: e.activation(out=TB[:, kk, :], in_=TB[:, kk, :], func=AF.Identity,
                                                      bias=modcol(k, l, 3, kk), scale=1.0),
                 reads=[("TB", kk), "MODV%d" % l], writes=[("TB", kk)])
            if not sparse:
                S.op("dve", lambda e, kk=kk, sl=sl: e.tensor_copy(out=H2[:, kk, sl], in_=TB[:, kk, :]),
                     reads=[("TB", kk)], writes=[("H2", kk, b)])
        if sparse:
            for t in range(4):
                for half in range(2):
                    hc_ = htc[0]
                    htc[0] += 1
                    pst = k.PS[hc_ % 4]
                    for j in range(4):
                        kk = half * 4 + j
                        S.op("pe", lambda e, pst=pst, j=j, kk=kk, t=t: e.transpose(out=pst[:, j * 128:(j + 1) * 128],
                                                                                    in_=TB[:, kk, t * 128:(t + 1) * 128], identity=k.IDENT_F[:]),
                             reads=[("TB", kk), "IDENT_F"], writes=[("ps", hc_ % 4)])
                    eng = "act" if hc_ % 2 == 0 else "dve"
                    dst = H2[:, b * 4 + t, half * 512:(half + 1) * 512]
                    if eng == "act":
                        S.op("act", lambda e, pst=pst, dst=dst: e.activation(out=dst, in_=pst[:], func=AF.Copy),
                             reads=[("ps", hc_ % 4)], writes=[("H2", b * 4 + t)])
                    else:
                        S.op("dve", lambda e, pst=pst, dst=dst: e.tensor_copy(out=dst, in_=pst[:]),
                             reads=[("ps", hc_ % 4)], writes=[("H2", b * 4 + t)])
        psr = k.PS[7]
        for t in range(4):
            for kk in range(KC):
                S.op("pe", lambda e, t=t, kk=kk: e.matmul(psr[:, t * 16:(t + 1) * 16], lhsT=TB[:, kk, t * 128:(t + 1) * 128],
                                                          rhs=WR[:, kk, :], start=(kk == 0), stop=(kk == KC - 1)),
                     reads=[("TB", kk), "WR"], writes=[("ps", 7)])
        S.op("act", lambda e: e.activation(out=EX[:], in_=psr[:, 0:64], func=AF.Exp), reads=[("ps", 7)], writes=["EX"])
        S.op("dve", lambda e: e.tensor_reduce(out=SM[:, 0:4], in_=AP(EX, 0, [[16, 4], [1, 16]]), axis=AX.X, op=ALU.add),
             reads=["EX"], writes=["SM"])
        S.op("dve", lambda e: e.reciprocal(out=SM[:, 4:8], in_=SM[:, 0:4]), reads=["SM"], writes=["SM"])
        S.op("dve", lambda e, b=b: e.tensor_tensor(out=AP(AFF, b * 64, [[16, 4], [1, 16]]), in0=AP(EX, 0, [[16, 4], [1, 16]]),
                                                   in1=AP(SM, 4, [[1, 4], [0, 16]]), op=ALU.mult),
             reads=["EX", "SM"], writes=["AFF"])
    A.release("SQ", "TB", "RS", "EX", "SM", "WR")
    S.dma("pool", lambda e: e.dma_start(out=k.ag_src[l].ap(), in_=AFF[:]), reads=["AFF"], writes=["ag_src%d" % l])
    S.op("pool", lambda e: e.collective_compute("AllGather", ALU.bypass, replica_groups=[list(range(NCORES))],
                                                ins=[k.ag_src[l].ap().opt()], outs=[k.ag_dst[l].ap().opt()]),
         reads=["ag_src%d" % l], writes=["ag_dst%d" % l])
    AFA = A.alloc("AFA", [128, 2048], F32)
    S.dma("sp", lambda e: e.dma_start(out=AP(AFA, 0, [[256, 8], [1, 256]]),
                                      in_=k.ag_dst[l].ap().rearrange("(r p) n -> p r n", p=128)),
          reads=["ag_dst%d" % l], writes=["AFA"])
    MK = A.alloc("MK", [128, 2048], BF16)
    C1 = A.alloc("C1", [128, 128], F32)
    C2 = A.alloc("C2", [128, 16], F32)
    LO = A.alloc("LO", [128, 16], F32)
    MID = A.alloc("MID", [128, 16], F32)
    GE = A.alloc("GE", [128, 16], F32)
    sb_o = ROW_OFF["sameb"]
    S.op("dve", lambda e: e.memset(LO[:], 0.0), writes=["LO"])
    psc = k.PS[7]
    bg = mod_begin(k, 1) if (l == 0 and k.cfg.get("hide_mod1")) else None
    for it in range(30):
        if bg is not None and it < 12:
            mod_slab(k, bg, it)
        c = 2.0 ** (-(it + 1))
        S.op("dve", lambda e, c=c: e.tensor_scalar(out=MID[:], in0=LO[:], scalar1=c, scalar2=None, op0=ALU.add),
             reads=["LO"], writes=["MID"])
        S.op("dve", lambda e: e.tensor_tensor(out=AP(MK, 0, [[16, 128], [1, 16]]), in0=AP(AFA, 0, [[16, 128], [1, 16]]),
                                              in1=AP(MID, 0, [[0, 128], [1, 16]]), op=ALU.is_gt),
             reads=["AFA", "MID"], writes=["MK"])
        S.op("dve", lambda e: e.tensor_reduce(out=C1[:], in_=AP(MK, 0, [[256, 8], [1, 16], [16, 16]]), axis=AX.X, op=ALU.add),
             reads=["MK"], writes=["C1"])
        S.op("dve", lambda e: e.tensor_tensor(out=C1[:], in0=C1[:], in1=k.ROWB[:, sb_o:sb_o + 128], op=ALU.mult),
             reads=["C1", "ROWB"], writes=["C1"])
        S.op("dve", lambda e: e.tensor_reduce(out=C2[:], in_=AP(C1, 0, [[1, 16], [16, 8]]), axis=AX.X, op=ALU.add),
             reads=["C1"], writes=["C2"])
        S.op("pe", lambda e: e.matmul(psc[:, 0:16], lhsT=k.ONES_F[:], rhs=C2[:], start=True, stop=True),
             reads=["C2", "ONES_F"], writes=[("ps", 7)])
        S.op("dve", lambda e, c=c: e.tensor_scalar(out=GE[:], in0=psc[:, 0:16], scalar1=CAP - 0.5, scalar2=c,
                                                   op0=ALU.is_ge, op1=ALU.mult),
             reads=[("ps", 7)], writes=["GE"])
        S.op("dve", lambda e: e.tensor_tensor(out=LO[:], in0=LO[:], in1=GE[:], op=ALU.add), reads=["LO", "GE"], writes=["LO"])
    if bg is not None:
        mod_end(k, bg)
    CW = A.alloc("CW", [128, 256], F32)
    S.op("dve", lambda e: e.tensor_tensor(out=AP(CW, 0, [[16, 16], [1, 16]]), in0=AP(AFF, 0, [[16, 16], [1, 16]]),
                                          in1=AP(LO, 0, [[0, 16], [1, 16]]), op=ALU.is_gt),
         reads=["AFF", "LO"], writes=["CW"])
    S.op("dve", lambda e: e.tensor_tensor(out=CW[:], in0=CW[:], in1=AFF[:], op=ALU.mult), reads=["CW", "AFF"], writes=["CW"])
    A.release("AFA", "MK", "C1", "C2", "MID", "GE")
    if sparse:
        moe_sparse_experts(k, l, H2, AFF, LO, CW)
        return
    NSLOT = 6
    WS = [A.alloc("WS%d" % i, [128, KC, 512], BF16) for i in range(NSLOT)]
    HID = A.alloc("HID", [128, KC, T], BF16)
    CWB = [A.alloc("CWB%d" % i, [128, T], BF16) for i in range(2)]
    SG = [A.alloc("SG%d" % i, [128, 512], BF16) for i in range(2)]
    T1 = [A.alloc("T1_%d" % i, [128, 512], BF16) for i in range(2)]
    slot_ctr = [0]

    def load_slab(w_ap, e_idx, h):
        i = slot_ctr[0] % NSLOT
        slot_ctr[0] += 1
        src = w_ap[l, e_idx].rearrange("(k p) n -> p k n", p=128)
        S.dma("pool", lambda e: e.dma_start(out=WS[i][:], in_=src[:, :, h * 512:(h + 1) * 512]), writes=["WS%d" % i])
        return i

    def issue_loads(e_idx):
        return {"g": [load_slab(k.w_gate, e_idx, 0), None], "u": [load_slab(k.w_up, e_idx, 0), None], "e": e_idx}

    def loads_for(e_idx):
        ids = {}
        ids["g0"] = load_slab(k.w_gate, e_idx, 0)
        ids["u0"] = load_slab(k.w_up, e_idx, 0)
        ids["g1"] = load_slab(k.w_gate, e_idx, 1)
        ids["u1"] = load_slab(k.w_up, e_idx, 1)
        ids["d0"] = load_slab(k.w_down, e_idx, 0)
        ids["d1"] = load_slab(k.w_down, e_idx, 1)
        return ids

    cnt = [0]
    ids = loads_for(0)
    for ex in range(NEXP):
        cwb = CWB[ex % 2]
        cwbt = "CWB%d" % (ex % 2)
        for b in range(NBLK):
            psb = k.PS[6]
            for t in range(4):
                tt = b * 4 + t
                S.op("pe", lambda e, t=t, tt=tt, ex=ex: e.matmul(psb[:, t * 128:(t + 1) * 128],
                                                                lhsT=AP(CW, tt * 16 + ex, [[0, 128]]),
                                                                rhs=k.IDENT_F[:], start=True, stop=True),
                     reads=["CW", "IDENT_F"], writes=[("ps", 6)])
            S.op("act", lambda e, b=b, cwb=cwb: e.activation(out=cwb[:, b * 512:(b + 1) * 512], in_=psb[:], func=AF.Copy),
                 reads=[("ps", 6)], writes=[(cwbt, b)])
        cur = ids
        for h in range(2):
            gs, us = cur["g%d" % h], cur["u%d" % h]
            for b in range(NBLK):
                sl = slice(b * 512, (b + 1) * 512)
                for fi in range(4):
                    f = 4 * h + fi
                    c = cnt[0]
                    cnt[0] += 1
                    pg, pu = k.PS[c % 2], k.PS[2 + c % 2]
                    for kk in range(KC):
                        S.op("pe", lambda e, kk=kk, pg=pg, gs=gs, fi=fi, sl=sl: e.matmul(
                            pg[:], lhsT=WS[gs][:, kk, fi * 128:(fi + 1) * 128], rhs=H2[:, kk, sl],
                            start=(kk == 0), stop=(kk == KC - 1)),
                            reads=["WS%d" % gs, ("H2", kk, b)], writes=[("ps", c % 2)])
                    for kk in range(KC):
                        S.op("pe", lambda e, kk=kk, pu=pu, us=us, fi=fi, sl=sl: e.matmul(
                            pu[:], lhsT=WS[us][:, kk, fi * 128:(fi + 1) * 128], rhs=H2[:, kk, sl],
                            start=(kk == 0), stop=(kk == KC - 1)),
                            reads=["WS%d" % us, ("H2", kk, b)], writes=[("ps", 2 + c % 2)])
                    sg, t1 = SG[c % 2], T1[c % 2]
                    S.op("act", lambda e, sg=sg, pg=pg: e.activation(out=sg[:], in_=pg[:], func=AF.Silu),
                         reads=[("ps", c % 2)], writes=["SG%d" % (c % 2)])
                    S.op("dve", lambda e, sg=sg, t1=t1, pu=pu: e.tensor_tensor(out=t1[:], in0=sg[:], in1=pu[:], op=ALU.mult),
                         reads=["SG%d" % (c % 2), ("ps", 2 + c % 2)], writes=["T1_%d" % (c % 2)])
                    S.op("dve", lambda e, t1=t1, f=f, sl=sl, cwb=cwb: e.tensor_tensor(out=HID[:, f, sl], in0=t1[:], in1=cwb[:, sl],
                                                                                 op=ALU.mult),
                         reads=["T1_%d" % (c % 2), (cwbt, b)], writes=[("HID", f, b)])
        if ex + 1 < NEXP:
            nxt = {}
            nxt["g0"] = load_slab(k.w_gate, ex + 1, 0)
            nxt["u0"] = load_slab(k.w_up, ex + 1, 0)
            nxt["g1"] = load_slab(k.w_gate, ex + 1, 1)
            nxt["u1"] = load_slab(k.w_up, ex + 1, 1)
        for h in range(2):
            ds = cur["d%d" % h]
            for b in range(NBLK):
                sl = slice(b * 512, (b + 1) * 512)
                for di in range(4):
                    d = 4 * h + di
                    c = cnt[0]
                    cnt[0] += 1
                    pd = k.PS[4 + c % 2]
                    for fk in range(KC):
                        S.op("pe", lambda e, fk=fk, pd=pd, ds=ds, di=di, sl=sl: e.matmul(
                            pd[:], lhsT=WS[ds][:, fk, di * 128:(di + 1) * 128], rhs=HID[:, fk, sl],
                            start=(fk == 0), stop=(fk == KC - 1)),
                            reads=["WS%d" % ds, ("HID", fk, b)], writes=[("ps", 4 + c % 2)])
                    S.op("dve", lambda e, pd=pd, d=d, sl=sl: e.scalar_tensor_tensor(
                        out=X[:, d, sl], in0=pd[:], scalar=modcol(k, l, 5, d), in1=X[:, d, sl], op0=ALU.mult, op1=ALU.add),
                        reads=[("ps", 4 + c % 2), ("X", d, b), "MODV%d" % l], writes=[("X", d, b)])
        if ex + 1 < NEXP:
            nxt["d0"] = load_slab(k.w_down, ex + 1, 0)
            nxt["d1"] = load_slab(k.w_down, ex + 1, 1)
            ids = nxt
    A.release("H2", "AFF", "LO", "CW", "HID", "CWB0", "CWB1", "SG0", "SG1", "T1_0", "T1_1",
              *["WS%d" % i for i in range(NSLOT)])


def phase_gmlp(k):
    nc, S, A = k.nc, k.S, k.A
    X = k.X
    l = 1
    GELU = AF.Gelu_apprx_tanh
    HB = A.alloc("HB", [128, KC, 512], BF16)
    UT = A.alloc("UT", [128, 16, 512], BF16)
    VN = A.alloc("VN", [128, 4, 2048], BF16)
    UV = A.alloc("UV", [128, 16, 512], BF16)
    SQ = A.alloc("SQ", [128, KC, 512], F32)
    TBK = [A.alloc("TBK%d" % i, [128, 512], F32) for i in range(2)]
    RS = A.alloc("RS", [128, 512], F32)
    B2 = A.alloc("B2", [128, 16 * 128], F32)
    WST = A.alloc("WST", [128, 1024], BF16)
    BSR = A.alloc("BSR", [128, 1024], F32)
    ST = A.alloc("ST", [128, 32], F32)
    NSL = 4
    WS = [A.alloc("GW%d" % i, [128, 4096], BF16) for i in range(NSL)]
    sc = [0]
    S.dma("pool", lambda e: e.dma_start(out=WST[:], in_=k.gm_wsT), writes=["WST"])
    S.dma("sp", lambda e: e.dma_start(out=BSR[:], in_=k.bsrow), writes=["BSR"])
    for j in range(16):
        g = j // 2
        ps = k.PS[6 + j % 2]
        S.op("pe", lambda e, g=g, ps=ps: e.matmul(ps[:, 0:128], lhsT=k.ONES_B[:], rhs=WST[:, g * 128:(g + 1) * 128], start=True, stop=True),
             reads=["WST", "ONES_B"], writes=[("ps", 6 + j % 2)])
        S.op("dve", lambda e, j=j, g=g, ps=ps: e.scalar_tensor_tensor(
            out=B2[:, j * 128:(j + 1) * 128], in0=ps[:, 0:128], scalar=vcol(k, "lnb", j), in1=BSR[:, g * 128:(g + 1) * 128],
            op0=ALU.mult, op1=ALU.add), reads=[("ps", 6 + j % 2), "BSR", "VEC"], writes=[("B2", j)])

    def load_in(c0):
        i = sc[0] % NSL
        sc[0] += 1
        src = k.gm_w_in.rearrange("(k p) n -> p k n", p=128)
        S.dma("pool", lambda e: e.dma_start(out=AP(WS[i], 0, [[512, KC], [1, 512]]), in_=src[:, :, c0:c0 + 512]), writes=["GW%d" % i])
        return i

    def load_out(c0):
        i = sc[0] % NSL
        sc[0] += 1
        src = k.gm_w_out.rearrange("(k p) n -> p k n", p=128)
        S.dma("pool", lambda e: e.dma_start(out=AP(WS[i], 0, [[256, 16], [1, 256]]), in_=src[:, :, c0:c0 + 256]), writes=["GW%d" % i])
        return i

    cnt = [0]
    for b in range(NBLK):
        sl = slice(b * 512, (b + 1) * 512)
        rstd_block(k, X[:, :, sl], 512, xt(b), SQ, [("SQ", h_, v_) for h_ in range(2) for v_ in range(4)], RS, "RS", 6)
        for kk in range(KC):
            tb = TBK[kk % 2]
            S.op("dve", lambda e, kk=kk, sl=sl, tb=tb: e.scalar_tensor_tensor(
                out=tb[:], in0=X[:, kk, sl], scalar=k.GS1[l][:, 2 * kk:2 * kk + 1], in1=RS[:], op0=ALU.mult, op1=ALU.mult),
                reads=[("X", kk, b), "RS", "GS1_%d" % l], writes=["TBK%d" % (kk % 2)])
            S.op("act", lambda e, kk=kk, tb=tb: e.activation(out=HB[:, kk, :], in_=tb[:], func=AF.Identity,
                                                              bias=modcol(k, l, 0, kk), scale=1.0),
                 reads=["TBK%d" % (kk % 2), "MODV%d" % l], writes=[("HB", kk)])
        hbt = [("HB", kk) for kk in range(KC)]
        for us in range(4):
            wi = load_in(us * 512)
            for fi in range(4):
                fc = us * 4 + fi
                c = cnt[0]
                cnt[0] += 1
                ps = k.PS[c % 4]
                for kk in range(KC):
                    S.op("pe", lambda e, kk=kk, ps=ps, wi=wi, fi=fi: e.matmul(
                        ps[:], lhsT=AP(WS[wi], kk * 512 + fi * 128, [[1, 128]]), rhs=HB[:, kk, :], start=(kk == 0), stop=(kk == KC - 1)),
                        reads=["GW%d" % wi, ("HB", kk)], writes=[("ps", c % 4)])
                S.op("act", lambda e, fc=fc, ps=ps: e.activation(out=UT[:, fc, :], in_=ps[:], func=GELU),
                     reads=[("ps", c % 4)], writes=[("UT", fc)])
        vsl = [load_in(2048 + vs * 512) for vs in range(4)]
        for t in range(4):
            for vs in range(4):
                wi = vsl[vs]
                c = cnt[0]
                cnt[0] += 1
                ps = k.PS[c % 4]
                for kk in range(KC):
                    S.op("pe", lambda e, kk=kk, ps=ps, wi=wi, t=t: e.matmul(
                        ps[:], lhsT=HB[:, kk, t * 128:(t + 1) * 128], rhs=AP(WS[wi], kk * 512, [[1, 512]]),
                        start=(kk == 0), stop=(kk == KC - 1)),
                        reads=["GW%d" % wi, ("HB", kk)], writes=[("ps", c % 4)])
                vg = AP(SQ, (t % 2) * 2048 + vs * 512, [[1, 512]])
                S.op("act", lambda e, ps=ps, vg=vg, vs=vs: e.activation(out=vg, in_=ps[:], func=GELU, accum_out=ST[:, vs:vs + 1]),
                     reads=[("ps", c % 4)], writes=[("SQ", t % 2, vs), ("ST", vs)])
                S.op("act", lambda e, vg=vg, vs=vs: e.activation(out=TBK[0][:], in_=vg, func=AF.Square, accum_out=ST[:, 4 + vs:5 + vs]),
                     reads=[("SQ", t % 2, vs)], writes=["TBK0", ("ST", 4 + vs)])
            stt = [("ST", i) for i in range(8)]
            S.op("dve", lambda e: e.tensor_reduce(out=ST[:, 8:10], in_=AP(ST, 0, [[4, 2], [1, 4]]), axis=AX.X, op=ALU.add),
                 reads=stt, writes=[("ST", 8)])
            S.op("dve", lambda e: e.tensor_scalar(out=ST[:, 10:12], in0=ST[:, 8:10], scalar1=1.0 / 2048, scalar2=None, op0=ALU.mult),
                 reads=[("ST", 8)], writes=[("ST", 10)])
            S.op("dve", lambda e: e.tensor_tensor(out=ST[:, 12:13], in0=ST[:, 10:11], in1=ST[:, 10:11], op=ALU.mult),
                 reads=[("ST", 10)], writes=[("ST", 12)])
            S.op("dve", lambda e: e.tensor_tensor(out=ST[:, 13:14], in0=ST[:, 11:12], in1=ST[:, 12:13], op=ALU.subtract),
                 reads=[("ST", 10), ("ST", 12)], writes=[("ST", 13)])
            S.op("act", lambda e: e.activation(out=ST[:, 14:15], in_=ST[:, 13:14], func=AF.Ln, bias=k.EPSC[:, 0:1], scale=1.0),
                 reads=[("ST", 13), "EPSC"], writes=[("ST", 14)])
            S.op("act", lambda e: e.activation(out=ST[:, 15:16], in_=ST[:, 14:15], func=AF.Exp, scale=-0.5),
                 reads=[("ST", 14)], writes=[("ST", 15)])
            S.op("dve", lambda e, t=t: e.tensor_scalar(out=VN[:, t, :], in0=AP(SQ, (t % 2) * 2048, [[1, 2048]]),
                                                       scalar1=ST[:, 10:11], scalar2=ST[:, 15:16], op0=ALU.subtract, op1=ALU.mult),
                 reads=[("SQ", t % 2, vs) for vs in range(4)] + [("ST", 10), ("ST", 15)], writes=[("VN", t)])
        for j in range(16):
            g = j // 2
            c = cnt[0]
            cnt[0] += 1
            ps = k.PS[c % 4]
            for t in range(4):
                S.op("pe", lambda e, t=t, j=j, g=g, ps=ps: e.matmul(
                    ps[:, t * 128:(t + 1) * 128], lhsT=VN[:, t, j * 128:(j + 1) * 128], rhs=WST[:, g * 128:(g + 1) * 128],
                    start=True, stop=True), reads=[("VN", t), "WST"], writes=[("ps", c % 4)])
            tb = TBK[j % 2]
            S.op("dve", lambda e, j=j, ps=ps, tb=tb: e.scalar_tensor_tensor(
                out=AP(tb, 0, [[128, 4], [1, 128]]), in0=AP(ps, 0, [[128, 4], [1, 128]]), scalar=vcol(k, "lng", j),
                in1=AP(B2, j * 128, [[0, 4], [1, 128]]), op0=ALU.mult, op1=ALU.add),
                reads=[("ps", c % 4), ("B2", j), "VEC"], writes=["TBK%d" % (j % 2)])
            S.op("dve", lambda e, j=j, tb=tb: e.tensor_tensor(out=UV[:, j, :], in0=tb[:], in1=UT[:, j, :], op=ALU.mult),
                 reads=["TBK%d" % (j % 2), ("UT", j)], writes=[("UV", j)])
        for os_ in range(4):
            wi = load_out(os_ * 256)
            for di in range(2):
                d = os_ * 2 + di
                c = cnt[0]
                cnt[0] += 1
                ps = k.PS[c % 4]
                for j in range(16):
                    S.op("pe", lambda e, j=j, ps=ps, wi=wi, di=di: e.matmul(
                        ps[:], lhsT=AP(WS[wi], j * 256 + di * 128, [[1, 128]]), rhs=UV[:, j, :], start=(j == 0), stop=(j == 15)),
                        reads=["GW%d" % wi, ("UV", j)], writes=[("ps", c % 4)])
                S.op("dve", lambda e, ps=ps, d=d, sl=sl: e.scalar_tensor_tensor(
                    out=X[:, d, sl], in0=ps[:], scalar=modcol(k, l, 2, d), in1=X[:, d, sl], op0=ALU.mult, op1=ALU.add),
                    reads=[("ps", c % 4), ("X", d, b), "MODV%d" % l], writes=[("X", d, b)])
    A.release("HB", "UT", "VN", "UV", "SQ", "TBK0", "TBK1", "RS", "B2", "WST", "BSR", "ST", *["GW%d" % i for i in range(NSL)])


def prep_inputs(inp, cfg):
    f32 = lambda a: np.ascontiguousarray(np.asarray(a, np.float32))
    x = f32(inp["x"])
    vec = np.zeros((128, NV), np.float32)

    def put(name, arr):
        o = VEC_OFF[name]
        vec[:, o:o + arr.shape[1]] = arr

    for l in range(2):
        put("bmod%d" % l, dup2(fm_cols(inp["b_mod"][l])))
        put("gmix%d" % l, dup2(fm_cols(inp["norm_mix_g"][l])))
        put("gffn%d" % l, dup2(fm_cols(inp["norm_ffn_g"][l])))
    put("gfin", fm_cols(inp["final_norm_g"]))
    cw = f32(inp["ab_conv_w"][0])
    put("conv", np.concatenate([fm_cols(cw[t]) for t in range(5)], axis=1))
    put("lng", fm_cols(inp["gm_ln_g"][0]))
    put("lnb", fm_cols(inp["gm_ln_b"][0]))
    ident = np.eye(128, dtype=np.float32)
    bs = f32(inp["gm_b_s"][0])
    bsrow = np.ascontiguousarray(np.broadcast_to(bs.reshape(1, 1024), (128, 1024)))
    wsT = f32(inp["gm_w_s"][0]).transpose(2, 0, 1).reshape(128, 8 * 128)
    w_in = f32(inp["ab_w_in"][0])
    aq, ak, av, ao, ag = w_in[:, 0:512], w_in[:, 512:1024], w_in[:, 1024:1536], w_in[:, 1536:2048], w_in[:, 2048:2064]
    bq, bk, bv = w_in[:, 2064:2576], w_in[:, 2576:2704], w_in[:, 2704:2832]

    def swp(w, nh):
        w = w.reshape(D, nh, 2, 32)
        return w[:, :, ::-1, :].reshape(D, nh * 64)
    perm = np.concatenate([np.arange(h * 64, (h + 1) * 64) for c_ in range(4) for h in (c_, 4 + c_)])
    w_in_fm = np.ascontiguousarray(np.concatenate([aq, ak, bq[:, perm], swp(bq, 8)[:, perm], bk, swp(bk, 2)], axis=1))
    w_in_tm = np.ascontiguousarray(np.concatenate([av, ao, ag, bv], axis=1))
    jj = np.arange(128)
    tri_f = (jj[:, None] <= jj[None, :]).astype(np.float32)
    tri_b = (jj[:, None] >= jj[None, :]).astype(np.float32)
    tri = np.ascontiguousarray(np.concatenate([tri_f, tri_b, (1 - tri_f) * 30000.0, (1 - tri_b) * 30000.0], axis=1).astype(np.float32))
    iota_t = np.ascontiguousarray(np.concatenate([np.broadcast_to(jj[None, :], (128, 128)), jj[:, None]], axis=1).astype(np.float32))
    band_prev = np.where(jj[None, :] <= jj[:, None], 0.0, -30000.0).astype(np.float32)
    band_next = np.where(jj[:, None] <= jj[None, :], 0.0, -30000.0).astype(np.float32)
    allneg = np.full((128, 128), -30000.0, np.float32)
    inv = (np.float32(10000.0) ** (-np.arange(16, dtype=np.float32) / np.float32(16))).astype(np.float32)
    maps = []
    xin_all = cfg.get("xin")
    for i in range(NCORES):
        b, s = i // 4, i % 4
        t0 = s * T
        xh = np.zeros((T + 256, D), np.float32)
        lo, hi = t0 - 128, t0 + T + 128
        a, bnd = max(lo, 0), min(hi, 8192)
        xh[a - lo:bnd - lo] = x[b, a:bnd]
        xT = np.ascontiguousarray(xh.T.reshape(KC, 128, T + 256).transpose(1, 0, 2))
        if xin_all is not None:
            xc = xin_all[b, t0:t0 + T]
        else:
            xc = x[b, t0:t0 + T]
        xin = np.ascontiguousarray(xc.T.reshape(KC, 128, T).transpose(1, 0, 2))
        ctxT = np.ascontiguousarray(f32(inp["ctx"][b]).T.reshape(KC, 128, 256).transpose(1, 0, 2))
        cT = np.zeros((128, KC, 2), np.float32)
        cT[:, :, 0] = fm_cols(inp["c"][b])
        cT[:, :, 1] = fm_cols(inp["c_ctx"])
        rowb = np.zeros((128, NR), np.float32)
        sameb = np.zeros((8, 16), np.float32)
        for r in range(NCORES):
            if r // 4 == b:
                sameb[r, :] = 1.0
        rowb[:, ROW_OFF["sameb"]:ROW_OFF["sameb"] + 128] = sameb.reshape(1, 128)
        for r in range(NCORES):
            if r // 4 == b and r % 4 < s:
                rowb[:, ROW_OFF["mf"] + r] = 1.0
            if r // 4 == b and r % 4 > s:
                rowb[:, ROW_OFF["mb"] + r] = 1.0
        rowb[:, ROW_OFF["vl"]] = 0.0 if s == 0 else 1.0
        rowb[:, ROW_OFF["vr"]] = 0.0 if s == 3 else 1.0
        rowb[:, ROW_OFF["gate_b"]:ROW_OFF["gate_b"] + 16] = f32(inp["ab_gate_b"][0])[None]
        rowb[:, ROW_OFF["head_g"]:ROW_OFF["head_g"] + 512] = f32(inp["ab_head_g"][0])[None]
        rowb[:, ROW_OFF["sink"]:ROW_OFF["sink"] + 8] = f32(inp["ab_sink"][0])[None]
        tg = np.arange(t0 - 128, t0 + T + 128)
        rowi = (tg // 64).astype(np.float32)
        coli = (tg % 64).astype(np.float32)
        ang = np.concatenate([rowi[:, None] * inv[None], coli[:, None] * inv[None]], axis=1).astype(np.float32)
        cs, sn = np.cos(ang).astype(np.float32), np.sin(ang).astype(np.float32)
        pidx = np.arange(128) % 64
        cosT = np.ascontiguousarray(cs[:, pidx % 32].T)
        sinT = np.ascontiguousarray((sn[:, pidx % 32] * np.where(pidx < 32, -1.0, 1.0)[None]).T.astype(np.float32))
        am = [band_prev, band_next, allneg if s == 0 else band_prev, allneg if s == 3 else band_next]
        amask = np.ascontiguousarray(np.concatenate([np.tile(m_, (1, 4)) for m_ in am], axis=1))
        maps.append({
            "w_in_fm": w_in_fm, "w_in_tm": w_in_tm, "ab_w_out": f32(inp["ab_w_out"][0]), "cosT": cosT, "sinT": sinT,
            "amask": amask, "tri": tri, "iota": iota_t,
            "xT": xT, "xin": xin, "ctxT": ctxT, "cT": cT.reshape(128, 16), "w_mod": f32(inp["w_mod"]), "vec": vec, "rowb": rowb,
            "ident": ident, "bsrow": bsrow, "w_router": f32(inp["moe_w_router"]), "w_gate": f32(inp["moe_w_gate"]),
            "w_up": f32(inp["moe_w_up"]), "w_down": f32(inp["moe_w_down"]), "gm_w_in": f32(inp["gm_w_in"][0]),
            "gm_wsT": wsT, "gm_w_out": f32(inp["gm_w_out"][0]),
        })
    return maps


def gather_out(res):
    out = np.zeros((2, 8192, D), np.float32)
    for i in range(NCORES):
        b, s = i // 4, i % 4
        o = res.results[i]["outT"]
        out[b, s * T:(s + 1) * T] = o.transpose(2, 1, 0).reshape(T, D)
    return out


def run(inp, cfg, trace=False):
    nc = build(cfg)
    maps = prep_inputs(inp, cfg)
    maps = [{n: m[n] for n in build.last.in_names} for m in maps]
    res = run_bass_kernel_spmd(nc, maps, core_ids=list(range(NCORES)), trace=trace)
    return gather_out(res), res


def kernel(**inputs):
    cfg = {"phases": ["mod0", "hx0", "attn", "lstm", "mixout", "moe0", "gmlp", "moe1", "final"], "hide_mod1": True}
    out, _ = run(inputs, cfg)
    return out


def mix0_hx(k):
    nc, S, A = k.nc, k.S, k.A
    HX = k.HX = A.alloc("HX", [128, KC, T + 256], BF16)
    HC = k.HC = A.alloc("HC", [128, KC, 256], BF16)
    XB = [A.alloc("XB%d" % i, [128, KC, 512], F32) for i in range(2)]
    SQ = A.alloc("SQ", [128, KC, 512], F32)
    RS = A.alloc("RS", [128, 512], F32)
    TBK = [A.alloc("TBK%d" % i, [128, 512], F32) for i in range(2)]
    blocks = [(k.xT, 0, 512, HX, 0), (k.xT, 512, 512, HX, 0), (k.xT, 1024, 512, HX, 0), (k.xT, 1536, 512, HX, 0),
              (k.xT, 2048, 256, HX, 0), (k.ctxT, 0, 256, HC, 1)]
    for bi, (src, c0, n, dst, cc) in enumerate(blocks):
        xb = XB[bi % 2]
        xtag = "XB%d" % (bi % 2)
        S.dma("sp", lambda e, xb=xb, src=src, c0=c0, n=n: e.dma_start(out=xb[:, :, 0:n], in_=src[:, :, c0:c0 + n]), writes=[xtag])
        rstd_block(k, xb[:, :, 0:n], n, [xtag], SQ, "SQ", RS, "RS", 6)
        for kk in range(KC):
            tb = TBK[kk % 2]
            S.op("dve", lambda e, kk=kk, xb=xb, tb=tb, n=n, cc=cc: e.scalar_tensor_tensor(
                out=tb[:, 0:n], in0=xb[:, kk, 0:n], scalar=k.GS1[0][:, 2 * kk + cc:2 * kk + cc + 1], in1=RS[:, 0:n],
                op0=ALU.mult, op1=ALU.mult), reads=[xtag, "RS", "GS1_0"], writes=["TBK%d" % (kk % 2)])
            S.op("act", lambda e, kk=kk, tb=tb, n=n, cc=cc, dst=dst, c0=c0: e.activation(
                out=dst[:, kk, c0:c0 + n], in_=tb[:, 0:n], func=AF.Identity, bias=modcol(k, 0, 0, kk, cc), scale=1.0),
                reads=["TBK%d" % (kk % 2), "MODV0"], writes=["HX" if dst is HX else "HC"])
    A.release("XB0", "XB1", "SQ", "RS", "TBK0", "TBK1")


def phase_attn(k):
    nc, S, A = k.nc, k.S, k.A
    HX, HC = k.HX, k.HC
    NSL = 3
    MW = [A.alloc("MW%d" % i, [128, KC, 512], BF16) for i in range(NSL)]
    sc = [0]

    def slab(src, c0, n):
        i = sc[0] % NSL
        sc[0] += 1
        v = src.rearrange("(k p) n -> p k n", p=128)
        S.dma("pool", lambda e: e.dma_start(out=MW[i][:, :, 0:n], in_=v[:, :, c0:c0 + n]), writes=["MW%d" % i])
        return i

    COS = A.alloc("COS", [128, T + 256], F32)
    SIN = A.alloc("SIN", [128, T + 256], F32)
    S.dma("sp", lambda e: e.dma_start(out=COS[:], in_=k.cosT), writes=["COS"])
    S.dma("sp", lambda e: e.dma_start(out=SIN[:], in_=k.sinT), writes=["SIN"])
    AM = A.alloc("AM", [128, 4 * 512], BF16)
    S.dma("pool", lambda e: e.dma_start(out=AM[:], in_=k.amask), writes=["AM"])
    BQ = A.alloc("BQ", [128, 4, T], BF16)
    BK = A.alloc("BK", [128, T + 256], BF16)
    KCT = A.alloc("KCT", [128, 256], BF16)
    BV = A.alloc("BV", [128, 18 * 130], BF16)
    BVC = A.alloc("BVC", [128, 2 * 130], BF16)
    R1 = [A.alloc("R1_%d" % i, [128, 512], F32) for i in range(2)]
    R2 = [A.alloc("R2_%d" % i, [128, 512], F32) for i in range(2)]
    S.op("dve", lambda e: e.memset(BV[:], 1.0), writes=["BV"])
    S.op("dve", lambda e: e.memset(BVC[:], 1.0), writes=["BVC"])
    cnt = [0]

    def proj_fm(wi, col, rhs_fn, n, psi):
        ps = k.PS[psi]
        for kk in range(KC):
            S.op("pe", lambda e, kk=kk: e.matmul(ps[:, 0:n], lhsT=MW[wi][:, kk, col:col + 128], rhs=rhs_fn(kk),
                                                 start=(kk == 0), stop=(kk == KC - 1)),
                 reads=["MW%d" % wi, "HX", "HC"], writes=[("ps", psi)])
        return ps

    def rope(psq, pss, n, c0, out_ap, out_tag):
        c = cnt[0]
        cnt[0] += 1
        r1, r2 = R1[c % 2], R2[c % 2]
        S.op("dve", lambda e: e.tensor_tensor(out=r1[:, 0:n], in0=k.PS[psq][:, 0:n], in1=COS[:, c0:c0 + n], op=ALU.mult),
             reads=[("ps", psq), "COS"], writes=["R1_%d" % (c % 2)])
        S.op("dve", lambda e: e.tensor_tensor(out=r2[:, 0:n], in0=k.PS[pss][:, 0:n], in1=SIN[:, c0:c0 + n], op=ALU.mult),
             reads=[("ps", pss), "SIN"], writes=["R2_%d" % (c % 2)])
        S.op("dve", lambda e: e.tensor_tensor(out=out_ap, in0=r1[:, 0:n], in1=r2[:, 0:n], op=ALU.add),
             reads=["R1_%d" % (c % 2), "R2_%d" % (c % 2)], writes=[out_tag])

    wa = slab(k.w_in_fm, 1024, 512)
    wb = slab(k.w_in_fm, 1536, 512)
    for b in range(NBLK):
        for c in range(4):
            pq = 0 + (b * 4 + c) % 2
            ps_ = 2 + (b * 4 + c) % 2
            proj_fm(wa, c * 128, lambda kk, b=b: HX[:, kk, 128 + b * 512:128 + (b + 1) * 512], 512, pq)
            proj_fm(wb, c * 128, lambda kk, b=b: HX[:, kk, 128 + b * 512:128 + (b + 1) * 512], 512, ps_)
            rope(pq, ps_, 512, 128 + b * 512, BQ[:, c, b * 512:(b + 1) * 512], ("BQ", b))
    wc = slab(k.w_in_fm, 2048, 256)
    for bi, (c0, n) in enumerate([(0, 512), (512, 512), (1024, 512), (1536, 512), (2048, 256)]):
        pq, ps_ = 0 + bi % 2, 2 + bi % 2
        proj_fm(wc, 0, lambda kk, c0=c0, n=n: HX[:, kk, c0:c0 + n], n, pq)
        proj_fm(wc, 128, lambda kk, c0=c0, n=n: HX[:, kk, c0:c0 + n], n, ps_)
        rope(pq, ps_, n, c0, BK[:, c0:c0 + n], "BK")
    proj_fm(wc, 0, lambda kk: HC[:, kk, :], 256, 4)
    S.op("act", lambda e: e.activation(out=KCT[:], in_=k.PS[4][:, 0:256], func=AF.Copy), reads=[("ps", 4)], writes=["KCT"])
    wg = slab(k.w_in_tm, 1024, 144)
    for t in range(20):
        ps = k.PS[4 + t % 2]
        src = HX if t < 18 else HC
        tt = t if t < 18 else t - 18
        for kk in range(KC):
            S.op("pe", lambda e, kk=kk, ps=ps, src=src, tt=tt: e.matmul(
                ps[:, 0:128], lhsT=src[:, kk, tt * 128:(tt + 1) * 128], rhs=MW[wg][:, kk, 16:144], start=(kk == 0), stop=(kk == KC - 1)),
                reads=["MW%d" % wg, "HX", "HC"], writes=[("ps", 4 + t % 2)])
        dst = AP(BV, tt * 130, [[65, 2], [1, 64]]) if t < 18 else AP(BVC, tt * 130, [[65, 2], [1, 64]])
        S.op("act", lambda e, ps=ps, dst=dst: e.activation(out=dst, in_=AP(ps, 0, [[64, 2], [1, 64]], rowlen=512), func=AF.Copy),
             reads=[("ps", 4 + t % 2)], writes=["BV" if t < 18 else "BVC"])
    A.release("COS", "SIN", "R1_0", "R1_1", "R2_0", "R2_1", *["MW%d" % i for i in range(NSL)])
    BXT = k.BXT = A.alloc("BXT", [128, 4, T], BF16)
    PT = [A.alloc("PT%d" % i, [128, 512], BF16) for i in range(10)]
    BX = A.alloc("BX", [128, 512], F32)
    ESK = A.alloc("ESK", [128, 8], F32)
    DN = A.alloc("DN", [128, 16], F32)
    so = ROW_OFF["sink"]
    S.op("act", lambda e: e.activation(out=ESK[:], in_=k.ROWB[:, so:so + 8], func=AF.Exp), reads=["ROWB"], writes=["ESK"])
    pc = [0]
    for qb in range(NTILE):
        for g in range(2):
            po = k.PS[4 + g]
            tiles = []
            for j, kind in enumerate(("prev", "cen", "next")):
                col = (qb + j) * 128
                m = None
                if kind == "prev":
                    m = 2 if qb == 0 else 0
                if kind == "next":
                    m = 3 if qb == NTILE - 1 else 1
                tiles.append((BK[g * 64:(g + 1) * 64, col:col + 128], AP(BV, (qb + j) * 130 + g * 65, [[1, 65]]), m, "BK", "BV"))
            for j in range(2):
                tiles.append((KCT[g * 64:(g + 1) * 64, j * 128:(j + 1) * 128], AP(BVC, j * 130 + g * 65, [[1, 65]]), None, "KCT", "BVC"))
            pts = []
            for ti, (kT, vv, m, ktag, vtag) in enumerate(tiles):
                c = pc[0]
                pc[0] += 1
                psi = c % 3
                ps = k.PS[psi]
                rhs_q = BQ[g * 64:(g + 1) * 64, :, qb * 128:(qb + 1) * 128]
                S.op("pe", lambda e, ps=ps, kT=kT, rhs_q=rhs_q, m=m: e.matmul(ps[:], lhsT=kT, rhs=rhs_q, start=True, stop=(m is None)),
                     reads=[ktag, ("BQ", qb // 4)], writes=[("ps", psi)])
                if m is not None:
                    S.op("pe", lambda e, ps=ps, m=m: e.matmul(ps[:], lhsT=k.IDENT_B[:], rhs=AM[:, m * 512:(m + 1) * 512], start=False, stop=True),
                         reads=["AM", "IDENT_B"], writes=[("ps", psi)])
                pi = c % 10
                pt = PT[pi]
                pts.append((pt, pi, vv, vtag))
                S.op("act", lambda e, ps=ps, pt=pt: e.activation(out=pt[:], in_=ps[:], func=AF.Exp, scale=0.125),
                     reads=[("ps", psi)], writes=["PT%d" % pi])
            for r in range(4):
                for ti, (pt, pi, vv, vtag) in enumerate(pts):
                    S.op("pe", lambda e, pt=pt, r=r, vv=vv, ti=ti, po=po: e.matmul(
                        po[:, r * 65:(r + 1) * 65], lhsT=pt[:, r * 128:(r + 1) * 128], rhs=vv, start=(ti == 0), stop=(ti == 4)),
                        reads=["PT%d" % pi, vtag], writes=[("ps", 4 + g)])
            S.op("dve", lambda e, po=po, g=g: e.tensor_tensor(out=DN[:, g * 4:(g + 1) * 4], in0=AP(po, 64, [[65, 4]], rowlen=512),
                                                              in1=ESK[:, g * 4:(g + 1) * 4], op=ALU.add),
                 reads=[("ps", 4 + g), "ESK"], writes=[("DN", g)])
            S.op("dve", lambda e, g=g: e.reciprocal(out=DN[:, 8 + g * 4:8 + (g + 1) * 4], in_=DN[:, g * 4:(g + 1) * 4]),
                 reads=[("DN", g)], writes=[("DN", 2 + g)])
            S.op("dve", lambda e, po=po, g=g: e.tensor_tensor(
                out=AP(BX, g * 256, [[64, 4], [1, 64]]), in0=AP(po, 0, [[65, 4], [1, 64]], rowlen=512),
                in1=AP(DN, 8 + g * 4, [[1, 4], [0, 64]]), op=ALU.mult),
                reads=[("ps", 4 + g), ("DN", 2 + g)], writes=[("BX", g)])
        pst = k.PS[6 + qb % 2]
        for c in range(4):
            S.op("pe", lambda e, c=c, pst=pst: e.transpose(out=pst[:, c * 128:(c + 1) * 128], in_=BX[:, c * 128:(c + 1) * 128],
                                                           identity=k.IDENT_F[:]),
                 reads=[("BX", 0), ("BX", 1), "IDENT_F"], writes=[("ps", 6 + qb % 2)])
        S.op("act", lambda e, qb=qb, pst=pst: e.activation(
            out=BXT[:, :, qb * 128:(qb + 1) * 128], in_=AP(pst, 0, [[128, 4], [1, 128]], rowlen=512), func=AF.Copy),
            reads=[("ps", 6 + qb % 2)], writes=[("BXT", qb)])
    A.release("AM", "BQ", "BK", "KCT", "BV", "BVC", "BX", "ESK", "DN", *["PT%d" % i for i in range(10)])


LNS = float(np.log(128.0 ** -0.5))


def phase_lstm(k):
    nc, S, A = k.nc, k.S, k.A
    HX, HC = k.HX, k.HC
    rb = lambda name, n=1: k.ROWB[:, ROW_OFF[name]:ROW_OFF[name] + n]
    NSL = 3
    MW = [A.alloc("MW%d" % i, [128, KC, 512], BF16) for i in range(NSL)]
    sc = [0]

    def slab(src, c0, n):
        i = sc[0] % NSL
        sc[0] += 1
        v = src.rearrange("(k p) n -> p k n", p=128)
        S.dma("pool", lambda e: e.dma_start(out=MW[i][:, :, 0:n], in_=v[:, :, c0:c0 + n]), writes=["MW%d" % i])
        return i

    TRI = A.alloc("TRI", [128, 512], F32)
    S.dma("sp", lambda e: e.dma_start(out=TRI[:], in_=k.tri), writes=["TRI"])
    LNSC = A.alloc("LNSC", [128, 1], F32)
    S.op("dve", lambda e: e.memset(LNSC[:], LNS), writes=["LNSC"])
    DG = A.alloc("DG", [128, 40, 128], BF16)
    for i in range(40):
        S.op("dve", lambda e, i=i: e.tensor_scalar(out=DG[:, i, :], in0=k.IDENT_F[:], scalar1=vcol(k, "conv", i), scalar2=None, op0=ALU.mult),
             reads=["IDENT_F", "VEC"], writes=[("DG", i)])
    QT = A.alloc("QT", [128, 4, T], BF16)
    KT = A.alloc("KT", [128, 4, T], BF16)
    KTOK = A.alloc("KTOK", [128, 18, 512], BF16)
    VP = A.alloc("VP", [128, 18, 516], BF16)
    OT = A.alloc("OT", [128, 16, 512], BF16)
    G = A.alloc("G", [128, 18, 16], F32)
    PRE = [A.alloc("PRE%d" % i, [128, 2052], BF16) for i in range(2)]
    PREC = A.alloc("PREC", [128, 260], BF16)
    S.op("dve", lambda e: e.memset(VP[:], 1.0), writes=["VP"])
    pcn = [0]

    def nextps(n=4, base=0):
        c = pcn[0]
        pcn[0] += 1
        return base + c % n

    for qk in range(2):
        wi = slab(k.w_in_fm, qk * 512, 512)
        for h in range(4):
            ch = qk * 4 + h
            pre = PRE[(qk * 4 + h) % 2]
            ptag = "PRE%d" % ((qk * 4 + h) % 2)
            for gi, (c0, n) in enumerate([(126, 512), (638, 512), (1150, 512), (1662, 512), (2174, 4)]):
                psi = nextps()
                ps = k.PS[psi]
                for kk in range(KC):
                    S.op("pe", lambda e, kk=kk, ps=ps, wi=wi, h=h, c0=c0, n=n: e.matmul(
                        ps[:, 0:n], lhsT=MW[wi][:, kk, h * 128:(h + 1) * 128], rhs=HX[:, kk, c0:c0 + n], start=(kk == 0), stop=(kk == KC - 1)),
                        reads=["MW%d" % wi, "HX"], writes=[("ps", psi)])
                S.op("act", lambda e, ps=ps, pre=pre, c0=c0, n=n: e.activation(out=pre[:, c0 - 126:c0 - 126 + n], in_=ps[:, 0:n], func=AF.Copy),
                     reads=[("ps", psi)], writes=[(ptag, gi)])
            S.op("dve", lambda e, pre=pre: e.tensor_scalar(out=pre[:, 0:2], in0=pre[:, 0:2], scalar1=rb("vl"), scalar2=None, op0=ALU.mult),
                 reads=[(ptag, 0), "ROWB"], writes=[(ptag, 0)])
            S.op("dve", lambda e, pre=pre: e.tensor_scalar(out=pre[:, 2050:2052], in0=pre[:, 2050:2052], scalar1=rb("vr"), scalar2=None, op0=ALU.mult),
                 reads=[(ptag, 4), "ROWB"], writes=[(ptag, 4)])
            ptags = [(ptag, gi) for gi in range(5)]
            dst = QT if qk == 0 else KT
            dtag = "QT" if qk == 0 else "KT"
            for b in range(NBLK):
                psi = nextps()
                ps = k.PS[psi]
                for tap in range(5):
                    S.op("pe", lambda e, tap=tap, ps=ps, pre=pre, b=b, ch=ch: e.matmul(
                        ps[:], lhsT=DG[:, tap * 8 + ch, :], rhs=pre[:, b * 512 + tap:b * 512 + tap + 512], start=(tap == 0), stop=(tap == 4)),
                        reads=ptags + [("DG", tap * 8 + ch)], writes=[("ps", psi)])
                S.op("act", lambda e, ps=ps, dst=dst, h=h, b=b: e.activation(out=dst[:, h, b * 512:(b + 1) * 512], in_=ps[:], func=AF.Silu),
                     reads=[("ps", psi)], writes=[(dtag, h, b)])
            if qk == 1:
                for t in range(NTILE):
                    psi = nextps()
                    ps = k.PS[psi]
                    for tap in range(5):
                        S.op("pe", lambda e, tap=tap, ps=ps, pre=pre, t=t, ch=ch: e.matmul(
                            ps[:, 0:128], lhsT=pre[:, t * 128 + tap:t * 128 + tap + 128], rhs=DG[:, tap * 8 + ch, :], start=(tap == 0), stop=(tap == 4)),
                            reads=ptags + [("DG", tap * 8 + ch)], writes=[("ps", psi)])
                    S.op("act", lambda e, ps=ps, h=h, t=t: e.activation(out=KTOK[:, t, h * 128:(h + 1) * 128], in_=ps[:, 0:128], func=AF.Silu),
                         reads=[("ps", psi)], writes=[("KTOK", t)])
                S.op("dve", lambda e: e.memset(PREC[:], 0.0), writes=["PREC"])
                psi = nextps()
                ps = k.PS[psi]
                for kk in range(KC):
                    S.op("pe", lambda e, kk=kk, ps=ps, wi=wi, h=h: e.matmul(
                        ps[:, 0:256], lhsT=MW[wi][:, kk, h * 128:(h + 1) * 128], rhs=HC[:, kk, :], start=(kk == 0), stop=(kk == KC - 1)),
                        reads=["MW%d" % wi, "HC"], writes=[("ps", psi)])
                S.op("act", lambda e, ps=ps: e.activation(out=PREC[:, 2:258], in_=ps[:, 0:256], func=AF.Copy), reads=[("ps", psi)], writes=["PREC"])
                for t in range(2):
                    psi = nextps()
                    ps = k.PS[psi]
                    for tap in range(5):
                        S.op("pe", lambda e, tap=tap, ps=ps, t=t, ch=ch: e.matmul(
                            ps[:, 0:128], lhsT=PREC[:, t * 128 + tap:t * 128 + tap + 128], rhs=DG[:, tap * 8 + ch, :], start=(tap == 0), stop=(tap == 4)),
                            reads=["PREC", ("DG", tap * 8 + ch)], writes=[("ps", psi)])
                    S.op("act", lambda e, ps=ps, h=h, t=t: e.activation(out=KTOK[:, 16 + t, h * 128:(h + 1) * 128], in_=ps[:, 0:128], func=AF.Silu),
                         reads=[("ps", psi)], writes=[("KTOK", 16 + t)])
    wv = slab(k.w_in_tm, 0, 512)
    wo = slab(k.w_in_tm, 512, 512)
    wg = slab(k.w_in_tm, 1024, 144)
    for t in range(18):
        src = HX if t < 16 else HC
        c0 = 128 + t * 128 if t < 16 else (t - 16) * 128
        stag = "HX" if t < 16 else "HC"
        for which, wi_, n in (("v", wv, 512), ("o", wo, 512), ("g", wg, 16)):
            if which == "o" and t >= 16:
                continue
            psi = nextps()
            ps = k.PS[psi]
            for kk in range(KC):
                S.op("pe", lambda e, kk=kk, ps=ps, src=src, c0=c0, wi_=wi_, n=n: e.matmul(
                    ps[:, 0:n], lhsT=src[:, kk, c0:c0 + 128], rhs=MW[wi_][:, kk, 0:n], start=(kk == 0), stop=(kk == KC - 1)),
                    reads=["MW%d" % wi_, stag], writes=[("ps", psi)])
            if which == "v":
                S.op("act", lambda e, ps=ps, t=t: e.activation(out=AP(VP, t * 516, [[129, 4], [1, 128]]),
                                                               in_=AP(ps, 0, [[128, 4], [1, 128]], rowlen=512), func=AF.Copy),
                     reads=[("ps", psi)], writes=[("VP", t)])
            elif which == "o":
                S.op("act", lambda e, ps=ps, t=t: e.activation(out=OT[:, t, :], in_=ps[:], func=AF.Copy), reads=[("ps", psi)], writes=[("OT", t)])
            else:
                S.op("dve", lambda e, ps=ps, t=t: e.tensor_tensor(out=G[:, t, :], in0=ps[:, 0:16], in1=rb("gate_b", 16), op=ALU.add),
                     reads=[("ps", psi), "ROWB"], writes=["G"])
    A.release("HX", "HC", "PRE0", "PRE1", "PREC", "DG", *["MW%d" % i for i in range(NSL)])
    ZS = A.alloc("ZS", [128, 16, 8, 129], BF16)
    def galloc(n):
        return A.alloc(n, [128, 18, 8], F32)
    LI, NLF, NB, NG, ARG, BIASD, EB, WCOL, EG = [galloc(n) for n in ("LI", "NLF", "NB", "NG", "ARG", "BIASD", "EB", "WCOL", "EG")]
    gv = lambda off: AP(G, off, [[16, 18], [8, 2], [1, 4]])
    v4 = lambda tns: AP(tns, 0, [[8, 18], [4, 2], [1, 4]])
    S.op("dve", lambda e: e.tensor_copy(out=v4(LI), in_=gv(0)), reads=["G"], writes=["LI"])
    S.op("act", lambda e: e.activation(out=v4(NLF), in_=gv(4), func=AF.Exp, scale=-1.0), reads=["G"], writes=["NLF"])
    S.op("act", lambda e: e.activation(out=NLF[:], in_=NLF[:], func=AF.Ln, bias=k.ONES_F[:, 0:1], scale=1.0), reads=["NLF", "ONES_F"], writes=["NLF"])
    psn = k.PS[0]
    for t in range(18):
        for d in range(2):
            S.op("pe", lambda e, t=t, d=d: e.matmul(psn[:, t * 8 + d * 4:t * 8 + d * 4 + 4], lhsT=TRI[:, d * 128:(d + 1) * 128],
                                                    rhs=NLF[:, t, d * 4:(d + 1) * 4], start=True, stop=True),
                 reads=["TRI", "NLF"], writes=[("ps", 0)])
    S.op("dve", lambda e: e.tensor_copy(out=AP(NB, 0, [[1, 144]]), in_=psn[:, 0:144]), reads=[("ps", 0)], writes=["NB"])
    psg = k.PS[1]
    for t in range(18):
        S.op("pe", lambda e, t=t: e.matmul(psg[:, t * 8:(t + 1) * 8], lhsT=k.ONES_F[:], rhs=NLF[:, t, :], start=True, stop=True),
             reads=["ONES_F", "NLF"], writes=[("ps", 1)])
    S.op("dve", lambda e: e.tensor_copy(out=AP(NG, 0, [[1, 144]]), in_=psg[:, 0:144]), reads=[("ps", 1)], writes=["NG"])
    S.op("dve", lambda e: e.tensor_tensor(out=ARG[:], in0=NB[:], in1=LI[:], op=ALU.add), reads=["NB", "LI"], writes=["ARG"])
    S.op("dve", lambda e: e.tensor_scalar(out=BIASD[:], in0=ARG[:], scalar1=LNS, scalar2=None, op0=ALU.add), reads=["ARG"], writes=["BIASD"])
    S.op("act", lambda e: e.activation(out=EB[:], in_=NB[:], func=AF.Exp, scale=-1.0, bias=LNSC[:, 0:1]), reads=["NB", "LNSC"], writes=["EB"])
    S.op("dve", lambda e: e.tensor_tensor(out=WCOL[:], in0=ARG[:], in1=NG[:], op=ALU.subtract), reads=["ARG", "NG"], writes=["WCOL"])
    S.op("act", lambda e: e.activation(out=WCOL[:], in_=WCOL[:], func=AF.Exp), reads=["WCOL"], writes=["WCOL"])
    S.op("act", lambda e: e.activation(out=EG[:], in_=NG[:], func=AF.Exp, scale=-1.0), reads=["NG"], writes=["EG"])
    NGC = A.alloc("NGC", [128, 16, 8], F32)
    EGC = A.alloc("EGC", [128, 16, 8], F32)
    SUMM = A.alloc("SUMM", [128, 8, 130], F32)
    S.op("dve", lambda e: e.memset(NGC[:], 0.0), writes=["NGC"])
    for c in range(1, 16):
        S.op("dve", lambda e, c=c: e.tensor_tensor(out=NGC[:, c, 0:4], in0=NGC[:, c - 1, 0:4], in1=NG[:, c - 1, 0:4], op=ALU.add),
             reads=["NGC", "NG"], writes=["NGC"])
    for c in range(14, -1, -1):
        S.op("dve", lambda e, c=c: e.tensor_tensor(out=NGC[:, c, 4:8], in0=NGC[:, c + 1, 4:8], in1=NG[:, c + 1, 4:8], op=ALU.add),
             reads=["NGC", "NG"], writes=["NGC"])
    S.op("act", lambda e: e.activation(out=EGC[:], in_=NGC[:], func=AF.Exp, scale=-1.0), reads=["NGC"], writes=["EGC"])
    S.op("dve", lambda e: e.tensor_tensor(out=AP(SUMM, 129, [[130, 4]]), in0=NGC[:, 15, 0:4], in1=NG[:, 15, 0:4], op=ALU.add),
         reads=["NGC", "NG"], writes=[("SUMM", "g")])
    S.op("dve", lambda e: e.tensor_tensor(out=AP(SUMM, 4 * 130 + 129, [[130, 4]]), in0=NGC[:, 0, 4:8], in1=NG[:, 0, 4:8], op=ALU.add),
         reads=["NGC", "NG"], writes=[("SUMM", "g")])
    Z = A.alloc("Z", [128, 8, 129], F32)
    CCTX = A.alloc("CCTX", [128, 8, 129], F32)
    VW = [A.alloc("VW%d" % i, [128, 516], BF16) for i in range(2)]
    vc = [0]

    def chain_step(t, d, zbuf, ztag, save_c):
        i = vc[0] % 2
        vc[0] += 1
        vw = VW[i]
        S.op("dve", lambda e: e.tensor_tensor(out=AP(vw, 0, [[129, 4], [1, 129]]), in0=AP(VP, t * 516, [[129, 4], [1, 129]]),
                                              in1=AP(WCOL, t * 8 + d * 4, [[1, 4], [0, 129]]), op=ALU.mult),
             reads=[("VP", t), "WCOL"], writes=["VW%d" % i])
        for hp in range(2):
            psi = 2 + 2 * i + hp
            ps = k.PS[psi]
            for hh in range(2):
                h = hp * 2 + hh
                S.op("pe", lambda e, ps=ps, h=h, hh=hh: e.matmul(ps[:, hh * 129:(hh + 1) * 129], lhsT=KTOK[:, t, h * 128:(h + 1) * 128],
                                                                 rhs=vw[:, h * 129:(h + 1) * 129], start=True, stop=True),
                     reads=[("KTOK", t), "VW%d" % i], writes=[("ps", psi)])
        for h in range(4):
            hd = d * 4 + h
            ps = k.PS[2 + 2 * i + h // 2]
            if save_c is not None:
                S.op("act", lambda e, hd=hd: e.activation(out=ZS[:, save_c, hd, :], in_=zbuf[:, hd, :], func=AF.Copy),
                     reads=[(ztag, hd)], writes=[("ZS", save_c, hd)])
            S.op("dve", lambda e, hd=hd, ps=ps, h=h: e.scalar_tensor_tensor(
                out=zbuf[:, hd, :], in0=zbuf[:, hd, :], scalar=EG[:, t, hd:hd + 1], in1=ps[:, (h % 2) * 129:(h % 2 + 1) * 129],
                op0=ALU.mult, op1=ALU.add), reads=[(ztag, hd), "EG", ("ps", 2 + 2 * i + h // 2)], writes=[(ztag, hd)])

    S.op("dve", lambda e: e.memset(CCTX[:], 0.0), writes=[("CCTX", hd) for hd in range(8)])
    S.op("dve", lambda e: e.memset(Z[:], 0.0), writes=[("Z", hd) for hd in range(8)])
    for t in (16, 17):
        chain_step(t, 0, CCTX, "CCTX", None)
    for t in (17, 16):
        chain_step(t, 1, CCTX, "CCTX", None)
    for c in range(16):
        chain_step(c, 0, Z, "Z", c)
    for c in range(15, -1, -1):
        chain_step(c, 1, Z, "Z", c)
    S.op("dve", lambda e: e.tensor_copy(out=SUMM[:, :, 0:129], in_=Z[:]), reads=[("Z", hd) for hd in range(8)], writes=[("SUMM", "z")])
    stags = [("SUMM", "z"), ("SUMM", "g")]
    S.dma("pool", lambda e: e.dma_start(out=k.ag2_src.ap(), in_=AP(SUMM, 0, [[1, 1040]])), reads=stags, writes=["ag2_src"])
    S.op("pool", lambda e: e.collective_compute("AllGather", ALU.bypass, replica_groups=[list(range(NCORES))],
                                                ins=[k.ag2_src.ap().opt()], outs=[k.ag2_dst.ap().opt()]),
         reads=["ag2_src"], writes=["ag2_dst"])
    A.release("KTOK", "VW0", "VW1", "Z")
    GATH = [A.alloc("GATH%d" % r, [128, 1040], F32) for r in range(NCORES)]
    EGR = A.alloc("EGR", [128, 64], F32)
    for r in range(NCORES):
        S.dma("sp", lambda e, r=r: e.dma_start(out=GATH[r][:], in_=k.ag2_dst.ap()[r * 128:(r + 1) * 128, :]), reads=["ag2_dst"], writes=["GATH%d" % r])
        S.op("act", lambda e, r=r: e.activation(out=EGR[:, r * 8:(r + 1) * 8], in_=AP(GATH[r], 129, [[130, 8]]), func=AF.Exp, scale=-1.0),
             reads=["GATH%d" % r], writes=[("EGR", r)])
    CST = CCTX
    TM1 = A.alloc("TM1", [128, 129], F32)
    for d in range(2):
        order = range(8) if d == 0 else range(7, -1, -1)
        mname = "mf" if d == 0 else "mb"
        for r in order:
            for h in range(4):
                hd = d * 4 + h
                S.op("dve", lambda e, r=r, hd=hd: e.scalar_tensor_tensor(
                    out=TM1[:], in0=CST[:, hd, :], scalar=EGR[:, r * 8 + hd:r * 8 + hd + 1], in1=GATH[r][:, hd * 130:hd * 130 + 129],
                    op0=ALU.mult, op1=ALU.add), reads=[("CCTX", hd), ("EGR", r), "GATH%d" % r], writes=["TM1"])
                S.op("dve", lambda e, hd=hd: e.tensor_tensor(out=TM1[:], in0=TM1[:], in1=CST[:, hd, :], op=ALU.subtract),
                     reads=["TM1", ("CCTX", hd)], writes=["TM1"])
                mo = ROW_OFF[mname] + r
                S.op("dve", lambda e, hd=hd, mo=mo: e.scalar_tensor_tensor(
                    out=CST[:, hd, :], in0=TM1[:], scalar=k.ROWB[:, mo:mo + 1], in1=CST[:, hd, :], op0=ALU.mult, op1=ALU.add),
                    reads=["TM1", ("CCTX", hd), "ROWB"], writes=[("CCTX", hd)])
    A.release("EGR", "TM1", *["GATH%d" % r for r in range(NCORES)])
    AXT = k.AXT = A.alloc("AXT", [128, 4, T], BF16)
    DT = [A.alloc("DT%d" % i, [128, 128], F32) for i in range(2)]
    STb = [A.alloc("STb%d" % i, [128, 128], BF16) for i in range(2)]
    CK = [A.alloc("CK%d" % i, [128, 129], BF16) for i in range(2)]
    TI = [A.alloc("TI%d" % i, [128, 129], F32) for i in range(2)]
    TT = [A.alloc("TT%d" % i, [128, 129], F32) for i in range(2)]
    DEN = A.alloc("DEN", [128, 16], F32)
    HS = [A.alloc("HS%d" % i, [128, 512], F32) for i in range(2)]
    SS = A.alloc("SS", [128, 16], F32)
    OG = A.alloc("OG", [128, 512], F32)
    JK = A.alloc("JK", [128, 128], F32)
    ho = ROW_OFF["head_g"]
    n2 = [0]
    for c in range(16):
        hs = HS[c % 2]
        hst = "HS%d" % (c % 2)
        for h in range(4):
            pss_i = (c * 4 + h) % 2
            pss = k.PS[pss_i]
            S.op("pe", lambda e, pss=pss, h=h, c=c: e.matmul(pss[:, 0:128], lhsT=KT[:, h, c * 128:(c + 1) * 128], rhs=QT[:, h, c * 128:(c + 1) * 128],
                                                             start=True, stop=True),
                 reads=[("KT", h, c // 4), ("QT", h, c // 4)], writes=[("ps", pss_i)])
            for d in range(2):
                hd = d * 4 + h
                i = n2[0] % 2
                n2[0] += 1
                psb_i = 2 + i
                psb = k.PS[psb_i]
                S.op("pe", lambda e, psb=psb, c=c, hd=hd, d=d: e.matmul(psb[:, 0:128], lhsT=AP(NLF, c * 8 + hd, [[0, 128]]),
                                                                        rhs=TRI[:, d * 128:(d + 1) * 128], start=True, stop=False),
                     reads=["NLF", "TRI"], writes=[("ps", psb_i)])
                S.op("pe", lambda e, psb=psb, d=d: e.matmul(psb[:, 0:128], lhsT=k.IDENT_F[:], rhs=TRI[:, 256 + d * 128:256 + (d + 1) * 128],
                                                            start=False, stop=True),
                     reads=["IDENT_F", "TRI"], writes=[("ps", psb_i)])
                dt, stb, ck, ti, tt = DT[i], STb[i], CK[i], TI[i], TT[i]
                S.op("act", lambda e, dt=dt, psb=psb, c=c, hd=hd: e.activation(out=dt[:], in_=psb[:, 0:128], func=AF.Exp, scale=-1.0,
                                                                               bias=BIASD[:, c, hd:hd + 1]),
                     reads=[("ps", psb_i), "BIASD"], writes=["DT%d" % i])
                S.op("dve", lambda e, stb=stb, pss=pss, dt=dt: e.tensor_tensor(out=stb[:], in0=pss[:, 0:128], in1=dt[:], op=ALU.mult),
                     reads=[("ps", pss_i), "DT%d" % i], writes=["STb%d" % i])
                S.op("dve", lambda e, ck=ck, c=c, hd=hd: e.scalar_tensor_tensor(
                    out=ck[:], in0=CST[:, hd, :], scalar=EGC[:, c, hd:hd + 1], in1=ZS[:, c, hd, :], op0=ALU.mult, op1=ALU.add),
                    reads=[("CCTX", hd), "EGC", ("ZS", c, hd)], writes=["CK%d" % i])
                pso_i = 4 + i
                pso = k.PS[pso_i]
                S.op("pe", lambda e, pso=pso, stb=stb, c=c, h=h: e.matmul(pso[:, 0:129], lhsT=stb[:], rhs=VP[:, c, h * 129:(h + 1) * 129],
                                                                          start=True, stop=True),
                     reads=["STb%d" % i, ("VP", c)], writes=[("ps", pso_i)])
                S.op("pe", lambda e, pso=pso, ck=ck, c=c, h=h: e.matmul(pso[:, 256:385], lhsT=QT[:, h, c * 128:(c + 1) * 128], rhs=ck[:],
                                                                        start=True, stop=True),
                     reads=["CK%d" % i, ("QT", h, c // 4)], writes=[("ps", pso_i)])
                S.op("act", lambda e, ti=ti, pso=pso, c=c, hd=hd: e.activation(out=ti[:], in_=pso[:, 256:385], func=AF.Copy, scale=EB[:, c, hd:hd + 1]),
                     reads=[("ps", pso_i), "EB"], writes=["TI%d" % i])
                S.op("dve", lambda e, tt=tt, pso=pso, ti=ti: e.tensor_tensor(out=tt[:], in0=pso[:, 0:129], in1=ti[:], op=ALU.add),
                     reads=[("ps", pso_i), "TI%d" % i], writes=["TT%d" % i])
                S.op("act", lambda e, tt=tt, i=i: e.activation(out=DEN[:, 4 + i:5 + i], in_=tt[:, 128:129], func=AF.Abs),
                     reads=["TT%d" % i], writes=[("DEN", 4 + i)])
                S.op("dve", lambda e, i=i: e.tensor_scalar(out=DEN[:, i:i + 1], in0=DEN[:, 4 + i:5 + i], scalar1=1.0, scalar2=None, op0=ALU.max),
                     reads=[("DEN", 4 + i)], writes=[("DEN", i)])
                S.op("dve", lambda e, i=i: e.reciprocal(out=DEN[:, 2 + i:3 + i], in_=DEN[:, i:i + 1]), reads=[("DEN", i)], writes=[("DEN", 2 + i)])
                if d == 0:
                    S.op("dve", lambda e, hs=hs, tt=tt, h=h, i=i: e.tensor_scalar(out=hs[:, h * 128:(h + 1) * 128], in0=tt[:, 0:128],
                                                                                 scalar1=DEN[:, 2 + i:3 + i], scalar2=None, op0=ALU.mult),
                         reads=["TT%d" % i, ("DEN", 2 + i)], writes=[(hst, h)])
                else:
                    S.op("dve", lambda e, hs=hs, tt=tt, h=h, i=i: e.scalar_tensor_tensor(
                        out=hs[:, h * 128:(h + 1) * 128], in0=tt[:, 0:128], scalar=DEN[:, 2 + i:3 + i], in1=hs[:, h * 128:(h + 1) * 128],
                        op0=ALU.mult, op1=ALU.add), reads=["TT%d" % i, ("DEN", 2 + i), (hst, h)], writes=[(hst, h)])
            S.op("act", lambda e, hs=hs, h=h: e.activation(out=JK[:], in_=hs[:, h * 128:(h + 1) * 128], func=AF.Square, accum_out=SS[:, h:h + 1]),
                 reads=[(hst, h)], writes=["JK", ("SS", h)])
        sst = [("SS", h) for h in range(4)]
        S.op("act", lambda e: e.activation(out=SS[:, 4:8], in_=SS[:, 0:4], func=AF.Ln, bias=k.EPSC[:, 0:1], scale=1.0 / 128), reads=sst + ["EPSC"],
             writes=[("SS", 4)])
        S.op("act", lambda e: e.activation(out=SS[:, 8:12], in_=SS[:, 4:8], func=AF.Exp, scale=-0.5), reads=[("SS", 4)], writes=[("SS", 8)])
        S.op("act", lambda e, c=c: e.activation(out=OG[:], in_=OT[:, c, :], func=AF.Sigmoid), reads=[("OT", c)], writes=["OG"])
        hst4 = [(hst, h) for h in range(4)]
        S.op("dve", lambda e, hs=hs: e.tensor_tensor(out=AP(hs, 0, [[128, 4], [1, 128]]), in0=AP(hs, 0, [[128, 4], [1, 128]]),
                                                     in1=AP(SS, 8, [[1, 4], [0, 128]]), op=ALU.mult), reads=hst4 + [("SS", 8)], writes=hst4)
        S.op("dve", lambda e, hs=hs: e.tensor_tensor(out=hs[:], in0=hs[:], in1=k.ROWB[:, ho:ho + 512], op=ALU.mult), reads=hst4 + ["ROWB"], writes=hst4)
        S.op("dve", lambda e, hs=hs: e.tensor_tensor(out=hs[:], in0=hs[:], in1=OG[:], op=ALU.mult), reads=hst4 + ["OG"], writes=hst4)
        pst_i = 6 + c % 2
        pst = k.PS[pst_i]
        for h in range(4):
            S.op("pe", lambda e, h=h, pst=pst, hs=hs: e.transpose(out=pst[:, h * 128:(h + 1) * 128], in_=hs[:, h * 128:(h + 1) * 128], identity=k.IDENT_F[:]),
                 reads=hst4 + ["IDENT_F"], writes=[("ps", pst_i)])
        S.op("act", lambda e, c=c, pst=pst: e.activation(out=AXT[:, :, c * 128:(c + 1) * 128], in_=AP(pst, 0, [[128, 4], [1, 128]], rowlen=512), func=AF.Copy),
             reads=[("ps", pst_i)], writes=[("AXT", c)])
    A.release("TRI", "LNSC", "QT", "KT", "VP", "OT", "G", "LI", "NLF", "NB", "NG", "ARG", "BIASD", "EB", "WCOL", "EG", "NGC", "EGC", "SUMM",
              "ZS", "CCTX", "DT0", "DT1", "STb0", "STb1", "CK0", "CK1", "TI0", "TI1", "TT0", "TT1", "DEN", "HS0", "HS1", "SS", "OG", "JK")


def phase_mixout(k):
    nc, S, A = k.nc, k.S, k.A
    X = k.X = A.alloc("X", [128, KC, T], F32)
    for b in range(NBLK):
        S.dma("sp", lambda e, b=b: e.dma_start(out=X[:, :, b * 512:(b + 1) * 512], in_=k.xT[:, :, 128 + b * 512:128 + (b + 1) * 512]),
              writes=xt(b))
    WO = [A.alloc("WO%d" % i, [128, KC, 512], BF16) for i in range(2)]
    src = k.ab_w_out.rearrange("(k p) n -> p k n", p=128)
    for hh in range(2):
        S.dma("pool", lambda e, hh=hh: e.dma_start(out=WO[hh][:], in_=src[:, :, hh * 512:(hh + 1) * 512]), writes=["WO%d" % hh])
    cnt = 0
    for d in range(KC):
        wo = WO[d // 4]
        for b in range(NBLK):
            sl = slice(b * 512, (b + 1) * 512)
            psi = cnt % 4
            cnt += 1
            ps = k.PS[psi]
            for kk in range(KC):
                srcb = k.AXT if kk < 4 else k.BXT
                stg = "AXT" if kk < 4 else "BXT"
                S.op("pe", lambda e, kk=kk, ps=ps, wo=wo, d=d, srcb=srcb, sl=sl: e.matmul(
                    ps[:], lhsT=wo[:, kk, (d % 4) * 128:(d % 4 + 1) * 128], rhs=srcb[:, kk % 4, sl], start=(kk == 0), stop=(kk == KC - 1)),
                    reads=["WO%d" % (d // 4)] + [(stg, q) for q in range(b * 4, b * 4 + 4)], writes=[("ps", psi)])
            S.op("dve", lambda e, ps=ps, d=d, sl=sl: e.scalar_tensor_tensor(
                out=X[:, d, sl], in0=ps[:], scalar=modcol(k, 0, 2, d), in1=X[:, d, sl], op0=ALU.mult, op1=ALU.add),
                reads=[("ps", psi), ("X", d, b), "MODV0"], writes=[("X", d, b)])
    A.release("WO0", "WO1", "AXT", "BXT")


def moe_sparse_experts(k, l, H2T, AFF, LO, CW):
    nc, S, A = k.nc, k.S, k.A
    X = k.X
    M01 = A.alloc("M01", [128, 256], F32)
    S.op("dve", lambda e: e.tensor_tensor(out=AP(M01, 0, [[16, 16], [1, 16]]), in0=AP(AFF, 0, [[16, 16], [1, 16]]),
                                          in1=AP(LO, 0, [[0, 16], [1, 16]]), op=ALU.is_gt), reads=["AFF", "LO"], writes=["M01"])
    IOTA = A.alloc("IOTA", [128, 129], F32)
    S.dma("sp", lambda e: e.dma_start(out=IOTA[:], in_=k.iota), writes=["IOTA"])
    STRI = A.alloc("STRI", [128, 128], F32)
    S.dma("sp", lambda e: e.dma_start(out=STRI[:], in_=k.tri[:, 0:128]), writes=["STRI"])
    S.op("dve", lambda e: e.tensor_tensor(out=STRI[:], in0=STRI[:], in1=k.IDENT_F[:], op=ALU.subtract), reads=["STRI", "IDENT_F"], writes=["STRI"])
    CNT = A.alloc("CNT", [128, 256], F32)
    OFF = A.alloc("OFF", [128, 256], F32)
    POS = A.alloc("POS", [128, 256], F32)
    S.op("pe", lambda e: e.matmul(k.PS[7][:, 0:256], lhsT=STRI[:], rhs=M01[:], start=True, stop=True), reads=["STRI", "M01"], writes=[("ps", 7)])
    S.op("pe", lambda e: e.matmul(k.PS[6][:, 0:256], lhsT=k.ONES_F[:], rhs=M01[:], start=True, stop=True), reads=["ONES_F", "M01"], writes=[("ps", 6)])
    S.op("dve", lambda e: e.tensor_copy(out=CNT[:], in_=k.PS[6][:, 0:256]), reads=[("ps", 6)], writes=["CNT"])
    S.op("dve", lambda e: e.memset(OFF[:], 0.0), writes=["OFF"])
    gview = lambda tns, tt: AP(tns, tt * 16, [[64, 4], [1, 16]])
    S.op("dve", lambda e: e.tensor_copy(out=gview(OFF, 1), in_=gview(CNT, 0)), reads=["CNT", "OFF"], writes=["OFF"])
    for tt in (2, 3):
        S.op("dve", lambda e, tt=tt: e.tensor_tensor(out=gview(OFF, tt), in0=gview(OFF, tt - 1), in1=gview(CNT, tt - 1), op=ALU.add),
             reads=["CNT", "OFF"], writes=["OFF"])
    S.op("dve", lambda e: e.tensor_tensor(out=POS[:], in0=k.PS[7][:, 0:256], in1=OFF[:], op=ALU.add), reads=[("ps", 7), "OFF"], writes=["POS"])
    A.release("CNT", "OFF", "STRI")
    POSB = A.alloc("POSB", [128, 256], BF16)
    CWBF = A.alloc("CWBF", [128, 256], BF16)
    S.op("dve", lambda e: e.tensor_copy(out=POSB[:], in_=POS[:]), reads=["POS"], writes=["POSB"])
    S.op("dve", lambda e: e.tensor_copy(out=CWBF[:], in_=CW[:]), reads=["CW"], writes=["CWBF"])
    NSLOT = 6
    WS = [A.alloc("WS%d" % i, [128, KC, 512], BF16) for i in range(NSLOT)]
    PSEL = A.alloc("PSEL", [128, NTILE, 128], BF16)
    HG = A.alloc("HG", [128, KC, 512], BF16)
    HID = A.alloc("HID", [128, KC, 512], BF16)
    YS = A.alloc("YS", [128, 4, D], BF16)
    PT = [A.alloc("PTS%d" % i, [128, 512], BF16) for i in range(2)]
    CWB = [A.alloc("CWB%d" % i, [128, 512], BF16) for i in range(2)]
    SG = [A.alloc("SG%d" % i, [128, 512], BF16) for i in range(2)]
    slot_ctr = [0]

    def load_slab(w_ap, e_idx, h):
        i = slot_ctr[0] % NSLOT
        slot_ctr[0] += 1
        src = w_ap[l, e_idx].rearrange("(k p) n -> p k n", p=128)
        S.dma("pool", lambda e: e.dma_start(out=WS[i][:], in_=src[:, :, h * 512:(h + 1) * 512]), writes=["WS%d" % i])
        return i

    ids = {}
    for nm, w_ap, h in (("g0", k.w_gate, 0), ("u0", k.w_up, 0), ("g1", k.w_gate, 1), ("u1", k.w_up, 1), ("d0", k.w_down, 0), ("d1", k.w_down, 1)):
        ids[nm] = load_slab(w_ap, 0, h)
    cnt = [0]
    for ex in range(NEXP):
        cur = ids
        for t in range(NTILE):
            col = t * 16 + ex
            S.op("dve", lambda e, t=t, col=col: e.tensor_scalar(out=PSEL[:, t, :], in0=IOTA[:, 0:128], scalar1=POS[:, col:col + 1],
                                                                scalar2=M01[:, col:col + 1], op0=ALU.is_equal, op1=ALU.mult),
                 reads=["IOTA", "POS", "M01"], writes=[("PSEL", t)])
        for fc in range(KC):
            c = cnt[0]
            cnt[0] += 1
            psi = c % 2
            ps = k.PS[psi]
            for g in range(4):
                for tt in range(4):
                    t = g * 4 + tt
                    S.op("pe", lambda e, ps=ps, g=g, tt=tt, t=t, fc=fc: e.matmul(
                        ps[:, g * 128:(g + 1) * 128], lhsT=H2T[:, t, fc * 128:(fc + 1) * 128], rhs=PSEL[:, t, :], start=(tt == 0), stop=(tt == 3)),
                        reads=[("H2", t), ("PSEL", t)], writes=[("ps", psi)])
            if fc % 2 == 0:
                S.op("act", lambda e, ps=ps, fc=fc: e.activation(out=HG[:, fc, :], in_=ps[:], func=AF.Copy), reads=[("ps", psi)], writes=[("HG", fc)])
            else:
                S.op("dve", lambda e, ps=ps, fc=fc: e.tensor_copy(out=HG[:, fc, :], in_=ps[:]), reads=[("ps", psi)], writes=[("HG", fc)])
        for fo in range(KC):
            h, fi = fo // 4, fo % 4
            gs, us = cur["g%d" % h], cur["u%d" % h]
            c = cnt[0]
            cnt[0] += 1
            pgi, pui = 2 + 2 * (c % 2), 3 + 2 * (c % 2)
            pg, pu = k.PS[pgi], k.PS[pui]
            for kk in range(KC):
                S.op("pe", lambda e, kk=kk, pg=pg, gs=gs, fi=fi: e.matmul(pg[:], lhsT=WS[gs][:, kk, fi * 128:(fi + 1) * 128], rhs=HG[:, kk, :],
                                                                          start=(kk == 0), stop=(kk == KC - 1)),
                     reads=["WS%d" % gs, ("HG", kk)], writes=[("ps", pgi)])
            for kk in range(KC):
                S.op("pe", lambda e, kk=kk, pu=pu, us=us, fi=fi: e.matmul(pu[:], lhsT=WS[us][:, kk, fi * 128:(fi + 1) * 128], rhs=HG[:, kk, :],
                                                                          start=(kk == 0), stop=(kk == KC - 1)),
                     reads=["WS%d" % us, ("HG", kk)], writes=[("ps", pui)])
            sg = SG[c % 2]
            S.op("act", lambda e, sg=sg, pg=pg: e.activation(out=sg[:], in_=pg[:], func=AF.Silu), reads=[("ps", pgi)], writes=["SG%d" % (c % 2)])
            S.op("dve", lambda e, sg=sg, pu=pu, fo=fo: e.tensor_tensor(out=HID[:, fo, :], in0=sg[:], in1=pu[:], op=ALU.mult),
                 reads=["SG%d" % (c % 2), ("ps", pui)], writes=[("HID", fo)])
        if ex + 1 < NEXP:
            nxt = {}
            for nm, w_ap, h in (("g0", k.w_gate, 0), ("u0", k.w_up, 0), ("g1", k.w_gate, 1), ("u1", k.w_up, 1)):
                nxt[nm] = load_slab(w_ap, ex + 1, h)
        for g in range(4):
            for dh in range(2):
                ds = cur["d%d" % dh]
                c = cnt[0]
                cnt[0] += 1
                psi = c % 2
                ps = k.PS[psi]
                for fk in range(KC):
                    S.op("pe", lambda e, fk=fk, ps=ps, ds=ds, g=g: e.matmul(ps[:], lhsT=HID[:, fk, g * 128:(g + 1) * 128], rhs=WS[ds][:, fk, :],
                                                                            start=(fk == 0), stop=(fk == KC - 1)),
                         reads=["WS%d" % ds, ("HID", fk)], writes=[("ps", psi)])
                S.op("act", lambda e, ps=ps, g=g, dh=dh: e.activation(out=YS[:, g, dh * 512:(dh + 1) * 512], in_=ps[:], func=AF.Copy),
                     reads=[("ps", psi)], writes=[("YS", g)])
        for g in range(4):
            i = g % 2
            pa, pb = k.PS[6], k.PS[7]
            for tt in range(4):
                col = (g * 4 + tt) * 16 + ex
                S.op("pe", lambda e, tt=tt, col=col: e.matmul(pa[:, tt * 128:(tt + 1) * 128], lhsT=AP(POSB, col, [[0, 128]]), rhs=k.IDENT_B[:],
                                                              start=True, stop=True), reads=["POSB", "IDENT_B"], writes=[("ps", 6)])
                S.op("pe", lambda e, tt=tt, col=col: e.matmul(pb[:, tt * 128:(tt + 1) * 128], lhsT=AP(CWBF, col, [[0, 128]]), rhs=k.IDENT_B[:],
                                                              start=True, stop=True), reads=["CWBF", "IDENT_B"], writes=[("ps", 7)])
            cwb, pt = CWB[i], PT[i]
            S.op("act", lambda e, cwb=cwb: e.activation(out=cwb[:], in_=pb[:], func=AF.Copy), reads=[("ps", 7)], writes=["CWB%d" % i])
            S.op("dve", lambda e, pt=pt, cwb=cwb: e.scalar_tensor_tensor(out=pt[:], in0=pa[:], scalar=IOTA[:, 128:129], in1=cwb[:],
                                                                        op0=ALU.is_equal, op1=ALU.mult),
                 reads=[("ps", 6), "IOTA", "CWB%d" % i], writes=["PTS%d" % i])
            sl = slice(g * 512, (g + 1) * 512)
            for d in range(KC):
                c = cnt[0]
                cnt[0] += 1
                psi = 2 + c % 4
                ps = k.PS[psi]
                S.op("pe", lambda e, ps=ps, g=g, d=d, pt=pt: e.matmul(ps[:], lhsT=YS[:, g, d * 128:(d + 1) * 128], rhs=pt[:], start=True, stop=True),
                     reads=[("YS", g), "PTS%d" % i], writes=[("ps", psi)])
                S.op("dve", lambda e, ps=ps, d=d, sl=sl: e.scalar_tensor_tensor(
                    out=X[:, d, sl], in0=ps[:], scalar=modcol(k, l, 5, d), in1=X[:, d, sl], op0=ALU.mult, op1=ALU.add),
                    reads=[("ps", psi), ("X", d, g), "MODV%d" % l], writes=[("X", d, g)])
        if ex + 1 < NEXP:
            nxt["d0"] = load_slab(k.w_down, ex + 1, 0)
            nxt["d1"] = load_slab(k.w_down, ex + 1, 1)
            ids = nxt
    A.release("H2", "AFF", "LO", "CW", "M01", "IOTA", "POS", "POSB", "CWBF", "PSEL", "HG", "HID", "YS", "PTS0", "PTS1", "CWB0", "CWB1", "SG0", "SG1",
              *["WS%d" % i for i in range(NSLOT)])
```

```python
import contextlib
import numpy as np
import concourse.bass as bass
import concourse.mybir as mybir
from concourse.bass_utils import run_bass_kernel_spmd

F32 = mybir.dt.float32
BF16 = mybir.dt.bfloat16
AF = mybir.ActivationFunctionType
ALU = mybir.AluOpType
AX = mybir.AxisListType

NCORES = 8
T = 2048
NBLK = 4
NTILE = 16
D = 1024
KC = 8
EPS = 1e-6
NEXP = 16
CAP = 1024


class Sched:
    COMPUTE = ("pe", "act", "dve", "pool")

    def __init__(self, nc, n_dma_sems=8):
        self.nc = nc
        self.ops = {e: [] for e in ("pe", "act", "dve", "pool", "sp")}
        self.seq = {e: 0 for e in self.COMPUTE}
        self.res = {}
        self.waited = {e: {} for e in self.ops}
        self.n_dma_sems = n_dma_sems
        self.dma_cnt = {e: 0 for e in self.ops}
        self.dma_val = {}
        self.final_tokens = []
        self.buf_pending = {}
        self.n_ops = 0

    @staticmethod
    def _buf(tag):
        return tag[0] if isinstance(tag, tuple) else tag

    def _deps(self, eng, reads, writes):
        deps = {}

        def add(tok, same_ok):
            if tok is None:
                return
            k, v = tok
            if k == eng and not same_ok:
                return
            if deps.get(k, 0) < v:
                deps[k] = v

        raw_same = eng != "pe"
        for r in reads:
            st = self.res.get(r)
            if st is not None:
                add(st["w"], raw_same)
        for w in writes:
            st = self.res.get(w)
            if st is not None:
                add(st["w"], raw_same)
                for k, v in st["r"].items():
                    add((k, v), raw_same)
        for t in list(reads) + list(writes):
            pend = self.buf_pending.get(self._buf(t))
            if pend:
                for k, v in pend.items():
                    add((k, v), k != "pe")
        out = []
        wd = self.waited[eng]
        for k, v in deps.items():
            if wd.get(k, 0) >= v:
                continue
            wd[k] = v
            out.append((k, v))
        return out

    def _commit(self, tok, reads, writes):
        k, v = tok
        for r in reads:
            st = self.res.setdefault(r, {"w": None, "r": {}})
            if st["r"].get(k, 0) < v:
                st["r"][k] = v
        for w in writes:
            self.res[w] = {"w": tok, "r": {}}

    def collect(self, bufname):
        toks = {}
        for tag, st in self.res.items():
            if self._buf(tag) != bufname:
                continue
            if st["w"] is not None:
                k, v = st["w"]
                toks[k] = max(toks.get(k, 0), v)
            for k, v in st["r"].items():
                toks[k] = max(toks.get(k, 0), v)
        for tag in [t for t in self.res if self._buf(t) == bufname]:
            del self.res[tag]
        return toks

    def op(self, eng, fn, reads=(), writes=()):
        waits = self._deps(eng, reads, writes)
        self.seq[eng] += 1
        tok = (eng, self.seq[eng])
        self.ops[eng].append([waits, fn, tok])
        self._commit(tok, reads, writes)
        self.n_ops += 1
        return tok

    def dma(self, eng, fn, reads=(), writes=(), final=False):
        waits = self._deps(eng, reads, writes)
        i = self.dma_cnt[eng] % self.n_dma_sems
        self.dma_cnt[eng] += 1
        key = "d_%s_%d" % (eng, i)
        prev = self.dma_val.get(key, 0)
        if prev and self.waited[eng].get(key, 0) < prev:
            self.waited[eng][key] = prev
            waits.append((key, prev))
        tok = (key, prev + 16)
        self.dma_val[key] = prev + 16
        self.ops[eng].append([waits, fn, tok])
        self._commit(tok, reads, writes)
        if final:
            self.final_tokens.append(tok)
        self.n_ops += 1
        return tok

    def wait_tokens(self, eng, toks):
        waits = []
        for k, v in toks:
            if self.waited[eng].get(k, 0) < v:
                self.waited[eng][k] = v
                waits.append((k, v))
        if waits:
            self.ops[eng].append([waits, None, None])

    def emit(self):
        nc = self.nc
        needed = {e: set() for e in self.COMPUTE}
        for e, lst in self.ops.items():
            for waits, fn, tok in lst:
                for k, v in waits:
                    if k in needed:
                        needed[k].add(v)
        remap = {e: {v: i + 1 for i, v in enumerate(sorted(needed[e]))} for e in self.COMPUTE}
        sem_keys = list(self.COMPUTE) + sorted(self.dma_val.keys())
        with contextlib.ExitStack() as st:
            sems = {k: st.enter_context(nc.semaphore("s_" + k)) for k in sem_keys}
            block = st.enter_context(nc.Block())

            def replay(ename, e):
                for waits, fn, tok in self.ops[ename]:
                    for k, v in waits:
                        vv = remap[k][v] if k in remap else v
                        e.wait_ge(sems[k], vv)
                    if fn is None:
                        continue
                    inst = fn(e)
                    k, v = tok
                    if k in remap:
                        if v in remap[k]:
                            inst.then_inc(sems[k], 1)
                    else:
                        inst.then_inc(sems[k], 16)

            @block.tensor
            def _(e):
                replay("pe", e)

            @block.scalar
            def _(e):
                replay("act", e)

            @block.vector
            def _(e):
                replay("dve", e)

            @block.gpsimd
            def _(e):
                replay("pool", e)

            @block.sync
            def _(e):
                replay("sp", e)


class Arena:
    def __init__(self, nc, S, lo=16512, hi=229344):
        self.nc, self.S = nc, S
        self.free = [(lo, hi)]
        self.live = {}
        self.freed = []
        self.uid = 0
        self.peak = 0
        self.hi = hi

    def alloc(self, name, shape, dtype):
        nbytes = int(np.prod(shape[1:])) * mybir.dt.size(dtype)
        nbytes = (nbytes + 31) // 32 * 32
        for i, (a, b) in enumerate(self.free):
            if b - a >= nbytes:
                off = a
                if b - a == nbytes:
                    self.free.pop(i)
                else:
                    self.free[i] = (a + nbytes, b)
                break
        else:
            raise RuntimeError("SBUF arena full allocating %s (%d B); live=%s" % (
                name, nbytes, {k: v[1] for k, v in self.live.items()}))
        assert name not in self.live, name
        self.live[name] = (off, nbytes)
        self.peak = max(self.peak, off + nbytes)
        pend = {}
        for (a, b, toks) in self.freed:
            if a < off + nbytes and off < b:
                for k, v in toks.items():
                    pend[k] = max(pend.get(k, 0), v)
        self.S.buf_pending[name] = pend
        self.uid += 1
        return self.nc.alloc_sbuf_tensor_at("%s_%d" % (name, self.uid), list(shape), dtype, offset=off)

    def release(self, *names):
        for name in names:
            off, nbytes = self.live.pop(name)
            toks = self.S.collect(name)
            pend = self.S.buf_pending.pop(name, {})
            for k, v in pend.items():
                toks[k] = max(toks.get(k, 0), v)
            self.freed.append((off, off + nbytes, toks))
            self.free.append((off, off + nbytes))
            self.free.sort()
            merged = []
            for a, b in self.free:
                if merged and merged[-1][1] == a:
                    merged[-1] = (merged[-1][0], b)
                else:
                    merged.append((a, b))
            self.free = merged


def AP(t, off, dims, nparts=128, rowlen=None):
    if rowlen is None:
        rowlen = int(np.prod(list(t.shape)[1:]))
    return bass.AP(t, off, [[rowlen, nparts]] + [list(d) for d in dims])


VEC_OFF = {}
_o = 0
for _n, _w in [("bmod0", 96), ("bmod1", 96), ("gmix0", 16), ("gmix1", 16), ("gffn0", 16), ("gffn1", 16),
               ("gfin", 8), ("conv", 40), ("lng", 16), ("lnb", 16)]:
    VEC_OFF[_n] = _o
    _o += _w
NV = _o

ROW_OFF = {}
_o = 0
for _n, _w in [("sameb", 128), ("gate_b", 16), ("head_g", 512), ("sink", 8), ("mf", 8), ("mb", 8), ("vl", 1), ("vr", 1)]:
    ROW_OFF[_n] = _o
    _o += _w
NR = _o


def fm_cols(v):
    v = np.asarray(v, np.float32)
    return np.ascontiguousarray(v.reshape(-1, 128).T)


def dup2(a):
    return np.repeat(a, 2, axis=1)


class K:
    pass


def build(cfg):
    nc = bass.Bass("TRN2", target_bir_lowering=False)
    k = K()
    k.nc = nc
    k.cfg = cfg
    k.in_names = []

    def din(n, s):
        k.in_names.append(n)
        return nc.dram_tensor(n, list(s), F32, kind="ExternalInput").ap()
    has_moe = any(p.startswith("moe") for p in cfg["phases"])
    k.xT = din("xT", [128, KC, T + 256])
    k.xin = din("xin", [128, KC, T])
    k.ctxT = din("ctxT", [128, KC, 256])
    k.cT = din("cT", [128, 16])
    k.w_mod = din("w_mod", [2, D, 6 * D])
    k.vec = din("vec", [128, NV])
    k.rowb = din("rowb", [128, NR])
    k.ident = din("ident", [128, 128])
    k.bsrow = din("bsrow", [128, 1024])
    k.w_router = din("w_router", [2, D, NEXP])
    if has_moe:
        k.w_gate = din("w_gate", [2, NEXP, D, D])
        k.w_up = din("w_up", [2, NEXP, D, D])
        k.w_down = din("w_down", [2, NEXP, D, D])
    k.w_in_fm = din("w_in_fm", [D, 2304])
    k.w_in_tm = din("w_in_tm", [D, 1168])
    k.ab_w_out = din("ab_w_out", [D, D])
    k.cosT = din("cosT", [128, T + 256])
    k.sinT = din("sinT", [128, T + 256])
    k.amask = din("amask", [128, 2048])
    k.tri = din("tri", [128, 512])
    k.iota = din("iota", [128, 129])
    k.gm_w_in = din("gm_w_in", [D, 4096])
    k.gm_wsT = din("gm_wsT", [128, 8 * 128])
    k.gm_w_out = din("gm_w_out", [2048, D])
    k.outT = nc.dram_tensor("outT", [128, KC, T], F32, kind="ExternalOutput").ap()
    k.ag2_src = nc.dram_tensor("ag2_src", [128, 1040], F32)
    k.ag2_dst = nc.dram_tensor("ag2_dst", [NCORES * 128, 1040], F32)
    k.ag_src = [nc.dram_tensor("ag_src%d" % l, [128, 256], F32) for l in range(2)]
    k.ag_dst = [nc.dram_tensor("ag_dst%d" % l, [NCORES * 128, 256], F32) for l in range(2)]

    S = k.S = Sched(nc)
    A = k.A = Arena(nc, S)
    with contextlib.ExitStack() as st:
        k.PS = [st.enter_context(nc.psum_tensor("ps%d" % i, [128, 512], F32)) for i in range(8)]
        setup_consts(k)
        phases = cfg["phases"]
        if "loadx" in phases:
            X = A.alloc("X", [128, KC, T], F32)
            k.X = X
            for b in range(NBLK):
                S.dma("sp", lambda e, b=b: e.dma_start(out=X[:, :, b * 512:(b + 1) * 512],
                                                       in_=k.xin[:, :, b * 512:(b + 1) * 512]),
                      writes=xt(b))
        for ph in phases:
            if ph == "mod0":
                phase_mod(k, 0)
            elif ph == "mod1":
                phase_mod(k, 1)
            elif ph == "moe0":
                phase_moe(k, 0)
            elif ph == "moe1":
                phase_moe(k, 1)
            elif ph == "gmlp":
                phase_gmlp(k)
            elif ph == "final":
                phase_final(k)
            elif ph == "hx0":
                mix0_hx(k)
            elif ph == "attn":
                phase_attn(k)
            elif ph == "lstm":
                phase_lstm(k)
            elif ph == "mixout":
                phase_mixout(k)
            elif ph == "store_bxt":
                S.dma("pool", lambda e: e.dma_start(out=k.outT[:, 4:8, :], in_=k.BXT[:]), reads=[("BXT", q) for q in range(NTILE)], final=True)
            elif ph == "store_axt":
                S.dma("pool", lambda e: e.dma_start(out=k.outT[:, 0:4, :], in_=k.AXT[:]), reads=[("AXT", q) for q in range(NTILE)], final=True)
            elif ph == "storex":
                for b in range(NBLK):
                    S.dma("sp", lambda e, b=b: e.dma_start(out=k.outT[:, :, b * 512:(b + 1) * 512],
                                                           in_=k.X[:, :, b * 512:(b + 1) * 512]),
                          reads=xt(b), final=True)
        S.wait_tokens("sp", S.final_tokens)
        S.emit()
    k.peak = A.peak
    build.last = k
    return nc


def xt(b):
    return [("X", d, b) for d in range(KC)]


def setup_consts(k):
    nc, S, A = k.nc, k.S, k.A
    k.VEC = A.alloc("VEC", [128, NV], F32)
    k.ROWB = A.alloc("ROWB", [128, NR], F32)
    k.ONES_F = A.alloc("ONES_F", [128, 128], F32)
    k.ONES_B = A.alloc("ONES_B", [128, 128], BF16)
    k.IDENT_F = A.alloc("IDENT_F", [128, 128], F32)
    k.IDENT_B = A.alloc("IDENT_B", [128, 128], BF16)
    k.EPSC = A.alloc("EPSC", [128, 1], F32)
    k.CS = A.alloc("CS", [128, 16], F32)
    k.MODV = [A.alloc("MODV%d" % l, [128, 96], F32) for l in range(2)]
    k.GS1 = [A.alloc("GS1_%d" % l, [128, 16], F32) for l in range(2)]
    k.GS2 = [A.alloc("GS2_%d" % l, [128, 16], F32) for l in range(2)]
    S.dma("sp", lambda e: e.dma_start(out=k.VEC[:], in_=k.vec), writes=["VEC"])
    S.dma("sp", lambda e: e.dma_start(out=k.ROWB[:], in_=k.rowb), writes=["ROWB"])
    S.dma("sp", lambda e: e.dma_start(out=k.IDENT_F[:], in_=k.ident), writes=["IDENT_F"])
    S.dma("sp", lambda e: e.dma_start(out=k.CS[:], in_=k.cT), writes=["CS"])
    S.dma("pool", lambda e: e.dma_start(out=k.IDENT_B[:], in_=k.ident), writes=["IDENT_B"])
    S.op("dve", lambda e: e.memset(k.ONES_F[:], 1.0), writes=["ONES_F"])
    S.op("dve", lambda e: e.memset(k.ONES_B[:], 1.0), writes=["ONES_B"])
    S.op("dve", lambda e: e.memset(k.EPSC[:], EPS), writes=["EPSC"])
    S.op("act", lambda e: e.activation(out=k.CS[:], in_=k.CS[:], func=AF.Silu), reads=["CS"], writes=["CS"])


def vcol(k, name, j):
    o = VEC_OFF[name] + j
    return k.VEC[:, o:o + 1]


def modcol(k, l, which, ch, c=0):
    j = 2 * (which * 8 + ch) + c
    return k.MODV[l][:, j:j + 1]


def phase_mod(k, l):
    nc, S, A = k.nc, k.S, k.A
    NW = 3
    WM = [A.alloc("WM%d" % i, [128, KC, 512], BF16) for i in range(NW)]
    CSB = A.alloc("CSB", [128, 16], BF16)
    ROWM = A.alloc("ROWM", [2, 6 * D], F32)
    S.op("dve", lambda e: e.tensor_copy(out=CSB[:], in_=k.CS[:]), reads=["CS"], writes=["CSB"])
    src = k.w_mod[l].rearrange("(k p) n -> p k n", p=128)
    for s in range(12):
        wm = WM[s % NW]
        psi = s % 2
        ps = k.PS[psi]
        S.dma("pool", lambda e, wm=wm, s=s: e.dma_start(out=wm[:], in_=src[:, :, s * 512:(s + 1) * 512]), writes=["WM%d" % (s % NW)])
        for kk in range(KC):
            S.op("pe", lambda e, wm=wm, kk=kk, ps=ps: e.matmul(ps[0:2, :], lhsT=CSB[:, 2 * kk:2 * kk + 2], rhs=wm[:, kk, :],
                                                               start=(kk == 0), stop=(kk == KC - 1)),
                 reads=["WM%d" % (s % NW), "CSB"], writes=[("ps", psi)])
        S.op("act", lambda e, ps=ps, s=s: e.activation(out=ROWM[:, s * 512:(s + 1) * 512], in_=ps[0:2, :], func=AF.Copy),
             reads=[("ps", psi)], writes=[("ROWM", s)])
    pst = k.PS[7]
    for oc in range(48):
        S.op("pe", lambda e, oc=oc: e.transpose(out=pst[:, 2 * oc:2 * oc + 2], in_=ROWM[0:2, oc * 128:(oc + 1) * 128], identity=k.IDENT_F[0:2, 0:2]),
             reads=[("ROWM", oc // 4), "IDENT_F"], writes=[("ps", 7)])
    ps = pst
    bo = VEC_OFF["bmod%d" % l]
    S.op("dve", lambda e: e.tensor_tensor(out=k.MODV[l][:], in0=ps[:, 0:96], in1=k.VEC[:, bo:bo + 96], op=ALU.add),
         reads=[("ps", 7), "VEC"], writes=["MODV%d" % l])
    go = VEC_OFF["gmix%d" % l]
    S.op("dve", lambda e: e.scalar_tensor_tensor(out=k.GS1[l][:], in0=k.MODV[l][:, 16:32], scalar=1.0,
                                                 in1=k.VEC[:, go:go + 16], op0=ALU.add, op1=ALU.mult),
         reads=["MODV%d" % l, "VEC"], writes=["GS1_%d" % l])
    go2 = VEC_OFF["gffn%d" % l]
    S.op("dve", lambda e: e.scalar_tensor_tensor(out=k.GS2[l][:], in0=k.MODV[l][:, 64:80], scalar=1.0,
                                                 in1=k.VEC[:, go2:go2 + 16], op0=ALU.add, op1=ALU.mult),
         reads=["MODV%d" % l, "VEC"], writes=["GS2_%d" % l])
    A.release("CSB", "ROWM", *["WM%d" % i for i in range(NW)])


def rstd_block(k, src_ap, n, src_tags, SQ, sq_tag, RS, rs_tag, psi):
    S = k.S
    ps = k.PS[psi]
    sq_tags = sq_tag if isinstance(sq_tag, list) else [sq_tag]
    S.op("act", lambda e: e.activation(out=SQ[:, :, 0:n], in_=src_ap, func=AF.Square), reads=src_tags, writes=sq_tags)
    for kk in range(KC):
        S.op("pe", lambda e, kk=kk: e.matmul(ps[:, 0:n], lhsT=k.ONES_F[:], rhs=SQ[:, kk, 0:n],
                                             start=(kk == 0), stop=(kk == KC - 1)),
             reads=sq_tags + ["ONES_F"], writes=[("ps", psi)])
    S.op("act", lambda e: e.activation(out=RS[:, 0:n], in_=ps[:, 0:n], func=AF.Ln, bias=k.EPSC[:, 0:1], scale=1.0 / D),
         reads=[("ps", psi), "EPSC"], writes=[rs_tag])
    S.op("act", lambda e: e.activation(out=RS[:, 0:n], in_=RS[:, 0:n], func=AF.Exp, scale=-0.5),
         reads=[rs_tag], writes=[rs_tag])


def phase_final(k):
    nc, S, A = k.nc, k.S, k.A
    SQ = A.alloc("SQ", [128, KC, 512], F32)
    RS = A.alloc("RS", [128, 512], F32)
    OB = [A.alloc("OB%d" % i, [128, KC, 512], F32) for i in range(1)]
    for b in range(NBLK):
        sl = slice(b * 512, (b + 1) * 512)
        rstd_block(k, k.X[:, :, sl], 512, xt(b), SQ, "SQ", RS, "RS", 6)
        ob = OB[0]
        for kk in range(KC):
            S.op("dve", lambda e, kk=kk, sl=sl: e.scalar_tensor_tensor(
                out=ob[:, kk, :], in0=k.X[:, kk, sl], scalar=vcol(k, "gfin", kk), in1=RS[:], op0=ALU.mult, op1=ALU.mult),
                reads=[("X", kk, b), "RS", "VEC"], writes=[("OB0", kk)])
        S.dma("sp", lambda e, sl=sl: e.dma_start(out=k.outT[:, :, sl], in_=ob[:]),
              reads=[("OB0", kk) for kk in range(KC)], final=True)
    A.release("SQ", "RS", "OB0")


def phase_moe(k, l):
    nc, S, A = k.nc, k.S, k.A
    X = k.X
    sparse = k.cfg.get("sparse", True)
    htc = [0]
    H2 = A.alloc("H2", [128, KC, T], BF16) if not sparse else A.alloc("H2", [128, NTILE, D], BF16)
    AFF = A.alloc("AFF", [128, 256], F32)
    WR = A.alloc("WR", [128, KC, NEXP], F32)
    S.dma("sp", lambda e: e.dma_start(out=WR[:], in_=k.w_router[l].rearrange("(k p) n -> p k n", p=128)), writes=["WR"])
    SQ = A.alloc("SQ", [128, KC, 512], F32)
    TB = A.alloc("TB", [128, KC, 512], F32)
    RS = A.alloc("RS", [128, 512], F32)
    EX = A.alloc("EX", [128, 64], F32)
    SM = A.alloc("SM", [128, 8], F32)
    for b in range(NBLK):
        sl = slice(b * 512, (b + 1) * 512)
        rstd_block(k, X[:, :, sl], 512, xt(b), SQ, "SQ", RS, "RS", 6)
        for kk in range(KC):
            S.op("dve", lambda e, kk=kk, sl=sl: e.scalar_tensor_tensor(
                out=TB[:, kk, :], in0=X[:, kk, sl], scalar=k.GS2[l][:, 2 * kk:2 * kk + 1], in1=RS[:],
                op0=ALU.mult, op1=ALU.mult),
                reads=[("X", kk, b), "RS", "GS2_%d" % l], writes=[("TB", kk)])
            S.op("act", lambda e, kk=kk: e.activation(out=TB[:, kk, :], in_=TB[:, kk, :], func=AF.Identity,
                                                      bias=modcol(k, l, 3, kk), scale=1.0),
                 reads=[("TB", kk), "MODV%d" % l], writes=[("TB", kk)])
            if not sparse:
                S.op("dve", lambda e, kk=kk, sl=sl: e.tensor_copy(out=H2[:, kk, sl], in_=TB[:, kk, :]),
                     reads=[("TB", kk)], writes=[("H2", kk, b)])
        if sparse:
            for t in range(4):
                for half in range(2):
                    hc_ = htc[0]
                    htc[0] += 1
                    pst = k.PS[hc_ % 4]
                    for j in range(4):
                        kk = half * 4 + j
                        S.op("pe", lambda e, pst=pst, j=j, kk=kk, t=t: e.transpose(out=pst[:, j * 128:(j + 1) * 128],
                                                                                    in_=TB[:, kk, t * 128:(t + 1) * 128], identity=k.IDENT_F[:]),
                             reads=[("TB", kk), "IDENT_F"], writes=[("ps", hc_ % 4)])
                    eng = "act" if hc_ % 2 == 0 else "dve"
                    dst = H2[:, b * 4 + t, half * 512:(half + 1) * 512]
                    if eng == "act":
                        S.op("act", lambda e, pst=pst, dst=dst: e.activation(out=dst, in_=pst[:], func=AF.Copy),
                             reads=[("ps", hc_ % 4)], writes=[("H2", b * 4 + t)])
                    else:
                        S.op("dve", lambda e, pst=pst, dst=dst: e.tensor_copy(out=dst, in_=pst[:]),
                             reads=[("ps", hc_ % 4)], writes=[("H2", b * 4 + t)])
        psr = k.PS[7]
        for t in range(4):
            for kk in range(KC):
                S.op("pe", lambda e, t=t, kk=kk: e.matmul(psr[:, t * 16:(t + 1) * 16], lhsT=TB[:, kk, t * 128:(t + 1) * 128],
                                                          rhs=WR[:, kk, :], start=(kk == 0), stop=(kk == KC - 1)),
                     reads=[("TB", kk), "WR"], writes=[("ps", 7)])
        S.op("act", lambda e: e.activation(out=EX[:], in_=psr[:, 0:64], func=AF.Exp), reads=[("ps", 7)], writes=["EX"])
        S.op("dve", lambda e: e.tensor_reduce(out=SM[:, 0:4], in_=AP(EX, 0, [[16, 4], [1, 16]]), axis=AX.X, op=ALU.add),
             reads=["EX"], writes=["SM"])
        S.op("dve", lambda e: e.reciprocal(out=SM[:, 4:8], in_=SM[:, 0:4]), reads=["SM"], writes=["SM"])
        S.op("dve", lambda e, b=b: e.tensor_tensor(out=AP(AFF, b * 64, [[16, 4], [1, 16]]), in0=AP(EX, 0, [[16, 4], [1, 16]]),
                                                   in1=AP(SM, 4, [[1, 4], [0, 16]]), op=ALU.mult),
             reads=["EX", "SM"], writes=["AFF"])
    A.release("SQ", "TB", "RS", "EX", "SM", "WR")
    S.dma("pool", lambda e: e.dma_start(out=k.ag_src[l].ap(), in_=AFF[:]), reads=["AFF"], writes=["ag_src%d" % l])
    S.op("pool", lambda e: e.collective_compute("AllGather", ALU.bypass, replica_groups=[list(range(NCORES))],
                                                ins=[k.ag_src[l].ap().opt()], outs=[k.ag_dst[l].ap().opt()]),
         reads=["ag_src%d" % l], writes=["ag_dst%d" % l])
    AFA = A.alloc("AFA", [128, 2048], F32)
    S.dma("sp", lambda e: e.dma_start(out=AP(AFA, 0, [[256, 8], [1, 256]]),
                                      in_=k.ag_dst[l].ap().rearrange("(r p) n -> p r n", p=128)),
          reads=["ag_dst%d" % l], writes=["AFA"])
    MK = A.alloc("MK", [128, 2048], BF16)
    C1 = A.alloc("C1", [128, 128], F32)
    C2 = A.alloc("C2", [128, 16], F32)
    LO = A.alloc("LO", [128, 16], F32)
    MID = A.alloc("MID", [128, 16], F32)
    GE = A.alloc("GE", [128, 16], F32)
    sb_o = ROW_OFF["sameb"]
    S.op("dve", lambda e: e.memset(LO[:], 0.0), writes=["LO"])
    psc = k.PS[7]
    for it in range(30):
        c = 2.0 ** (-(it + 1))
        S.op("dve", lambda e, c=c: e.tensor_scalar(out=MID[:], in0=LO[:], scalar1=c, scalar2=None, op0=ALU.add),
             reads=["LO"], writes=["MID"])
        S.op("dve", lambda e: e.tensor_tensor(out=AP(MK, 0, [[16, 128], [1, 16]]), in0=AP(AFA, 0, [[16, 128], [1, 16]]),
                                              in1=AP(MID, 0, [[0, 128], [1, 16]]), op=ALU.is_gt),
             reads=["AFA", "MID"], writes=["MK"])
        S.op("dve", lambda e: e.tensor_reduce(out=C1[:], in_=AP(MK, 0, [[256, 8], [1, 16], [16, 16]]), axis=AX.X, op=ALU.add),
             reads=["MK"], writes=["C1"])
        S.op("dve", lambda e: e.tensor_tensor(out=C1[:], in0=C1[:], in1=k.ROWB[:, sb_o:sb_o + 128], op=ALU.mult),
             reads=["C1", "ROWB"], writes=["C1"])
        S.op("dve", lambda e: e.tensor_reduce(out=C2[:], in_=AP(C1, 0, [[1, 16], [16, 8]]), axis=AX.X, op=ALU.add),
             reads=["C1"], writes=["C2"])
        S.op("pe", lambda e: e.matmul(psc[:, 0:16], lhsT=k.ONES_F[:], rhs=C2[:], start=True, stop=True),
             reads=["C2", "ONES_F"], writes=[("ps", 7)])
        S.op("dve", lambda e, c=c: e.tensor_scalar(out=GE[:], in0=psc[:, 0:16], scalar1=CAP - 0.5, scalar2=c,
                                                   op0=ALU.is_ge, op1=ALU.mult),
             reads=[("ps", 7)], writes=["GE"])
        S.op("dve", lambda e: e.tensor_tensor(out=LO[:], in0=LO[:], in1=GE[:], op=ALU.add), reads=["LO", "GE"], writes=["LO"])
    CW = A.alloc("CW", [128, 256], F32)
    S.op("dve", lambda e: e.tensor_tensor(out=AP(CW, 0, [[16, 16], [1, 16]]), in0=AP(AFF, 0, [[16, 16], [1, 16]]),
                                          in1=AP(LO, 0, [[0, 16], [1, 16]]), op=ALU.is_gt),
         reads=["AFF", "LO"], writes=["CW"])
    S.op("dve", lambda e: e.tensor_tensor(out=CW[:], in0=CW[:], in1=AFF[:], op=ALU.mult), reads=["CW", "AFF"], writes=["CW"])
    A.release("AFA", "MK", "C1", "C2", "MID", "GE")
    if sparse:
        moe_sparse_experts(k, l, H2, AFF, LO, CW)
        return
    NSLOT = 6
    WS = [A.alloc("WS%d" % i, [128, KC, 512], BF16) for i in range(NSLOT)]
    HID = A.alloc("HID", [128, KC, T], BF16)
    CWB = [A.alloc("CWB%d" % i, [128, T], BF16) for i in range(2)]
    SG = [A.alloc("SG%d" % i, [128, 512], BF16) for i in range(2)]
    T1 = [A.alloc("T1_%d" % i, [128, 512], BF16) for i in range(2)]
    slot_ctr = [0]

    def load_slab(w_ap, e_idx, h):
        i = slot_ctr[0] % NSLOT
        slot_ctr[0] += 1
        src = w_ap[l, e_idx].rearrange("(k p) n -> p k n", p=128)
        S.dma("pool", lambda e: e.dma_start(out=WS[i][:], in_=src[:, :, h * 512:(h + 1) * 512]), writes=["WS%d" % i])
        return i

    def issue_loads(e_idx):
        return {"g": [load_slab(k.w_gate, e_idx, 0), None], "u": [load_slab(k.w_up, e_idx, 0), None], "e": e_idx}

    def loads_for(e_idx):
        ids = {}
        ids["g0"] = load_slab(k.w_gate, e_idx, 0)
        ids["u0"] = load_slab(k.w_up, e_idx, 0)
        ids["g1"] = load_slab(k.w_gate, e_idx, 1)
        ids["u1"] = load_slab(k.w_up, e_idx, 1)
        ids["d0"] = load_slab(k.w_down, e_idx, 0)
        ids["d1"] = load_slab(k.w_down, e_idx, 1)
        return ids

    cnt = [0]
    ids = loads_for(0)
    for ex in range(NEXP):
        cwb = CWB[ex % 2]
        cwbt = "CWB%d" % (ex % 2)
        for b in range(NBLK):
            psb = k.PS[6]
            for t in range(4):
                tt = b * 4 + t
                S.op("pe", lambda e, t=t, tt=tt, ex=ex: e.matmul(psb[:, t * 128:(t + 1) * 128],
                                                                lhsT=AP(CW, tt * 16 + ex, [[0, 128]]),
                                                                rhs=k.IDENT_F[:], start=True, stop=True),
                     reads=["CW", "IDENT_F"], writes=[("ps", 6)])
            S.op("act", lambda e, b=b, cwb=cwb: e.activation(out=cwb[:, b * 512:(b + 1) * 512], in_=psb[:], func=AF.Copy),
                 reads=[("ps", 6)], writes=[(cwbt, b)])
        cur = ids
        for h in range(2):
            gs, us = cur["g%d" % h], cur["u%d" % h]
            for b in range(NBLK):
                sl = slice(b * 512, (b + 1) * 512)
                for fi in range(4):
                    f = 4 * h + fi
                    c = cnt[0]
                    cnt[0] += 1
                    pg, pu = k.PS[c % 2], k.PS[2 + c % 2]
                    for kk in range(KC):
                        S.op("pe", lambda e, kk=kk, pg=pg, gs=gs, fi=fi, sl=sl: e.matmul(
                            pg[:], lhsT=WS[gs][:, kk, fi * 128:(fi + 1) * 128], rhs=H2[:, kk, sl],
                            start=(kk == 0), stop=(kk == KC - 1)),
                            reads=["WS%d" % gs, ("H2", kk, b)], writes=[("ps", c % 2)])
                    for kk in range(KC):
                        S.op("pe", lambda e, kk=kk, pu=pu, us=us, fi=fi, sl=sl: e.matmul(
                            pu[:], lhsT=WS[us][:, kk, fi * 128:(fi + 1) * 128], rhs=H2[:, kk, sl],
                            start=(kk == 0), stop=(kk == KC - 1)),
                            reads=["WS%d" % us, ("H2", kk, b)], writes=[("ps", 2 + c % 2)])
                    sg, t1 = SG[c % 2], T1[c % 2]
                    S.op("act", lambda e, sg=sg, pg=pg: e.activation(out=sg[:], in_=pg[:], func=AF.Silu),
                         reads=[("ps", c % 2)], writes=["SG%d" % (c % 2)])
                    S.op("dve", lambda e, sg=sg, t1=t1, pu=pu: e.tensor_tensor(out=t1[:], in0=sg[:], in1=pu[:], op=ALU.mult),
                         reads=["SG%d" % (c % 2), ("ps", 2 + c % 2)], writes=["T1_%d" % (c % 2)])
                    S.op("dve", lambda e, t1=t1, f=f, sl=sl, cwb=cwb: e.tensor_tensor(out=HID[:, f, sl], in0=t1[:], in1=cwb[:, sl],
                                                                                 op=ALU.mult),
                         reads=["T1_%d" % (c % 2), (cwbt, b)], writes=[("HID", f, b)])
        if ex + 1 < NEXP:
            nxt = {}
            nxt["g0"] = load_slab(k.w_gate, ex + 1, 0)
            nxt["u0"] = load_slab(k.w_up, ex + 1, 0)
            nxt["g1"] = load_slab(k.w_gate, ex + 1, 1)
            nxt["u1"] = load_slab(k.w_up, ex + 1, 1)
        for h in range(2):
            ds = cur["d%d" % h]
            for b in range(NBLK):
                sl = slice(b * 512, (b + 1) * 512)
                for di in range(4):
                    d = 4 * h + di
                    c = cnt[0]
                    cnt[0] += 1
                    pd = k.PS[4 + c % 2]
                    for fk in range(KC):
                        S.op("pe", lambda e, fk=fk, pd=pd, ds=ds, di=di, sl=sl: e.matmul(
                            pd[:], lhsT=WS[ds][:, fk, di * 128:(di + 1) * 128], rhs=HID[:, fk, sl],
                            start=(fk == 0), stop=(fk == KC - 1)),
                            reads=["WS%d" % ds, ("HID", fk, b)], writes=[("ps", 4 + c % 2)])
                    S.op("dve", lambda e, pd=pd, d=d, sl=sl: e.scalar_tensor_tensor(
                        out=X[:, d, sl], in0=pd[:], scalar=modcol(k, l, 5, d), in1=X[:, d, sl], op0=ALU.mult, op1=ALU.add),
                        reads=[("ps", 4 + c % 2), ("X", d, b), "MODV%d" % l], writes=[("X", d, b)])
        if ex + 1 < NEXP:
            nxt["d0"] = load_slab(k.w_down, ex + 1, 0)
            nxt["d1"] = load_slab(k.w_down, ex + 1, 1)
            ids = nxt
    A.release("H2", "AFF", "LO", "CW", "HID", "CWB0", "CWB1", "SG0", "SG1", "T1_0", "T1_1",
              *["WS%d" % i for i in range(NSLOT)])


def phase_gmlp(k):
    nc, S, A = k.nc, k.S, k.A
    X = k.X
    l = 1
    GELU = AF.Gelu_apprx_tanh
    HB = A.alloc("HB", [128, KC, 512], BF16)
    UT = A.alloc("UT", [128, 16, 512], BF16)
    VN = A.alloc("VN", [128, 4, 2048], BF16)
    UV = A.alloc("UV", [128, 16, 512], BF16)
    SQ = A.alloc("SQ", [128, KC, 512], F32)
    TBK = [A.alloc("TBK%d" % i, [128, 512], F32) for i in range(2)]
    RS = A.alloc("RS", [128, 512], F32)
    B2 = A.alloc("B2", [128, 16 * 128], F32)
    WST = A.alloc("WST", [128, 1024], BF16)
    BSR = A.alloc("BSR", [128, 1024], F32)
    ST = A.alloc("ST", [128, 32], F32)
    NSL = 4
    WS = [A.alloc("GW%d" % i, [128, 4096], BF16) for i in range(NSL)]
    sc = [0]
    S.dma("pool", lambda e: e.dma_start(out=WST[:], in_=k.gm_wsT), writes=["WST"])
    S.dma("sp", lambda e: e.dma_start(out=BSR[:], in_=k.bsrow), writes=["BSR"])
    for j in range(16):
        g = j // 2
        ps = k.PS[6 + j % 2]
        S.op("pe", lambda e, g=g, ps=ps: e.matmul(ps[:, 0:128], lhsT=k.ONES_B[:], rhs=WST[:, g * 128:(g + 1) * 128], start=True, stop=True),
             reads=["WST", "ONES_B"], writes=[("ps", 6 + j % 2)])
        S.op("dve", lambda e, j=j, g=g, ps=ps: e.scalar_tensor_tensor(
            out=B2[:, j * 128:(j + 1) * 128], in0=ps[:, 0:128], scalar=vcol(k, "lnb", j), in1=BSR[:, g * 128:(g + 1) * 128],
            op0=ALU.mult, op1=ALU.add), reads=[("ps", 6 + j % 2), "BSR", "VEC"], writes=[("B2", j)])

    def load_in(c0):
        i = sc[0] % NSL
        sc[0] += 1
        src = k.gm_w_in.rearrange("(k p) n -> p k n", p=128)
        S.dma("pool", lambda e: e.dma_start(out=AP(WS[i], 0, [[512, KC], [1, 512]]), in_=src[:, :, c0:c0 + 512]), writes=["GW%d" % i])
        return i

    def load_out(c0):
        i = sc[0] % NSL
        sc[0] += 1
        src = k.gm_w_out.rearrange("(k p) n -> p k n", p=128)
        S.dma("pool", lambda e: e.dma_start(out=AP(WS[i], 0, [[256, 16], [1, 256]]), in_=src[:, :, c0:c0 + 256]), writes=["GW%d" % i])
        return i

    cnt = [0]
    for b in range(NBLK):
        sl = slice(b * 512, (b + 1) * 512)
        rstd_block(k, X[:, :, sl], 512, xt(b), SQ, [("SQ", h_, v_) for h_ in range(2) for v_ in range(4)], RS, "RS", 6)
        for kk in range(KC):
            tb = TBK[kk % 2]
            S.op("dve", lambda e, kk=kk, sl=sl, tb=tb: e.scalar_tensor_tensor(
                out=tb[:], in0=X[:, kk, sl], scalar=k.GS1[l][:, 2 * kk:2 * kk + 1], in1=RS[:], op0=ALU.mult, op1=ALU.mult),
                reads=[("X", kk, b), "RS", "GS1_%d" % l], writes=["TBK%d" % (kk % 2)])
            S.op("act", lambda e, kk=kk, tb=tb: e.activation(out=HB[:, kk, :], in_=tb[:], func=AF.Identity,
                                                              bias=modcol(k, l, 0, kk), scale=1.0),
                 reads=["TBK%d" % (kk % 2), "MODV%d" % l], writes=[("HB", kk)])
        hbt = [("HB", kk) for kk in range(KC)]
        for us in range(4):
            wi = load_in(us * 512)
            for fi in range(4):
                fc = us * 4 + fi
                c = cnt[0]
                cnt[0] += 1
                ps = k.PS[c % 4]
                for kk in range(KC):
                    S.op("pe", lambda e, kk=kk, ps=ps, wi=wi, fi=fi: e.matmul(
                        ps[:], lhsT=AP(WS[wi], kk * 512 + fi * 128, [[1, 128]]), rhs=HB[:, kk, :], start=(kk == 0), stop=(kk == KC - 1)),
                        reads=["GW%d" % wi, ("HB", kk)], writes=[("ps", c % 4)])
                S.op("act", lambda e, fc=fc, ps=ps: e.activation(out=UT[:, fc, :], in_=ps[:], func=GELU),
                     reads=[("ps", c % 4)], writes=[("UT", fc)])
        vsl = [load_in(2048 + vs * 512) for vs in range(4)]
        for t in range(4):
            for vs in range(4):
                wi = vsl[vs]
                c = cnt[0]
                cnt[0] += 1
                ps = k.PS[c % 4]
                for kk in range(KC):
                    S.op("pe", lambda e, kk=kk, ps=ps, wi=wi, t=t: e.matmul(
                        ps[:], lhsT=HB[:, kk, t * 128:(t + 1) * 128], rhs=AP(WS[wi], kk * 512, [[1, 512]]),
                        start=(kk == 0), stop=(kk == KC - 1)),
                        reads=["GW%d" % wi, ("HB", kk)], writes=[("ps", c % 4)])
                vg = AP(SQ, (t % 2) * 2048 + vs * 512, [[1, 512]])
                S.op("act", lambda e, ps=ps, vg=vg, vs=vs: e.activation(out=vg, in_=ps[:], func=GELU, accum_out=ST[:, vs:vs + 1]),
                     reads=[("ps", c % 4)], writes=[("SQ", t % 2, vs), ("ST", vs)])
                S.op("act", lambda e, vg=vg, vs=vs: e.activation(out=TBK[0][:], in_=vg, func=AF.Square, accum_out=ST[:, 4 + vs:5 + vs]),
                     reads=[("SQ", t % 2, vs)], writes=["TBK0", ("ST", 4 + vs)])
            stt = [("ST", i) for i in range(8)]
            S.op("dve", lambda e: e.tensor_reduce(out=ST[:, 8:10], in_=AP(ST, 0, [[4, 2], [1, 4]]), axis=AX.X, op=ALU.add),
                 reads=stt, writes=[("ST", 8)])
            S.op("dve", lambda e: e.tensor_scalar(out=ST[:, 10:12], in0=ST[:, 8:10], scalar1=1.0 / 2048, scalar2=None, op0=ALU.mult),
                 reads=[("ST", 8)], writes=[("ST", 10)])
            S.op("dve", lambda e: e.tensor_tensor(out=ST[:, 12:13], in0=ST[:, 10:11], in1=ST[:, 10:11], op=ALU.mult),
                 reads=[("ST", 10)], writes=[("ST", 12)])
            S.op("dve", lambda e: e.tensor_tensor(out=ST[:, 13:14], in0=ST[:, 11:12], in1=ST[:, 12:13], op=ALU.subtract),
                 reads=[("ST", 10), ("ST", 12)], writes=[("ST", 13)])
            S.op("act", lambda e: e.activation(out=ST[:, 14:15], in_=ST[:, 13:14], func=AF.Ln, bias=k.EPSC[:, 0:1], scale=1.0),
                 reads=[("ST", 13), "EPSC"], writes=[("ST", 14)])
            S.op("act", lambda e: e.activation(out=ST[:, 15:16], in_=ST[:, 14:15], func=AF.Exp, scale=-0.5),
                 reads=[("ST", 14)], writes=[("ST", 15)])
            S.op("dve", lambda e, t=t: e.tensor_scalar(out=VN[:, t, :], in0=AP(SQ, (t % 2) * 2048, [[1, 2048]]),
                                                       scalar1=ST[:, 10:11], scalar2=ST[:, 15:16], op0=ALU.subtract, op1=ALU.mult),
                 reads=[("SQ", t % 2, vs) for vs in range(4)] + [("ST", 10), ("ST", 15)], writes=[("VN", t)])
        for j in range(16):
            g = j // 2
            c = cnt[0]
            cnt[0] += 1
            ps = k.PS[c % 4]
            for t in range(4):
                S.op("pe", lambda e, t=t, j=j, g=g, ps=ps: e.matmul(
                    ps[:, t * 128:(t + 1) * 128], lhsT=VN[:, t, j * 128:(j + 1) * 128], rhs=WST[:, g * 128:(g + 1) * 128],
                    start=True, stop=True), reads=[("VN", t), "WST"], writes=[("ps", c % 4)])
            tb = TBK[j % 2]
            S.op("dve", lambda e, j=j, ps=ps, tb=tb: e.scalar_tensor_tensor(
                out=AP(tb, 0, [[128, 4], [1, 128]]), in0=AP(ps, 0, [[128, 4], [1, 128]]), scalar=vcol(k, "lng", j),
                in1=AP(B2, j * 128, [[0, 4], [1, 128]]), op0=ALU.mult, op1=ALU.add),
                reads=[("ps", c % 4), ("B2", j), "VEC"], writes=["TBK%d" % (j % 2)])
            S.op("dve", lambda e, j=j, tb=tb: e.tensor_tensor(out=UV[:, j, :], in0=tb[:], in1=UT[:, j, :], op=ALU.mult),
                 reads=["TBK%d" % (j % 2), ("UT", j)], writes=[("UV", j)])
        for os_ in range(4):
            wi = load_out(os_ * 256)
            for di in range(2):
                d = os_ * 2 + di
                c = cnt[0]
                cnt[0] += 1
                ps = k.PS[c % 4]
                for j in range(16):
                    S.op("pe", lambda e, j=j, ps=ps, wi=wi, di=di: e.matmul(
                        ps[:], lhsT=AP(WS[wi], j * 256 + di * 128, [[1, 128]]), rhs=UV[:, j, :], start=(j == 0), stop=(j == 15)),
                        reads=["GW%d" % wi, ("UV", j)], writes=[("ps", c % 4)])
                S.op("dve", lambda e, ps=ps, d=d, sl=sl: e.scalar_tensor_tensor(
                    out=X[:, d, sl], in0=ps[:], scalar=modcol(k, l, 2, d), in1=X[:, d, sl], op0=ALU.mult, op1=ALU.add),
                    reads=[("ps", c % 4), ("X", d, b), "MODV%d" % l], writes=[("X", d, b)])
    A.release("HB", "UT", "VN", "UV", "SQ", "TBK0", "TBK1", "RS", "B2", "WST", "BSR", "ST", *["GW%d" % i for i in range(NSL)])


def prep_inputs(inp, cfg):
    f32 = lambda a: np.ascontiguousarray(np.asarray(a, np.float32))
    x = f32(inp["x"])
    vec = np.zeros((128, NV), np.float32)

    def put(name, arr):
        o = VEC_OFF[name]
        vec[:, o:o + arr.shape[1]] = arr

    for l in range(2):
        put("bmod%d" % l, dup2(fm_cols(inp["b_mod"][l])))
        put("gmix%d" % l, dup2(fm_cols(inp["norm_mix_g"][l])))
        put("gffn%d" % l, dup2(fm_cols(inp["norm_ffn_g"][l])))
    put("gfin", fm_cols(inp["final_norm_g"]))
    cw = f32(inp["ab_conv_w"][0])
    put("conv", np.concatenate([fm_cols(cw[t]) for t in range(5)], axis=1))
    put("lng", fm_cols(inp["gm_ln_g"][0]))
    put("lnb", fm_cols(inp["gm_ln_b"][0]))
    ident = np.eye(128, dtype=np.float32)
    bs = f32(inp["gm_b_s"][0])
    bsrow = np.ascontiguousarray(np.broadcast_to(bs.reshape(1, 1024), (128, 1024)))
    wsT = f32(inp["gm_w_s"][0]).transpose(2, 0, 1).reshape(128, 8 * 128)
    w_in = f32(inp["ab_w_in"][0])
    aq, ak, av, ao, ag = w_in[:, 0:512], w_in[:, 512:1024], w_in[:, 1024:1536], w_in[:, 1536:2048], w_in[:, 2048:2064]
    bq, bk, bv = w_in[:, 2064:2576], w_in[:, 2576:2704], w_in[:, 2704:2832]

    def swp(w, nh):
        w = w.reshape(D, nh, 2, 32)
        return w[:, :, ::-1, :].reshape(D, nh * 64)
    perm = np.concatenate([np.arange(h * 64, (h + 1) * 64) for c_ in range(4) for h in (c_, 4 + c_)])
    w_in_fm = np.ascontiguousarray(np.concatenate([aq, ak, bq[:, perm], swp(bq, 8)[:, perm], bk, swp(bk, 2)], axis=1))
    w_in_tm = np.ascontiguousarray(np.concatenate([av, ao, ag, bv], axis=1))
    jj = np.arange(128)
    tri_f = (jj[:, None] <= jj[None, :]).astype(np.float32)
    tri_b = (jj[:, None] >= jj[None, :]).astype(np.float32)
    tri = np.ascontiguousarray(np.concatenate([tri_f, tri_b, (1 - tri_f) * 30000.0, (1 - tri_b) * 30000.0], axis=1).astype(np.float32))
    iota_t = np.ascontiguousarray(np.concatenate([np.broadcast_to(jj[None, :], (128, 128)), jj[:, None]], axis=1).astype(np.float32))
    band_prev = np.where(jj[None, :] <= jj[:, None], 0.0, -30000.0).astype(np.float32)
    band_next = np.where(jj[:, None] <= jj[None, :], 0.0, -30000.0).astype(np.float32)
    allneg = np.full((128, 128), -30000.0, np.float32)
    inv = (np.float32(10000.0) ** (-np.arange(16, dtype=np.float32) / np.float32(16))).astype(np.float32)
    maps = []
    xin_all = cfg.get("xin")
    for i in range(NCORES):
        b, s = i // 4, i % 4
        t0 = s * T
        xh = np.zeros((T + 256, D), np.float32)
        lo, hi = t0 - 128, t0 + T + 128
        a, bnd = max(lo, 0), min(hi, 8192)
        xh[a - lo:bnd - lo] = x[b, a:bnd]
        xT = np.ascontiguousarray(xh.T.reshape(KC, 128, T + 256).transpose(1, 0, 2))
        if xin_all is not None:
            xc = xin_all[b, t0:t0 + T]
        else:
            xc = x[b, t0:t0 + T]
        xin = np.ascontiguousarray(xc.T.reshape(KC, 128, T).transpose(1, 0, 2))
        ctxT = np.ascontiguousarray(f32(inp["ctx"][b]).T.reshape(KC, 128, 256).transpose(1, 0, 2))
        cT = np.zeros((128, KC, 2), np.float32)
        cT[:, :, 0] = fm_cols(inp["c"][b])
        cT[:, :, 1] = fm_cols(inp["c_ctx"])
        rowb = np.zeros((128, NR), np.float32)
        sameb = np.zeros((8, 16), np.float32)
        for r in range(NCORES):
            if r // 4 == b:
                sameb[r, :] = 1.0
        rowb[:, ROW_OFF["sameb"]:ROW_OFF["sameb"] + 128] = sameb.reshape(1, 128)
        for r in range(NCORES):
            if r // 4 == b and r % 4 < s:
                rowb[:, ROW_OFF["mf"] + r] = 1.0
            if r // 4 == b and r % 4 > s:
                rowb[:, ROW_OFF["mb"] + r] = 1.0
        rowb[:, ROW_OFF["vl"]] = 0.0 if s == 0 else 1.0
        rowb[:, ROW_OFF["vr"]] = 0.0 if s == 3 else 1.0
        rowb[:, ROW_OFF["gate_b"]:ROW_OFF["gate_b"] + 16] = f32(inp["ab_gate_b"][0])[None]
        rowb[:, ROW_OFF["head_g"]:ROW_OFF["head_g"] + 512] = f32(inp["ab_head_g"][0])[None]
        rowb[:, ROW_OFF["sink"]:ROW_OFF["sink"] + 8] = f32(inp["ab_sink"][0])[None]
        tg = np.arange(t0 - 128, t0 + T + 128)
        rowi = (tg // 64).astype(np.float32)
        coli = (tg % 64).astype(np.float32)
        ang = np.concatenate([rowi[:, None] * inv[None], coli[:, None] * inv[None]], axis=1).astype(np.float32)
        cs, sn = np.cos(ang).astype(np.float32), np.sin(ang).astype(np.float32)
        pidx = np.arange(128) % 64
        cosT = np.ascontiguousarray(cs[:, pidx % 32].T)
        sinT = np.ascontiguousarray((sn[:, pidx % 32] * np.where(pidx < 32, -1.0, 1.0)[None]).T.astype(np.float32))
        am = [band_prev, band_next, allneg if s == 0 else band_prev, allneg if s == 3 else band_next]
        amask = np.ascontiguousarray(np.concatenate([np.tile(m_, (1, 4)) for m_ in am], axis=1))
        maps.append({
            "w_in_fm": w_in_fm, "w_in_tm": w_in_tm, "ab_w_out": f32(inp["ab_w_out"][0]), "cosT": cosT, "sinT": sinT,
            "amask": amask, "tri": tri, "iota": iota_t,
            "xT": xT, "xin": xin, "ctxT": ctxT, "cT": cT.reshape(128, 16), "w_mod": f32(inp["w_mod"]), "vec": vec, "rowb": rowb,
            "ident": ident, "bsrow": bsrow, "w_router": f32(inp["moe_w_router"]), "w_gate": f32(inp["moe_w_gate"]),
            "w_up": f32(inp["moe_w_up"]), "w_down": f32(inp["moe_w_down"]), "gm_w_in": f32(inp["gm_w_in"][0]),
            "gm_wsT": wsT, "gm_w_out": f32(inp["gm_w_out"][0]),
        })
    return maps


def gather_out(res):
    out = np.zeros((2, 8192, D), np.float32)
    for i in range(NCORES):
        b, s = i // 4, i % 4
        o = res.results[i]["outT"]
        out[b, s * T:(s + 1) * T] = o.transpose(2, 1, 0).reshape(T, D)
    return out


def run(inp, cfg, trace=False):
    nc = build(cfg)
    maps = prep_inputs(inp, cfg)
    maps = [{n: m[n] for n in build.last.in_names} for m in maps]
    res = run_bass_kernel_spmd(nc, maps, core_ids=list(range(NCORES)), trace=trace)
    return gather_out(res), res


def kernel(**inputs):
    cfg = {"phases": ["mod0", "hx0", "attn", "lstm", "mixout", "moe0", "mod1", "gmlp", "moe1", "final"]}
    out, _ = run(inputs, cfg)
    return out


def mix0_hx(k):
    nc, S, A = k.nc, k.S, k.A
    HX = k.HX = A.alloc("HX", [128, KC, T + 256], BF16)
    HC = k.HC = A.alloc("HC", [128, KC, 256], BF16)
    XB = [A.alloc("XB%d" % i, [128, KC, 512], F32) for i in range(2)]
    SQ = A.alloc("SQ", [128, KC, 512], F32)
    RS = A.alloc("RS", [128, 512], F32)
    TBK = [A.alloc("TBK%d" % i, [128, 512], F32) for i in range(2)]
    blocks = [(k.xT, 0, 512, HX, 0), (k.xT, 512, 512, HX, 0), (k.xT, 1024, 512, HX, 0), (k.xT, 1536, 512, HX, 0),
              (k.xT, 2048, 256, HX, 0), (k.ctxT, 0, 256, HC, 1)]
    for bi, (src, c0, n, dst, cc) in enumerate(blocks):
        xb = XB[bi % 2]
        xtag = "XB%d" % (bi % 2)
        S.dma("sp", lambda e, xb=xb, src=src, c0=c0, n=n: e.dma_start(out=xb[:, :, 0:n], in_=src[:, :, c0:c0 + n]), writes=[xtag])
        rstd_block(k, xb[:, :, 0:n], n, [xtag], SQ, "SQ", RS, "RS", 6)
        for kk in range(KC):
            tb = TBK[kk % 2]
            S.op("dve", lambda e, kk=kk, xb=xb, tb=tb, n=n, cc=cc: e.scalar_tensor_tensor(
                out=tb[:, 0:n], in0=xb[:, kk, 0:n], scalar=k.GS1[0][:, 2 * kk + cc:2 * kk + cc + 1], in1=RS[:, 0:n],
                op0=ALU.mult, op1=ALU.mult), reads=[xtag, "RS", "GS1_0"], writes=["TBK%d" % (kk % 2)])
            S.op("act", lambda e, kk=kk, tb=tb, n=n, cc=cc, dst=dst, c0=c0: e.activation(
                out=dst[:, kk, c0:c0 + n], in_=tb[:, 0:n], func=AF.Identity, bias=modcol(k, 0, 0, kk, cc), scale=1.0),
                reads=["TBK%d" % (kk % 2), "MODV0"], writes=["HX" if dst is HX else "HC"])
    A.release("XB0", "XB1", "SQ", "RS", "TBK0", "TBK1")


def phase_attn(k):
    nc, S, A = k.nc, k.S, k.A
    HX, HC = k.HX, k.HC
    NSL = 3
    MW = [A.alloc("MW%d" % i, [128, KC, 512], BF16) for i in range(NSL)]
    sc = [0]

    def slab(src, c0, n):
        i = sc[0] % NSL
        sc[0] += 1
        v = src.rearrange("(k p) n -> p k n", p=128)
        S.dma("pool", lambda e: e.dma_start(out=MW[i][:, :, 0:n], in_=v[:, :, c0:c0 + n]), writes=["MW%d" % i])
        return i

    COS = A.alloc("COS", [128, T + 256], F32)
    SIN = A.alloc("SIN", [128, T + 256], F32)
    S.dma("sp", lambda e: e.dma_start(out=COS[:], in_=k.cosT), writes=["COS"])
    S.dma("sp", lambda e: e.dma_start(out=SIN[:], in_=k.sinT), writes=["SIN"])
    AM = A.alloc("AM", [128, 4 * 512], BF16)
    S.dma("pool", lambda e: e.dma_start(out=AM[:], in_=k.amask), writes=["AM"])
    BQ = A.alloc("BQ", [128, 4, T], BF16)
    BK = A.alloc("BK", [128, T + 256], BF16)
    KCT = A.alloc("KCT", [128, 256], BF16)
    BV = A.alloc("BV", [128, 18 * 130], BF16)
    BVC = A.alloc("BVC", [128, 2 * 130], BF16)
    R1 = [A.alloc("R1_%d" % i, [128, 512], F32) for i in range(2)]
    R2 = [A.alloc("R2_%d" % i, [128, 512], F32) for i in range(2)]
    S.op("dve", lambda e: e.memset(BV[:], 1.0), writes=["BV"])
    S.op("dve", lambda e: e.memset(BVC[:], 1.0), writes=["BVC"])
    cnt = [0]

    def proj_fm(wi, col, rhs_fn, n, psi):
        ps = k.PS[psi]
        for kk in range(KC):
            S.op("pe", lambda e, kk=kk: e.matmul(ps[:, 0:n], lhsT=MW[wi][:, kk, col:col + 128], rhs=rhs_fn(kk),
                                                 start=(kk == 0), stop=(kk == KC - 1)),
                 reads=["MW%d" % wi, "HX", "HC"], writes=[("ps", psi)])
        return ps

    def rope(psq, pss, n, c0, out_ap, out_tag):
        c = cnt[0]
        cnt[0] += 1
        r1, r2 = R1[c % 2], R2[c % 2]
        S.op("dve", lambda e: e.tensor_tensor(out=r1[:, 0:n], in0=k.PS[psq][:, 0:n], in1=COS[:, c0:c0 + n], op=ALU.mult),
             reads=[("ps", psq), "COS"], writes=["R1_%d" % (c % 2)])
        S.op("dve", lambda e: e.tensor_tensor(out=r2[:, 0:n], in0=k.PS[pss][:, 0:n], in1=SIN[:, c0:c0 + n], op=ALU.mult),
             reads=[("ps", pss), "SIN"], writes=["R2_%d" % (c % 2)])
        S.op("dve", lambda e: e.tensor_tensor(out=out_ap, in0=r1[:, 0:n], in1=r2[:, 0:n], op=ALU.add),
             reads=["R1_%d" % (c % 2), "R2_%d" % (c % 2)], writes=[out_tag])

    wa = slab(k.w_in_fm, 1024, 512)
    wb = slab(k.w_in_fm, 1536, 512)
    for b in range(NBLK):
        for c in range(4):
            pq = 0 + (b * 4 + c) % 2
            ps_ = 2 + (b * 4 + c) % 2
            proj_fm(wa, c * 128, lambda kk, b=b: HX[:, kk, 128 + b * 512:128 + (b + 1) * 512], 512, pq)
            proj_fm(wb, c * 128, lambda kk, b=b: HX[:, kk, 128 + b * 512:128 + (b + 1) * 512], 512, ps_)
            rope(pq, ps_, 512, 128 + b * 512, BQ[:, c, b * 512:(b + 1) * 512], ("BQ", b))
    wc = slab(k.w_in_fm, 2048, 256)
    for bi, (c0, n) in enumerate([(0, 512), (512, 512), (1024, 512), (1536, 512), (2048, 256)]):
        pq, ps_ = 0 + bi % 2, 2 + bi % 2
        proj_fm(wc, 0, lambda kk, c0=c0, n=n: HX[:, kk, c0:c0 + n], n, pq)
        proj_fm(wc, 128, lambda kk, c0=c0, n=n: HX[:, kk, c0:c0 + n], n, ps_)
        rope(pq, ps_, n, c0, BK[:, c0:c0 + n], "BK")
    proj_fm(wc, 0, lambda kk: HC[:, kk, :], 256, 4)
    S.op("act", lambda e: e.activation(out=KCT[:], in_=k.PS[4][:, 0:256], func=AF.Copy), reads=[("ps", 4)], writes=["KCT"])
    wg = slab(k.w_in_tm, 1024, 144)
    for t in range(20):
        ps = k.PS[4 + t % 2]
        src = HX if t < 18 else HC
        tt = t if t < 18 else t - 18
        for kk in range(KC):
            S.op("pe", lambda e, kk=kk, ps=ps, src=src, tt=tt: e.matmul(
                ps[:, 0:128], lhsT=src[:, kk, tt * 128:(tt + 1) * 128], rhs=MW[wg][:, kk, 16:144], start=(kk == 0), stop=(kk == KC - 1)),
                reads=["MW%d" % wg, "HX", "HC"], writes=[("ps", 4 + t % 2)])
        dst = AP(BV, tt * 130, [[65, 2], [1, 64]]) if t < 18 else AP(BVC, tt * 130, [[65, 2], [1, 64]])
        S.op("act", lambda e, ps=ps, dst=dst: e.activation(out=dst, in_=AP(ps, 0, [[64, 2], [1, 64]], rowlen=512), func=AF.Copy),
             reads=[("ps", 4 + t % 2)], writes=["BV" if t < 18 else "BVC"])
    A.release("COS", "SIN", "R1_0", "R1_1", "R2_0", "R2_1", *["MW%d" % i for i in range(NSL)])
    BXT = k.BXT = A.alloc("BXT", [128, 4, T], BF16)
    PT = [A.alloc("PT%d" % i, [128, 512], BF16) for i in range(10)]
    BX = A.alloc("BX", [128, 512], F32)
    ESK = A.alloc("ESK", [128, 8], F32)
    DN = A.alloc("DN", [128, 16], F32)
    so = ROW_OFF["sink"]
    S.op("act", lambda e: e.activation(out=ESK[:], in_=k.ROWB[:, so:so + 8], func=AF.Exp), reads=["ROWB"], writes=["ESK"])
    pc = [0]
    for qb in range(NTILE):
        for g in range(2):
            po = k.PS[4 + g]
            tiles = []
            for j, kind in enumerate(("prev", "cen", "next")):
                col = (qb + j) * 128
                m = None
                if kind == "prev":
                    m = 2 if qb == 0 else 0
                if kind == "next":
                    m = 3 if qb == NTILE - 1 else 1
                tiles.append((BK[g * 64:(g + 1) * 64, col:col + 128], AP(BV, (qb + j) * 130 + g * 65, [[1, 65]]), m, "BK", "BV"))
            for j in range(2):
                tiles.append((KCT[g * 64:(g + 1) * 64, j * 128:(j + 1) * 128], AP(BVC, j * 130 + g * 65, [[1, 65]]), None, "KCT", "BVC"))
            pts = []
            for ti, (kT, vv, m, ktag, vtag) in enumerate(tiles):
                c = pc[0]
                pc[0] += 1
                psi = c % 3
                ps = k.PS[psi]
                rhs_q = BQ[g * 64:(g + 1) * 64, :, qb * 128:(qb + 1) * 128]
                S.op("pe", lambda e, ps=ps, kT=kT, rhs_q=rhs_q, m=m: e.matmul(ps[:], lhsT=kT, rhs=rhs_q, start=True, stop=(m is None)),
                     reads=[ktag, ("BQ", qb // 4)], writes=[("ps", psi)])
                if m is not None:
                    S.op("pe", lambda e, ps=ps, m=m: e.matmul(ps[:], lhsT=k.IDENT_B[:], rhs=AM[:, m * 512:(m + 1) * 512], start=False, stop=True),
                         reads=["AM", "IDENT_B"], writes=[("ps", psi)])
                pi = c % 10
                pt = PT[pi]
                pts.append((pt, pi, vv, vtag))
                S.op("act", lambda e, ps=ps, pt=pt: e.activation(out=pt[:], in_=ps[:], func=AF.Exp, scale=0.125),
                     reads=[("ps", psi)], writes=["PT%d" % pi])
            for r in range(4):
                for ti, (pt, pi, vv, vtag) in enumerate(pts):
                    S.op("pe", lambda e, pt=pt, r=r, vv=vv, ti=ti, po=po: e.matmul(
                        po[:, r * 65:(r + 1) * 65], lhsT=pt[:, r * 128:(r + 1) * 128], rhs=vv, start=(ti == 0), stop=(ti == 4)),
                        reads=["PT%d" % pi, vtag], writes=[("ps", 4 + g)])
            S.op("dve", lambda e, po=po, g=g: e.tensor_tensor(out=DN[:, g * 4:(g + 1) * 4], in0=AP(po, 64, [[65, 4]], rowlen=512),
                                                              in1=ESK[:, g * 4:(g + 1) * 4], op=ALU.add),
                 reads=[("ps", 4 + g), "ESK"], writes=[("DN", g)])
            S.op("dve", lambda e, g=g: e.reciprocal(out=DN[:, 8 + g * 4:8 + (g + 1) * 4], in_=DN[:, g * 4:(g + 1) * 4]),
                 reads=[("DN", g)], writes=[("DN", 2 + g)])
            S.op("dve", lambda e, po=po, g=g: e.tensor_tensor(
                out=AP(BX, g * 256, [[64, 4], [1, 64]]), in0=AP(po, 0, [[65, 4], [1, 64]], rowlen=512),
                in1=AP(DN, 8 + g * 4, [[1, 4], [0, 64]]), op=ALU.mult),
                reads=[("ps", 4 + g), ("DN", 2 + g)], writes=[("BX", g)])
        pst = k.PS[6 + qb % 2]
        for c in range(4):
            S.op("pe", lambda e, c=c, pst=pst: e.transpose(out=pst[:, c * 128:(c + 1) * 128], in_=BX[:, c * 128:(c + 1) * 128],
                                                           identity=k.IDENT_F[:]),
                 reads=[("BX", 0), ("BX", 1), "IDENT_F"], writes=[("ps", 6 + qb % 2)])
        S.op("act", lambda e, qb=qb, pst=pst: e.activation(
            out=BXT[:, :, qb * 128:(qb + 1) * 128], in_=AP(pst, 0, [[128, 4], [1, 128]], rowlen=512), func=AF.Copy),
            reads=[("ps", 6 + qb % 2)], writes=[("BXT", qb)])
    A.release("AM", "BQ", "BK", "KCT", "BV", "BVC", "BX", "ESK", "DN", *["PT%d" % i for i in range(10)])


LNS = float(np.log(128.0 ** -0.5))


def phase_lstm(k):
    nc, S, A = k.nc, k.S, k.A
    HX, HC = k.HX, k.HC
    rb = lambda name, n=1: k.ROWB[:, ROW_OFF[name]:ROW_OFF[name] + n]
    NSL = 3
    MW = [A.alloc("MW%d" % i, [128, KC, 512], BF16) for i in range(NSL)]
    sc = [0]

    def slab(src, c0, n):
        i = sc[0] % NSL
        sc[0] += 1
        v = src.rearrange("(k p) n -> p k n", p=128)
        S.dma("pool", lambda e: e.dma_start(out=MW[i][:, :, 0:n], in_=v[:, :, c0:c0 + n]), writes=["MW%d" % i])
        return i

    TRI = A.alloc("TRI", [128, 512], F32)
    S.dma("sp", lambda e: e.dma_start(out=TRI[:], in_=k.tri), writes=["TRI"])
    LNSC = A.alloc("LNSC", [128, 1], F32)
    S.op("dve", lambda e: e.memset(LNSC[:], LNS), writes=["LNSC"])
    DG = A.alloc("DG", [128, 40, 128], BF16)
    for i in range(40):
        S.op("dve", lambda e, i=i: e.tensor_scalar(out=DG[:, i, :], in0=k.IDENT_F[:], scalar1=vcol(k, "conv", i), scalar2=None, op0=ALU.mult),
             reads=["IDENT_F", "VEC"], writes=[("DG", i)])
    QT = A.alloc("QT", [128, 4, T], BF16)
    KT = A.alloc("KT", [128, 4, T], BF16)
    KTOK = A.alloc("KTOK", [128, 18, 512], BF16)
    VP = A.alloc("VP", [128, 18, 516], BF16)
    OT = A.alloc("OT", [128, 16, 512], BF16)
    G = A.alloc("G", [128, 18, 16], F32)
    PRE = [A.alloc("PRE%d" % i, [128, 2052], BF16) for i in range(2)]
    PREC = A.alloc("PREC", [128, 260], BF16)
    S.op("dve", lambda e: e.memset(VP[:], 1.0), writes=["VP"])
    pcn = [0]

    def nextps(n=4, base=0):
        c = pcn[0]
        pcn[0] += 1
        return base + c % n

    for qk in range(2):
        wi = slab(k.w_in_fm, qk * 512, 512)
        for h in range(4):
            ch = qk * 4 + h
            pre = PRE[(qk * 4 + h) % 2]
            ptag = "PRE%d" % ((qk * 4 + h) % 2)
            for gi, (c0, n) in enumerate([(126, 512), (638, 512), (1150, 512), (1662, 512), (2174, 4)]):
                psi = nextps()
                ps = k.PS[psi]
                for kk in range(KC):
                    S.op("pe", lambda e, kk=kk, ps=ps, wi=wi, h=h, c0=c0, n=n: e.matmul(
                        ps[:, 0:n], lhsT=MW[wi][:, kk, h * 128:(h + 1) * 128], rhs=HX[:, kk, c0:c0 + n], start=(kk == 0), stop=(kk == KC - 1)),
                        reads=["MW%d" % wi, "HX"], writes=[("ps", psi)])
                S.op("act", lambda e, ps=ps, pre=pre, c0=c0, n=n: e.activation(out=pre[:, c0 - 126:c0 - 126 + n], in_=ps[:, 0:n], func=AF.Copy),
                     reads=[("ps", psi)], writes=[(ptag, gi)])
            S.op("dve", lambda e, pre=pre: e.tensor_scalar(out=pre[:, 0:2], in0=pre[:, 0:2], scalar1=rb("vl"), scalar2=None, op0=ALU.mult),
                 reads=[(ptag, 0), "ROWB"], writes=[(ptag, 0)])
            S.op("dve", lambda e, pre=pre: e.tensor_scalar(out=pre[:, 2050:2052], in0=pre[:, 2050:2052], scalar1=rb("vr"), scalar2=None, op0=ALU.mult),
                 reads=[(ptag, 4), "ROWB"], writes=[(ptag, 4)])
            ptags = [(ptag, gi) for gi in range(5)]
            dst = QT if qk == 0 else KT
            dtag = "QT" if qk == 0 else "KT"
            for b in range(NBLK):
                psi = nextps()
                ps = k.PS[psi]
                for tap in range(5):
                    S.op("pe", lambda e, tap=tap, ps=ps, pre=pre, b=b, ch=ch: e.matmul(
                        ps[:], lhsT=DG[:, tap * 8 + ch, :], rhs=pre[:, b * 512 + tap:b * 512 + tap + 512], start=(tap == 0), stop=(tap == 4)),
                        reads=ptags + [("DG", tap * 8 + ch)], writes=[("ps", psi)])
                S.op("act", lambda e, ps=ps, dst=dst, h=h, b=b: e.activation(out=dst[:, h, b * 512:(b + 1) * 512], in_=ps[:], func=AF.Silu),
                     reads=[("ps", psi)], writes=[(dtag, h, b)])
            if qk == 1:
                for t in range(NTILE):
                    psi = nextps()
                    ps = k.PS[psi]
                    for tap in range(5):
                        S.op("pe", lambda e, tap=tap, ps=ps, pre=pre, t=t, ch=ch: e.matmul(
                            ps[:, 0:128], lhsT=pre[:, t * 128 + tap:t * 128 + tap + 128], rhs=DG[:, tap * 8 + ch, :], start=(tap == 0), stop=(tap == 4)),
                            reads=ptags + [("DG", tap * 8 + ch)], writes=[("ps", psi)])
                    S.op("act", lambda e, ps=ps, h=h, t=t: e.activation(out=KTOK[:, t, h * 128:(h + 1) * 128], in_=ps[:, 0:128], func=AF.Silu),
                         reads=[("ps", psi)], writes=[("KTOK", t)])
                S.op("dve", lambda e: e.memset(PREC[:], 0.0), writes=["PREC"])
                psi = nextps()
                ps = k.PS[psi]
                for kk in range(KC):
                    S.op("pe", lambda e, kk=kk, ps=ps, wi=wi, h=h: e.matmul(
                        ps[:, 0:256], lhsT=MW[wi][:, kk, h * 128:(h + 1) * 128], rhs=HC[:, kk, :], start=(kk == 0), stop=(kk == KC - 1)),
                        reads=["MW%d" % wi, "HC"], writes=[("ps", psi)])
                S.op("act", lambda e, ps=ps: e.activation(out=PREC[:, 2:258], in_=ps[:, 0:256], func=AF.Copy), reads=[("ps", psi)], writes=["PREC"])
                for t in range(2):
                    psi = nextps()
                    ps = k.PS[psi]
                    for tap in range(5):
                        S.op("pe", lambda e, tap=tap, ps=ps, t=t, ch=ch: e.matmul(
                            ps[:, 0:128], lhsT=PREC[:, t * 128 + tap:t * 128 + tap + 128], rhs=DG[:, tap * 8 + ch, :], start=(tap == 0), stop=(tap == 4)),
                            reads=["PREC", ("DG", tap * 8 + ch)], writes=[("ps", psi)])
                    S.op("act", lambda e, ps=ps, h=h, t=t: e.activation(out=KTOK[:, 16 + t, h * 128:(h + 1) * 128], in_=ps[:, 0:128], func=AF.Silu),
                         reads=[("ps", psi)], writes=[("KTOK", 16 + t)])
    wv = slab(k.w_in_tm, 0, 512)
    wo = slab(k.w_in_tm, 512, 512)
    wg = slab(k.w_in_tm, 1024, 144)
    for t in range(18):
        src = HX if t < 16 else HC
        c0 = 128 + t * 128 if t < 16 else (t - 16) * 128
        stag = "HX" if t < 16 else "HC"
        for which, wi_, n in (("v", wv, 512), ("o", wo, 512), ("g", wg, 16)):
            if which == "o" and t >= 16:
                continue
            psi = nextps()
            ps = k.PS[psi]
            for kk in range(KC):
                S.op("pe", lambda e, kk=kk, ps=ps, src=src, c0=c0, wi_=wi_, n=n: e.matmul(
                    ps[:, 0:n], lhsT=src[:, kk, c0:c0 + 128], rhs=MW[wi_][:, kk, 0:n], start=(kk == 0), stop=(kk == KC - 1)),
                    reads=["MW%d" % wi_, stag], writes=[("ps", psi)])
            if which == "v":
                S.op("act", lambda e, ps=ps, t=t: e.activation(out=AP(VP, t * 516, [[129, 4], [1, 128]]),
                                                               in_=AP(ps, 0, [[128, 4], [1, 128]], rowlen=512), func=AF.Copy),
                     reads=[("ps", psi)], writes=[("VP", t)])
            elif which == "o":
                S.op("act", lambda e, ps=ps, t=t: e.activation(out=OT[:, t, :], in_=ps[:], func=AF.Copy), reads=[("ps", psi)], writes=[("OT", t)])
            else:
                S.op("dve", lambda e, ps=ps, t=t: e.tensor_tensor(out=G[:, t, :], in0=ps[:, 0:16], in1=rb("gate_b", 16), op=ALU.add),
                     reads=[("ps", psi), "ROWB"], writes=["G"])
    A.release("HX", "HC", "PRE0", "PRE1", "PREC", "DG", *["MW%d" % i for i in range(NSL)])
    ZS = A.alloc("ZS", [128, 16, 8, 129], BF16)
    def galloc(n):
        return A.alloc(n, [128, 18, 8], F32)
    LI, NLF, NB, NG, ARG, BIASD, EB, WCOL, EG = [galloc(n) for n in ("LI", "NLF", "NB", "NG", "ARG", "BIASD", "EB", "WCOL", "EG")]
    gv = lambda off: AP(G, off, [[16, 18], [8, 2], [1, 4]])
    v4 = lambda tns: AP(tns, 0, [[8, 18], [4, 2], [1, 4]])
    S.op("dve", lambda e: e.tensor_copy(out=v4(LI), in_=gv(0)), reads=["G"], writes=["LI"])
    S.op("act", lambda e: e.activation(out=v4(NLF), in_=gv(4), func=AF.Exp, scale=-1.0), reads=["G"], writes=["NLF"])
    S.op("act", lambda e: e.activation(out=NLF[:], in_=NLF[:], func=AF.Ln, bias=k.ONES_F[:, 0:1], scale=1.0), reads=["NLF", "ONES_F"], writes=["NLF"])
    psn = k.PS[0]
    for t in range(18):
        for d in range(2):
            S.op("pe", lambda e, t=t, d=d: e.matmul(psn[:, t * 8 + d * 4:t * 8 + d * 4 + 4], lhsT=TRI[:, d * 128:(d + 1) * 128],
                                                    rhs=NLF[:, t, d * 4:(d + 1) * 4], start=True, stop=True),
                 reads=["TRI", "NLF"], writes=[("ps", 0)])
    S.op("dve", lambda e: e.tensor_copy(out=AP(NB, 0, [[1, 144]]), in_=psn[:, 0:144]), reads=[("ps", 0)], writes=["NB"])
    psg = k.PS[1]
    for t in range(18):
        S.op("pe", lambda e, t=t: e.matmul(psg[:, t * 8:(t + 1) * 8], lhsT=k.ONES_F[:], rhs=NLF[:, t, :], start=True, stop=True),
             reads=["ONES_F", "NLF"], writes=[("ps", 1)])
    S.op("dve", lambda e: e.tensor_copy(out=AP(NG, 0, [[1, 144]]), in_=psg[:, 0:144]), reads=[("ps", 1)], writes=["NG"])
    S.op("dve", lambda e: e.tensor_tensor(out=ARG[:], in0=NB[:], in1=LI[:], op=ALU.add), reads=["NB", "LI"], writes=["ARG"])
    S.op("dve", lambda e: e.tensor_scalar(out=BIASD[:], in0=ARG[:], scalar1=LNS, scalar2=None, op0=ALU.add), reads=["ARG"], writes=["BIASD"])
    S.op("act", lambda e: e.activation(out=EB[:], in_=NB[:], func=AF.Exp, scale=-1.0, bias=LNSC[:, 0:1]), reads=["NB", "LNSC"], writes=["EB"])
    S.op("dve", lambda e: e.tensor_tensor(out=WCOL[:], in0=ARG[:], in1=NG[:], op=ALU.subtract), reads=["ARG", "NG"], writes=["WCOL"])
    S.op("act", lambda e: e.activation(out=WCOL[:], in_=WCOL[:], func=AF.Exp), reads=["WCOL"], writes=["WCOL"])
    S.op("act", lambda e: e.activation(out=EG[:], in_=NG[:], func=AF.Exp, scale=-1.0), reads=["NG"], writes=["EG"])
    NGC = A.alloc("NGC", [128, 16, 8], F32)
    EGC = A.alloc("EGC", [128, 16, 8], F32)
    SUMM = A.alloc("SUMM", [128, 8, 130], F32)
    S.op("dve", lambda e: e.memset(NGC[:], 0.0), writes=["NGC"])
    for c in range(1, 16):
        S.op("dve", lambda e, c=c: e.tensor_tensor(out=NGC[:, c, 0:4], in0=NGC[:, c - 1, 0:4], in1=NG[:, c - 1, 0:4], op=ALU.add),
             reads=["NGC", "NG"], writes=["NGC"])
    for c in range(14, -1, -1):
        S.op("dve", lambda e, c=c: e.tensor_tensor(out=NGC[:, c, 4:8], in0=NGC[:, c + 1, 4:8], in1=NG[:, c + 1, 4:8], op=ALU.add),
             reads=["NGC", "NG"], writes=["NGC"])
    S.op("act", lambda e: e.activation(out=EGC[:], in_=NGC[:], func=AF.Exp, scale=-1.0), reads=["NGC"], writes=["EGC"])
    S.op("dve", lambda e: e.tensor_tensor(out=AP(SUMM, 129, [[130, 4]]), in0=NGC[:, 15, 0:4], in1=NG[:, 15, 0:4], op=ALU.add),
         reads=["NGC", "NG"], writes=[("SUMM", "g")])
    S.op("dve", lambda e: e.tensor_tensor(out=AP(SUMM, 4 * 130 + 129, [[130, 4]]), in0=NGC[:, 0, 4:8], in1=NG[:, 0, 4:8], op=ALU.add),
         reads=["NGC", "NG"], writes=[("SUMM", "g")])
    Z = A.alloc("Z", [128, 8, 129], F32)
    CCTX = A.alloc("CCTX", [128, 8, 129], F32)
    VW = [A.alloc("VW%d" % i, [128, 516], BF16) for i in range(2)]
    vc = [0]

    def chain_step(t, d, zbuf, ztag, save_c):
        i = vc[0] % 2
        vc[0] += 1
        vw = VW[i]
        S.op("dve", lambda e: e.tensor_tensor(out=AP(vw, 0, [[129, 4], [1, 129]]), in0=AP(VP, t * 516, [[129, 4], [1, 129]]),
                                              in1=AP(WCOL, t * 8 + d * 4, [[1, 4], [0, 129]]), op=ALU.mult),
             reads=[("VP", t), "WCOL"], writes=["VW%d" % i])
        for hp in range(2):
            psi = 2 + 2 * i + hp
            ps = k.PS[psi]
            for hh in range(2):
                h = hp * 2 + hh
                S.op("pe", lambda e, ps=ps, h=h, hh=hh: e.matmul(ps[:, hh * 129:(hh + 1) * 129], lhsT=KTOK[:, t, h * 128:(h + 1) * 128],
                                                                 rhs=vw[:, h * 129:(h + 1) * 129], start=True, stop=True),
                     reads=[("KTOK", t), "VW%d" % i], writes=[("ps", psi)])
        for h in range(4):
            hd = d * 4 + h
            ps = k.PS[2 + 2 * i + h // 2]
            if save_c is not None:
                S.op("act", lambda e, hd=hd: e.activation(out=ZS[:, save_c, hd, :], in_=zbuf[:, hd, :], func=AF.Copy),
                     reads=[(ztag, hd)], writes=[("ZS", save_c, hd)])
            S.op("dve", lambda e, hd=hd, ps=ps, h=h: e.scalar_tensor_tensor(
                out=zbuf[:, hd, :], in0=zbuf[:, hd, :], scalar=EG[:, t, hd:hd + 1], in1=ps[:, (h % 2) * 129:(h % 2 + 1) * 129],
                op0=ALU.mult, op1=ALU.add), reads=[(ztag, hd), "EG", ("ps", 2 + 2 * i + h // 2)], writes=[(ztag, hd)])

    S.op("dve", lambda e: e.memset(CCTX[:], 0.0), writes=[("CCTX", hd) for hd in range(8)])
    S.op("dve", lambda e: e.memset(Z[:], 0.0), writes=[("Z", hd) for hd in range(8)])
    for t in (16, 17):
        chain_step(t, 0, CCTX, "CCTX", None)
    for t in (17, 16):
        chain_step(t, 1, CCTX, "CCTX", None)
    for c in range(16):
        chain_step(c, 0, Z, "Z", c)
    for c in range(15, -1, -1):
        chain_step(c, 1, Z, "Z", c)
    S.op("dve", lambda e: e.tensor_copy(out=SUMM[:, :, 0:129], in_=Z[:]), reads=[("Z", hd) for hd in range(8)], writes=[("SUMM", "z")])
    stags = [("SUMM", "z"), ("SUMM", "g")]
    S.dma("pool", lambda e: e.dma_start(out=k.ag2_src.ap(), in_=AP(SUMM, 0, [[1, 1040]])), reads=stags, writes=["ag2_src"])
    S.op("pool", lambda e: e.collective_compute("AllGather", ALU.bypass, replica_groups=[list(range(NCORES))],
                                                ins=[k.ag2_src.ap().opt()], outs=[k.ag2_dst.ap().opt()]),
         reads=["ag2_src"], writes=["ag2_dst"])
    A.release("KTOK", "VW0", "VW1", "Z")
    GATH = [A.alloc("GATH%d" % r, [128, 1040], F32) for r in range(NCORES)]
    EGR = A.alloc("EGR", [128, 64], F32)
    for r in range(NCORES):
        S.dma("sp", lambda e, r=r: e.dma_start(out=GATH[r][:], in_=k.ag2_dst.ap()[r * 128:(r + 1) * 128, :]), reads=["ag2_dst"], writes=["GATH%d" % r])
        S.op("act", lambda e, r=r: e.activation(out=EGR[:, r * 8:(r + 1) * 8], in_=AP(GATH[r], 129, [[130, 8]]), func=AF.Exp, scale=-1.0),
             reads=["GATH%d" % r], writes=[("EGR", r)])
    CST = CCTX
    TM1S = [A.alloc("TM1_%d" % i, [128, 129], F32) for i in range(2)]
    tmc = [0]
    for d in range(2):
        order = range(8) if d == 0 else range(7, -1, -1)
        mname = "mf" if d == 0 else "mb"
        for r in order:
            for h in range(4):
                hd = d * 4 + h
                TM1 = TM1S[tmc[0] % 2]
                tmt = "TM1_%d" % (tmc[0] % 2)
                tmc[0] += 1
                S.op("dve", lambda e, r=r, hd=hd, TM1=TM1: e.scalar_tensor_tensor(
                    out=TM1[:], in0=CST[:, hd, :], scalar=EGR[:, r * 8 + hd:r * 8 + hd + 1], in1=GATH[r][:, hd * 130:hd * 130 + 129],
                    op0=ALU.mult, op1=ALU.add), reads=[("CCTX", hd), ("EGR", r), "GATH%d" % r], writes=[tmt])
                S.op("dve", lambda e, hd=hd, TM1=TM1: e.tensor_tensor(out=TM1[:], in0=TM1[:], in1=CST[:, hd, :], op=ALU.subtract),
                     reads=[tmt, ("CCTX", hd)], writes=[tmt])
                mo = ROW_OFF[mname] + r
                S.op("dve", lambda e, hd=hd, mo=mo, TM1=TM1: e.scalar_tensor_tensor(
                    out=CST[:, hd, :], in0=TM1[:], scalar=k.ROWB[:, mo:mo + 1], in1=CST[:, hd, :], op0=ALU.mult, op1=ALU.add),
                    reads=[tmt, ("CCTX", hd), "ROWB"], writes=[("CCTX", hd)])
    A.release("EGR", "TM1_0", "TM1_1", *["GATH%d" % r for r in range(NCORES)])
    AXT = k.AXT = A.alloc("AXT", [128, 4, T], BF16)
    DT = [A.alloc("DT%d" % i, [128, 128], F32) for i in range(2)]
    STb = [A.alloc("STb%d" % i, [128, 128], BF16) for i in range(2)]
    CK = [A.alloc("CK%d" % i, [128, 129], BF16) for i in range(2)]
    TI = [A.alloc("TI%d" % i, [128, 129], F32) for i in range(2)]
    TT = [A.alloc("TT%d" % i, [128, 129], F32) for i in range(2)]
    DEN = A.alloc("DEN", [128, 16], F32)
    HS = [A.alloc("HS%d" % i, [128, 512], F32) for i in range(2)]
    SS = A.alloc("SS", [128, 16], F32)
    OG = A.alloc("OG", [128, 512], F32)
    JK = A.alloc("JK", [128, 128], F32)
    ho = ROW_OFF["head_g"]
    n2 = [0]
    for c in range(16):
        hs = HS[c % 2]
        hst = "HS%d" % (c % 2)
        for h in range(4):
            pss_i = (c * 4 + h) % 2
            pss = k.PS[pss_i]
            S.op("pe", lambda e, pss=pss, h=h, c=c: e.matmul(pss[:, 0:128], lhsT=KT[:, h, c * 128:(c + 1) * 128], rhs=QT[:, h, c * 128:(c + 1) * 128],
                                                             start=True, stop=True),
                 reads=[("KT", h, c // 4), ("QT", h, c // 4)], writes=[("ps", pss_i)])
            for d in range(2):
                hd = d * 4 + h
                i = n2[0] % 2
                n2[0] += 1
                psb_i = 2 + i
                psb = k.PS[psb_i]
                S.op("pe", lambda e, psb=psb, c=c, hd=hd, d=d: e.matmul(psb[:, 0:128], lhsT=AP(NLF, c * 8 + hd, [[0, 128]]),
                                                                        rhs=TRI[:, d * 128:(d + 1) * 128], start=True, stop=False),
                     reads=["NLF", "TRI"], writes=[("ps", psb_i)])
                S.op("pe", lambda e, psb=psb, d=d: e.matmul(psb[:, 0:128], lhsT=k.IDENT_F[:], rhs=TRI[:, 256 + d * 128:256 + (d + 1) * 128],
                                                            start=False, stop=True),
                     reads=["IDENT_F", "TRI"], writes=[("ps", psb_i)])
                dt, stb, ck, ti, tt = DT[i], STb[i], CK[i], TI[i], TT[i]
                S.op("act", lambda e, dt=dt, psb=psb, c=c, hd=hd: e.activation(out=dt[:], in_=psb[:, 0:128], func=AF.Exp, scale=-1.0,
                                                                               bias=BIASD[:, c, hd:hd + 1]),
                     reads=[("ps", psb_i), "BIASD"], writes=["DT%d" % i])
                S.op("dve", lambda e, stb=stb, pss=pss, dt=dt: e.tensor_tensor(out=stb[:], in0=pss[:, 0:128], in1=dt[:], op=ALU.mult),
                     reads=[("ps", pss_i), "DT%d" % i], writes=["STb%d" % i])
                S.op("dve", lambda e, ck=ck, c=c, hd=hd: e.scalar_tensor_tensor(
                    out=ck[:], in0=CST[:, hd, :], scalar=EGC[:, c, hd:hd + 1], in1=ZS[:, c, hd, :], op0=ALU.mult, op1=ALU.add),
                    reads=[("CCTX", hd), "EGC", ("ZS", c, hd)], writes=["CK%d" % i])
                pso_i = 4 + i
                pso = k.PS[pso_i]
                S.op("pe", lambda e, pso=pso, stb=stb, c=c, h=h: e.matmul(pso[:, 0:129], lhsT=stb[:], rhs=VP[:, c, h * 129:(h + 1) * 129],
                                                                          start=True, stop=True),
                     reads=["STb%d" % i, ("VP", c)], writes=[("ps", pso_i)])
                S.op("pe", lambda e, pso=pso, ck=ck, c=c, h=h: e.matmul(pso[:, 256:385], lhsT=QT[:, h, c * 128:(c + 1) * 128], rhs=ck[:],
                                                                        start=True, stop=True),
                     reads=["CK%d" % i, ("QT", h, c // 4)], writes=[("ps", pso_i)])
                S.op("act", lambda e, ti=ti, pso=pso, c=c, hd=hd: e.activation(out=ti[:], in_=pso[:, 256:385], func=AF.Copy, scale=EB[:, c, hd:hd + 1]),
                     reads=[("ps", pso_i), "EB"], writes=["TI%d" % i])
                S.op("dve", lambda e, tt=tt, pso=pso, ti=ti: e.tensor_tensor(out=tt[:], in0=pso[:, 0:129], in1=ti[:], op=ALU.add),
                     reads=[("ps", pso_i), "TI%d" % i], writes=["TT%d" % i])
                S.op("act", lambda e, tt=tt, i=i: e.activation(out=DEN[:, 4 + i:5 + i], in_=tt[:, 128:129], func=AF.Abs),
                     reads=["TT%d" % i], writes=[("DEN", 4 + i)])
                S.op("dve", lambda e, i=i: e.tensor_scalar(out=DEN[:, i:i + 1], in0=DEN[:, 4 + i:5 + i], scalar1=1.0, scalar2=None, op0=ALU.max),
                     reads=[("DEN", 4 + i)], writes=[("DEN", i)])
                S.op("dve", lambda e, i=i: e.reciprocal(out=DEN[:, 2 + i:3 + i], in_=DEN[:, i:i + 1]), reads=[("DEN", i)], writes=[("DEN", 2 + i)])
                if d == 0:
                    S.op("dve", lambda e, hs=hs, tt=tt, h=h, i=i: e.tensor_scalar(out=hs[:, h * 128:(h + 1) * 128], in0=tt[:, 0:128],
                                                                                 scalar1=DEN[:, 2 + i:3 + i], scalar2=None, op0=ALU.mult),
                         reads=["TT%d" % i, ("DEN", 2 + i)], writes=[(hst, h)])
                else:
                    S.op("dve", lambda e, hs=hs, tt=tt, h=h, i=i: e.scalar_tensor_tensor(
                        out=hs[:, h * 128:(h + 1) * 128], in0=tt[:, 0:128], scalar=DEN[:, 2 + i:3 + i], in1=hs[:, h * 128:(h + 1) * 128],
                        op0=ALU.mult, op1=ALU.add), reads=["TT%d" % i, ("DEN", 2 + i), (hst, h)], writes=[(hst, h)])
            S.op("act", lambda e, hs=hs, h=h: e.activation(out=JK[:], in_=hs[:, h * 128:(h + 1) * 128], func=AF.Square, accum_out=SS[:, h:h + 1]),
                 reads=[(hst, h)], writes=["JK", ("SS", h)])
        sst = [("SS", h) for h in range(4)]
        S.op("act", lambda e: e.activation(out=SS[:, 4:8], in_=SS[:, 0:4], func=AF.Ln, bias=k.EPSC[:, 0:1], scale=1.0 / 128), reads=sst + ["EPSC"],
             writes=[("SS", 4)])
        S.op("act", lambda e: e.activation(out=SS[:, 8:12], in_=SS[:, 4:8], func=AF.Exp, scale=-0.5), reads=[("SS", 4)], writes=[("SS", 8)])
        S.op("act", lambda e, c=c: e.activation(out=OG[:], in_=OT[:, c, :], func=AF.Sigmoid), reads=[("OT", c)], writes=["OG"])
        hst4 = [(hst, h) for h in range(4)]
        S.op("dve", lambda e, hs=hs: e.tensor_tensor(out=AP(hs, 0, [[128, 4], [1, 128]]), in0=AP(hs, 0, [[128, 4], [1, 128]]),
                                                     in1=AP(SS, 8, [[1, 4], [0, 128]]), op=ALU.mult), reads=hst4 + [("SS", 8)], writes=hst4)
        S.op("dve", lambda e, hs=hs: e.tensor_tensor(out=hs[:], in0=hs[:], in1=k.ROWB[:, ho:ho + 512], op=ALU.mult), reads=hst4 + ["ROWB"], writes=hst4)
        S.op("dve", lambda e, hs=hs: e.tensor_tensor(out=hs[:], in0=hs[:], in1=OG[:], op=ALU.mult), reads=hst4 + ["OG"], writes=hst4)
        pst_i = 6 + c % 2
        pst = k.PS[pst_i]
        for h in range(4):
            S.op("pe", lambda e, h=h, pst=pst, hs=hs: e.transpose(out=pst[:, h * 128:(h + 1) * 128], in_=hs[:, h * 128:(h + 1) * 128], identity=k.IDENT_F[:]),
                 reads=hst4 + ["IDENT_F"], writes=[("ps", pst_i)])
        S.op("act", lambda e, c=c, pst=pst: e.activation(out=AXT[:, :, c * 128:(c + 1) * 128], in_=AP(pst, 0, [[128, 4], [1, 128]], rowlen=512), func=AF.Copy),
             reads=[("ps", pst_i)], writes=[("AXT", c)])
    A.release("TRI", "LNSC", "QT", "KT", "VP", "OT", "G", "LI", "NLF", "NB", "NG", "ARG", "BIASD", "EB", "WCOL", "EG", "NGC", "EGC", "SUMM",
              "ZS", "CCTX", "DT0", "DT1", "STb0", "STb1", "CK0", "CK1", "TI0", "TI1", "TT0", "TT1", "DEN", "HS0", "HS1", "SS", "OG", "JK")


def phase_mixout(k):
    nc, S, A = k.nc, k.S, k.A
    X = k.X = A.alloc("X", [128, KC, T], F32)
    for b in range(NBLK):
        S.dma("sp", lambda e, b=b: e.dma_start(out=X[:, :, b * 512:(b + 1) * 512], in_=k.xT[:, :, 128 + b * 512:128 + (b + 1) * 512]),
              writes=xt(b))
    WO = [A.alloc("WO%d" % i, [128, KC, 512], BF16) for i in range(2)]
    src = k.ab_w_out.rearrange("(k p) n -> p k n", p=128)
    for hh in range(2):
        S.dma("pool", lambda e, hh=hh: e.dma_start(out=WO[hh][:], in_=src[:, :, hh * 512:(hh + 1) * 512]), writes=["WO%d" % hh])
    cnt = 0
    for d in range(KC):
        wo = WO[d // 4]
        for b in range(NBLK):
            sl = slice(b * 512, (b + 1) * 512)
            psi = cnt % 4
            cnt += 1
            ps = k.PS[psi]
            for kk in range(KC):
                srcb = k.AXT if kk < 4 else k.BXT
                stg = "AXT" if kk < 4 else "BXT"
                S.op("pe", lambda e, kk=kk, ps=ps, wo=wo, d=d, srcb=srcb, sl=sl: e.matmul(
                    ps[:], lhsT=wo[:, kk, (d % 4) * 128:(d % 4 + 1) * 128], rhs=srcb[:, kk % 4, sl], start=(kk == 0), stop=(kk == KC - 1)),
                    reads=["WO%d" % (d // 4)] + [(stg, q) for q in range(b * 4, b * 4 + 4)], writes=[("ps", psi)])
            S.op("dve", lambda e, ps=ps, d=d, sl=sl: e.scalar_tensor_tensor(
                out=X[:, d, sl], in0=ps[:], scalar=modcol(k, 0, 2, d), in1=X[:, d, sl], op0=ALU.mult, op1=ALU.add),
                reads=[("ps", psi), ("X", d, b), "MODV0"], writes=[("X", d, b)])
    A.release("WO0", "WO1", "AXT", "BXT")


def moe_sparse_experts(k, l, H2T, AFF, LO, CW):
    nc, S, A = k.nc, k.S, k.A
    X = k.X
    M01 = A.alloc("M01", [128, 256], F32)
    S.op("dve", lambda e: e.tensor_tensor(out=AP(M01, 0, [[16, 16], [1, 16]]), in0=AP(AFF, 0, [[16, 16], [1, 16]]),
                                          in1=AP(LO, 0, [[0, 16], [1, 16]]), op=ALU.is_gt), reads=["AFF", "LO"], writes=["M01"])
    IOTA = A.alloc("IOTA", [128, 129], F32)
    S.dma("sp", lambda e: e.dma_start(out=IOTA[:], in_=k.iota), writes=["IOTA"])
    STRI = A.alloc("STRI", [128, 128], F32)
    S.dma("sp", lambda e: e.dma_start(out=STRI[:], in_=k.tri[:, 0:128]), writes=["STRI"])
    S.op("dve", lambda e: e.tensor_tensor(out=STRI[:], in0=STRI[:], in1=k.IDENT_F[:], op=ALU.subtract), reads=["STRI", "IDENT_F"], writes=["STRI"])
    CNT = A.alloc("CNT", [128, 256], F32)
    OFF = A.alloc("OFF", [128, 256], F32)
    POS = A.alloc("POS", [128, 256], F32)
    S.op("pe", lambda e: e.matmul(k.PS[7][:, 0:256], lhsT=STRI[:], rhs=M01[:], start=True, stop=True), reads=["STRI", "M01"], writes=[("ps", 7)])
    S.op("pe", lambda e: e.matmul(k.PS[6][:, 0:256], lhsT=k.ONES_F[:], rhs=M01[:], start=True, stop=True), reads=["ONES_F", "M01"], writes=[("ps", 6)])
    S.op("dve", lambda e: e.tensor_copy(out=CNT[:], in_=k.PS[6][:, 0:256]), reads=[("ps", 6)], writes=["CNT"])
    S.op("dve", lambda e: e.memset(OFF[:], 0.0), writes=["OFF"])
    gview = lambda tns, tt: AP(tns, tt * 16, [[64, 4], [1, 16]])
    S.op("dve", lambda e: e.tensor_copy(out=gview(OFF, 1), in_=gview(CNT, 0)), reads=["CNT", "OFF"], writes=["OFF"])
    for tt in (2, 3):
        S.op("dve", lambda e, tt=tt: e.tensor_tensor(out=gview(OFF, tt), in0=gview(OFF, tt - 1), in1=gview(CNT, tt - 1), op=ALU.add),
             reads=["CNT", "OFF"], writes=["OFF"])
    S.op("dve", lambda e: e.tensor_tensor(out=POS[:], in0=k.PS[7][:, 0:256], in1=OFF[:], op=ALU.add), reads=[("ps", 7), "OFF"], writes=["POS"])
    A.release("CNT", "OFF", "STRI")
    POSB = A.alloc("POSB", [128, 256], BF16)
    CWBF = A.alloc("CWBF", [128, 256], BF16)
    S.op("dve", lambda e: e.tensor_copy(out=POSB[:], in_=POS[:]), reads=["POS"], writes=["POSB"])
    S.op("dve", lambda e: e.tensor_copy(out=CWBF[:], in_=CW[:]), reads=["CW"], writes=["CWBF"])
    NSLOT = 6
    WS = [A.alloc("WS%d" % i, [128, KC, 512], BF16) for i in range(NSLOT)]
    PSEL = A.alloc("PSEL", [128, NTILE, 128], BF16)
    HG = A.alloc("HG", [128, KC, 512], BF16)
    HID = A.alloc("HID", [128, KC, 512], BF16)
    YS = A.alloc("YS", [128, 4, D], BF16)
    PT = [A.alloc("PTS%d" % i, [128, 512], BF16) for i in range(2)]
    CWB = [A.alloc("CWB%d" % i, [128, 512], BF16) for i in range(2)]
    SG = [A.alloc("SG%d" % i, [128, 512], BF16) for i in range(2)]
    slot_ctr = [0]

    def load_slab(w_ap, e_idx, h):
        i = slot_ctr[0] % NSLOT
        slot_ctr[0] += 1
        src = w_ap[l, e_idx].rearrange("(k p) n -> p k n", p=128)
        S.dma("pool", lambda e: e.dma_start(out=WS[i][:], in_=src[:, :, h * 512:(h + 1) * 512]), writes=["WS%d" % i])
        return i

    ids = {}
    for nm, w_ap, h in (("g0", k.w_gate, 0), ("u0", k.w_up, 0), ("g1", k.w_gate, 1), ("u1", k.w_up, 1), ("d0", k.w_down, 0), ("d1", k.w_down, 1)):
        ids[nm] = load_slab(w_ap, 0, h)
    cnt = [0]
    for ex in range(NEXP):
        cur = ids
        for t in range(NTILE):
            col = t * 16 + ex
            S.op("dve", lambda e, t=t, col=col: e.tensor_scalar(out=PSEL[:, t, :], in0=IOTA[:, 0:128], scalar1=POS[:, col:col + 1],
                                                                scalar2=M01[:, col:col + 1], op0=ALU.is_equal, op1=ALU.mult),
                 reads=["IOTA", "POS", "M01"], writes=[("PSEL", t)])
        for fc in range(KC):
            c = cnt[0]
            cnt[0] += 1
            psi = c % 2
            ps = k.PS[psi]
            for g in range(4):
                for tt in range(4):
                    t = g * 4 + tt
                    S.op("pe", lambda e, ps=ps, g=g, tt=tt, t=t, fc=fc: e.matmul(
                        ps[:, g * 128:(g + 1) * 128], lhsT=H2T[:, t, fc * 128:(fc + 1) * 128], rhs=PSEL[:, t, :], start=(tt == 0), stop=(tt == 3)),
                        reads=[("H2", t), ("PSEL", t)], writes=[("ps", psi)])
            if fc % 2 == 0:
                S.op("act", lambda e, ps=ps, fc=fc: e.activation(out=HG[:, fc, :], in_=ps[:], func=AF.Copy), reads=[("ps", psi)], writes=[("HG", fc)])
            else:
                S.op("dve", lambda e, ps=ps, fc=fc: e.tensor_copy(out=HG[:, fc, :], in_=ps[:]), reads=[("ps", psi)], writes=[("HG", fc)])
        for fo in range(KC):
            h, fi = fo // 4, fo % 4
            gs, us = cur["g%d" % h], cur["u%d" % h]
            c = cnt[0]
            cnt[0] += 1
            pgi, pui = 2 + 2 * (c % 2), 3 + 2 * (c % 2)
            pg, pu = k.PS[pgi], k.PS[pui]
            for kk in range(KC):
                S.op("pe", lambda e, kk=kk, pg=pg, gs=gs, fi=fi: e.matmul(pg[:], lhsT=WS[gs][:, kk, fi * 128:(fi + 1) * 128], rhs=HG[:, kk, :],
                                                                          start=(kk == 0), stop=(kk == KC - 1)),
                     reads=["WS%d" % gs, ("HG", kk)], writes=[("ps", pgi)])
            for kk in range(KC):
                S.op("pe", lambda e, kk=kk, pu=pu, us=us, fi=fi: e.matmul(pu[:], lhsT=WS[us][:, kk, fi * 128:(fi + 1) * 128], rhs=HG[:, kk, :],
                                                                          start=(kk == 0), stop=(kk == KC - 1)),
                     reads=["WS%d" % us, ("HG", kk)], writes=[("ps", pui)])
            sg = SG[c % 2]
            S.op("act", lambda e, sg=sg, pg=pg: e.activation(out=sg[:], in_=pg[:], func=AF.Silu), reads=[("ps", pgi)], writes=["SG%d" % (c % 2)])
            S.op("dve", lambda e, sg=sg, pu=pu, fo=fo: e.tensor_tensor(out=HID[:, fo, :], in0=sg[:], in1=pu[:], op=ALU.mult),
                 reads=["SG%d" % (c % 2), ("ps", pui)], writes=[("HID", fo)])
        if ex + 1 < NEXP:
            nxt = {}
            for nm, w_ap, h in (("g0", k.w_gate, 0), ("u0", k.w_up, 0), ("g1", k.w_gate, 1), ("u1", k.w_up, 1)):
                nxt[nm] = load_slab(w_ap, ex + 1, h)
        for g in range(4):
            for dh in range(2):
                ds = cur["d%d" % dh]
                c = cnt[0]
                cnt[0] += 1
                psi = c % 2
                ps = k.PS[psi]
                for fk in range(KC):
                    S.op("pe", lambda e, fk=fk, ps=ps, ds=ds, g=g: e.matmul(ps[:], lhsT=HID[:, fk, g * 128:(g + 1) * 128], rhs=WS[ds][:, fk, :],
                                                                            start=(fk == 0), stop=(fk == KC - 1)),
                         reads=["WS%d" % ds, ("HID", fk)], writes=[("ps", psi)])
                S.op("act", lambda e, ps=ps, g=g, dh=dh: e.activation(out=YS[:, g, dh * 512:(dh + 1) * 512], in_=ps[:], func=AF.Copy),
                     reads=[("ps", psi)], writes=[("YS", g)])
        for g in range(4):
            i = g % 2
            pa, pb = k.PS[6], k.PS[7]
            for tt in range(4):
                col = (g * 4 + tt) * 16 + ex
                S.op("pe", lambda e, tt=tt, col=col: e.matmul(pa[:, tt * 128:(tt + 1) * 128], lhsT=AP(POSB, col, [[0, 128]]), rhs=k.IDENT_B[:],
                                                              start=True, stop=True), reads=["POSB", "IDENT_B"], writes=[("ps", 6)])
                S.op("pe", lambda e, tt=tt, col=col: e.matmul(pb[:, tt * 128:(tt + 1) * 128], lhsT=AP(CWBF, col, [[0, 128]]), rhs=k.IDENT_B[:],
                                                              start=True, stop=True), reads=["CWBF", "IDENT_B"], writes=[("ps", 7)])
            cwb, pt = CWB[i], PT[i]
            S.op("act", lambda e, cwb=cwb: e.activation(out=cwb[:], in_=pb[:], func=AF.Copy), reads=[("ps", 7)], writes=["CWB%d" % i])
            S.op("dve", lambda e, pt=pt, cwb=cwb: e.scalar_tensor_tensor(out=pt[:], in0=pa[:], scalar=IOTA[:, 128:129], in1=cwb[:],
                                                                        op0=ALU.is_equal, op1=ALU.mult),
                 reads=[("ps", 6), "IOTA", "CWB%d" % i], writes=["PTS%d" % i])
            sl = slice(g * 512, (g + 1) * 512)
            for d in range(KC):
                c = cnt[0]
                cnt[0] += 1
                psi = 2 + c % 4
                ps = k.PS[psi]
                S.op("pe", lambda e, ps=ps, g=g, d=d, pt=pt: e.matmul(ps[:], lhsT=YS[:, g, d * 128:(d + 1) * 128], rhs=pt[:], start=True, stop=True),
                     reads=[("YS", g), "PTS%d" % i], writes=[("ps", psi)])
                S.op("dve", lambda e, ps=ps, d=d, sl=sl: e.scalar_tensor_tensor(
                    out=X[:, d, sl], in0=ps[:], scalar=modcol(k, l, 5, d), in1=X[:, d, sl], op0=ALU.mult, op1=ALU.add),
                    reads=[("ps", psi), ("X", d, g), "MODV%d" % l], writes=[("X", d, g)])
        if ex + 1 < NEXP:
            nxt["d0"] = load_slab(k.w_down, ex + 1, 0)
            nxt["d1"] = load_slab(k.w_down, ex + 1, 1)
            ids = nxt
    A.release("H2", "AFF", "LO", "CW", "M01", "IOTA", "POS", "POSB", "CWBF", "PSEL", "HG", "HID", "YS", "PTS0", "PTS1", "CWB0", "CWB1", "SG0", "SG1",
              *["WS%d" % i for i in range(NSLOT)])
```
